# Optimizing a Trainium2 kernel written in Bass

```python
import math
import jax, jax.numpy as jnp
from jax import lax
import numpy as np

D_MODEL = 2048
BATCH = 4
SEQ = 4096
DEPTH = 1

MIX_WIDTH = D_MODEL
HEAD_DIM = 128
ATTN_WIDTH = MIX_WIDTH // 2
ATTN_HEADS = ATTN_WIDTH // HEAD_DIM
POOL_WIDTH = MIX_WIDTH - ATTN_WIDTH
POOL_WINDOWS = (2, 4, 8, 16)
POOL_GROUPS = len(POOL_WINDOWS)
POOL_GROUP_WIDTH = POOL_WIDTH // POOL_GROUPS
IN_WIDTH = 3 * ATTN_WIDTH + POOL_WIDTH
MOBA_BLOCK = 256
MOBA_TOPK = 3
Q_CHUNK = 64
REL_BUCKETS = 32
REL_MAX_DIST = 1024
D_FF = 256 * ((8 * D_MODEL // 3 + 255) // 256)
CONV_WIDTH = 3
EPS = 1e-6
NEG = -1e30

kernel_name = 'hybrid_moba_pool_convffn'


def rms_norm(x, g):
    xf = x.astype(jnp.float32)
    y = xf * lax.rsqrt(jnp.mean(xf * xf, axis=-1, keepdims=True) + EPS)
    return (y * g.astype(jnp.float32)).astype(x.dtype)


def rel_bucket(dist):
    n = jnp.maximum(dist, 0)
    max_exact = REL_BUCKETS // 2
    nf = jnp.maximum(n, max_exact).astype(jnp.float32)
    large = max_exact + (jnp.log(nf / max_exact) / math.log(REL_MAX_DIST / max_exact)
                         * (REL_BUCKETS - max_exact)).astype(jnp.int32)
    large = jnp.minimum(large, REL_BUCKETS - 1)
    return jnp.where(n < max_exact, n, large)


def moba_attention(q, k, v, rel_bias):
    B, H, S, Dh = q.shape
    nb = -(-S // MOBA_BLOCK)
    S_pad = nb * MOBA_BLOCK
    pad = ((0, 0), (0, 0), (0, S_pad - S), (0, 0))
    q, k, v = jnp.pad(q, pad), jnp.pad(k, pad), jnp.pad(v, pad)
    K = min(MOBA_TOPK, nb)
    kb = k.reshape(B, H, nb, MOBA_BLOCK, Dh)
    vb = v.reshape(B, H, nb, MOBA_BLOCK, Dh)
    k_mean = jnp.mean(kb.astype(jnp.float32), axis=3)
    gate = jnp.einsum('bhsd,bhnd->bhsn', q.astype(jnp.float32), k_mean)
    q_blk = jnp.arange(S_pad) // MOBA_BLOCK
    past = jnp.arange(nb)[None, :] < q_blk[:, None]
    gate = jnp.where(past, gate, NEG)
    _, sel = lax.top_k(gate, K)
    n_chunks = S_pad // Q_CHUNK
    sel_chunks = sel.reshape(B, H, n_chunks, Q_CHUNK, K).transpose(2, 0, 1, 3, 4)
    b_idx = jnp.arange(B)[:, None, None, None]
    h_idx = jnp.arange(H)[None, :, None, None]
    offs = jnp.arange(MOBA_BLOCK)
    scale = Dh ** -0.5

    def chunk(args):
        c, sel_c = args
        q0 = c * Q_CHUNK
        own = q0 // MOBA_BLOCK
        qc = lax.dynamic_slice_in_dim(q, q0, Q_CHUNK, axis=2)
        q_pos = q0 + jnp.arange(Q_CHUNK)
        k_sel = kb[b_idx, h_idx, sel_c]
        v_sel = vb[b_idx, h_idx, sel_c]
        s_sel = jnp.einsum('bhqd,bhqnkd->bhqnk', qc, k_sel).astype(jnp.float32) * scale
        kpos_sel = sel_c[..., None] * MOBA_BLOCK + offs
        bias_sel = rel_bias[h_idx[..., None], rel_bucket(q_pos[:, None, None] - kpos_sel)]
        valid = (sel_c < own)[..., None]
        s_sel = jnp.where(valid, s_sel + bias_sel.astype(jnp.float32), NEG)
        s_sel = s_sel.reshape(B, H, Q_CHUNK, K * MOBA_BLOCK)
        k_own = lax.dynamic_index_in_dim(kb, own, axis=2, keepdims=False)
        v_own = lax.dynamic_index_in_dim(vb, own, axis=2, keepdims=False)
        s_own = jnp.einsum('bhqd,bhkd->bhqk', qc, k_own).astype(jnp.float32) * scale
        dist_own = q_pos[:, None] - (own * MOBA_BLOCK + offs)[None, :]
        bias_own = rel_bias[:, rel_bucket(dist_own)].astype(jnp.float32)
        s_own = jnp.where(dist_own >= 0, s_own + bias_own, NEG)
        probs = jax.nn.softmax(jnp.concatenate([s_sel, s_own], axis=-1), axis=-1)
        p_sel = probs[..., :K * MOBA_BLOCK].reshape(
            B, H, Q_CHUNK, K, MOBA_BLOCK).astype(v.dtype)
        p_own = probs[..., K * MOBA_BLOCK:].astype(v.dtype)
        return (jnp.einsum('bhqnk,bhqnkd->bhqd', p_sel, v_sel)
                + jnp.einsum('bhqk,bhkd->bhqd', p_own, v_own))

    outs = lax.map(chunk, (jnp.arange(n_chunks), sel_chunks))
    out = outs.transpose(1, 0, 3, 2, 4).reshape(B, S_pad, H * Dh)
    return out[:, :S]


def multiscale_pool(p, pool_w, pool_scale):
    B, S, C = p.shape
    pf = p.astype(jnp.float32)
    csum = jnp.concatenate([jnp.zeros((B, 1, C), jnp.float32), jnp.cumsum(pf, axis=1)], axis=1)
    t = jnp.arange(S)
    outs = []
    for g, w in enumerate(POOL_WINDOWS):
        sl = slice(g * POOL_GROUP_WIDTH, (g + 1) * POOL_GROUP_WIDTH)
        cg = csum[:, :, sl]
        start = jnp.maximum(t + 1 - w, 0)
        count = jnp.minimum(t + 1, w).astype(jnp.float32)[None, :, None]
        mixed = (cg[:, 1:] - cg[:, start]) / count - pf[:, :, sl]
        outs.append(jnp.einsum('bsc,cd->bsd', mixed.astype(p.dtype), pool_w[g]))
    return jnp.concatenate(outs, axis=-1) * pool_scale


def conv_ffn(h, w_up, conv_w, conv_b, w_down):
    u = h @ w_up
    S = u.shape[1]
    u_pad = jnp.pad(u, ((0, 0), (CONV_WIDTH - 1, 0), (0, 0)))
    y = conv_b
    for i in range(CONV_WIDTH):
        y = y + conv_w[i] * u_pad[:, i:i + S]
    gate, val = jnp.split(y, 2, axis=-1)
    return (jax.nn.silu(gate) * val) @ w_down


def setup_inputs(seed: int = 0) -> dict:
    key = jax.random.key(seed)
    ks = jax.random.split(key, 16)
    f32 = jnp.float32
    nrm = lambda k, shape, s: jax.random.normal(k, shape, f32) * s
    return {
        'x': nrm(ks[0], (BATCH, SEQ, D_MODEL), 1.0),
        'attn_norm_g': 1.0 + nrm(ks[1], (DEPTH, D_MODEL), 0.02),
        'w_in': nrm(ks[2], (DEPTH, D_MODEL, IN_WIDTH), D_MODEL ** -0.5),
        'q_norm_g': 1.0 + nrm(ks[3], (DEPTH, HEAD_DIM), 0.02),
        'k_norm_g': 1.0 + nrm(ks[4], (DEPTH, HEAD_DIM), 0.02),
        'rel_bias': nrm(ks[5], (ATTN_HEADS, REL_BUCKETS), 0.5),
        'pool_w': nrm(ks[6], (DEPTH, POOL_GROUPS, POOL_GROUP_WIDTH, POOL_GROUP_WIDTH), POOL_GROUP_WIDTH ** -0.5),
        'pool_scale': 1.0 + nrm(ks[7], (DEPTH, POOL_WIDTH), 0.1),
        'w_out': nrm(ks[8], (DEPTH, MIX_WIDTH, D_MODEL), MIX_WIDTH ** -0.5),
        'ffn_norm_g': 1.0 + nrm(ks[9], (DEPTH, D_MODEL), 0.02),
        'w_up': nrm(ks[10], (DEPTH, D_MODEL, 2 * D_FF), D_MODEL ** -0.5),
        'conv_w': nrm(ks[11], (DEPTH, CONV_WIDTH, 2 * D_FF), CONV_WIDTH ** -0.5),
        'conv_b': nrm(ks[12], (DEPTH, 2 * D_FF), 0.01),
        'w_down': nrm(ks[13], (DEPTH, D_FF, D_MODEL), D_FF ** -0.5),
    }


def reference(x, attn_norm_g, w_in, q_norm_g, k_norm_g, rel_bias, pool_w, pool_scale,
              w_out, ffn_norm_g, w_up, conv_w, conv_b, w_down):
    B, S, _ = x.shape

    def heads(t):
        return t.reshape(B, S, ATTN_HEADS, HEAD_DIM).transpose(0, 2, 1, 3)

    for l in range(DEPTH):
        h = rms_norm(x, attn_norm_g[l])
        proj = h @ w_in[l]
        q, k, v, p = jnp.split(proj, [ATTN_WIDTH, 2 * ATTN_WIDTH, 3 * ATTN_WIDTH], axis=-1)
        q = rms_norm(heads(q), q_norm_g[l])
        k = rms_norm(heads(k), k_norm_g[l])
        a = moba_attention(q, k, heads(v), rel_bias)
        m = multiscale_pool(p, pool_w[l], pool_scale[l])
        x = x + jnp.concatenate([a, m], axis=-1) @ w_out[l]
        x = x + conv_ffn(rms_norm(x, ffn_norm_g[l]), w_up[l], conv_w[l], conv_b[l], w_down[l])
    return x
```

```python
import math
import numpy as np
import concourse.bass as bass
import concourse.mybir as mybir
from concourse.bass_utils import run_bass_kernel_spmd

F32 = mybir.dt.float32
BF16 = mybir.dt.bfloat16
AF = mybir.ActivationFunctionType
ALU = mybir.AluOpType
AX = mybir.AxisListType

D = 2048
S = 4096
HD = 128
NH = 8
INW = 4096
DFF = 5632
NJ = DFF // 128
EPS = 1e-6
BIG = 32768.0
NQT = 17
QW = NQT * 128
NBT = 11
DEBUG = False


class Eng:
    def __init__(self, name, h, sem, is_pe=False):
        self.name, self.h, self.sem, self.n, self.is_pe = name, h, sem, 0, is_pe
        self.seen = {}
        self.dsems = []
        self.dvals = []
        self.dnext = 0


class Sched:
    def __init__(self, nc):
        self.nc = nc
        self.w = {}
        self.r = {}
        self.engs = []

    def add_engine(self, e):
        self.engs.append(e)

    def _wait(self, q, t):
        if t[0] == 'c':
            _, e, n = t
            if e is q and q.is_pe:
                return
            key = e.name
            if q.seen.get(key, 0) >= n:
                return
            q.h.wait_ge(e.sem, n)
            q.seen[key] = n
        else:
            _, sem, val, key = t
            if q.seen.get(key, 0) >= val:
                return
            q.h.wait_ge(sem, val)
            q.seen[key] = val

    def _deps(self, q, rd, wr):
        for b in rd:
            t = self.w.get(b)
            if t is not None:
                self._wait(q, t)
        for b in wr:
            t = self.w.get(b)
            if t is not None:
                self._wait(q, t)
            for t2 in self.r.get(b, {}).values():
                self._wait(q, t2)

    def _record(self, tk, rk, rd, wr):
        for b in rd:
            self.r.setdefault(b, {})[rk] = tk
        for b in wr:
            self.w[b] = tk
            self.r[b] = {}

    def op(self, q, fn, rd=(), wr=()):
        self._deps(q, rd, wr)
        ins = fn()
        ins.then_inc(q.sem, 1)
        q.n += 1
        tk = ('c', q, q.n)
        self._record(tk, q.name, rd, wr)
        return tk

    def ops(self, q, fns, rd=(), wr=()):
        self._deps(q, rd, wr)
        ins = None
        for fn in fns:
            ins = fn()
        ins.then_inc(q.sem, 1)
        q.n += 1
        tk = ('c', q, q.n)
        self._record(tk, q.name, rd, wr)
        return tk

    def dma(self, q, out, in_, rd=(), wr=()):
        self._deps(q, rd, wr)
        i = q.dnext % len(q.dsems)
        q.dnext += 1
        sem = q.dsems[i]
        key = q.name + "_d%d" % i
        if q.dvals[i] > 0 and q.seen.get(key, 0) < q.dvals[i]:
            q.h.wait_ge(sem, q.dvals[i])
            q.seen[key] = q.dvals[i]
        q.h.dma_start(out=out, in_=in_).then_inc(sem, 16)
        q.dvals[i] += 16
        tk = ('d', sem, q.dvals[i], key)
        self._record(tk, key, rd, wr)
        return tk

    def barrier(self, keep=()):
        kept = {k: self.w[k] for k in keep if k in self.w}
        skip = {}
        for t in kept.values():
            if t[0] == 'd':
                skip[t[3]] = t[2]
        for q in self.engs:
            for e in self.engs:
                if e is q or e.n == 0:
                    continue
                if q.seen.get(e.name, 0) < e.n:
                    q.h.wait_ge(e.sem, e.n)
                    q.seen[e.name] = e.n
            for e in self.engs:
                for i, sem in enumerate(e.dsems):
                    key = e.name + "_d%d" % i
                    val = e.dvals[i]
                    if key in skip and skip[key] == val:
                        val -= 16
                    if val > 0 and q.seen.get(key, 0) < val:
                        q.h.wait_ge(sem, val)
                        q.seen[key] = val
        self.w = dict(kept)
        self.r = {}


def build_program(debug=False):
    nc = bass.Bass("TRN2", target_bir_lowering=False)

    def din(name, shape, dt=F32):
        return nc.dram_tensor(name, list(shape), dt, kind="ExternalInput").ap()

    xs = din("xs", [S, D])
    w_in = din("w_in", [D, INW])
    w_out = din("w_out", [D, D])
    w_up = din("w_up", [D, 2 * DFF])
    w_down = din("w_down", [DFF, D])
    pool_w = din("pool_w", [4, 256, 256])
    g1bc_d = din("g1bc", [128, D])
    g2bc_d = din("g2bc", [128, D])
    gqk_d = din("gqk", [128, 2])
    pscale_d = din("pscale", [128, 8])
    cw_d = din("cw", [128, 3, 88])
    cb_d = din("cb", [128, 88])
    bias31_d = din("bias31", [128, 8])
    biasT_d = din("biasT", [NH, 128, NBT, 512])
    band_d = din("band", [128, 3, 4, 128])
    gmask_d = din("gmask", [128, NQT, 16])
    notown_d = din("notown", [128, NQT, 16])
    flag_d = din("flag", [128, 1])
    ident_d = din("ident", [128, 128])
    esel_d = din("esel", [128, 16, 128])

    out_d = nc.dram_tensor("out", [2048, D], F32, kind="ExternalOutput").ap()
    skind = "ExternalOutput" if debug else "Internal"
    kT_s = nc.dram_tensor("kT_s", [NH, 128, S], BF16, kind=skind).ap()
    qT_s = nc.dram_tensor("qT_s", [NH, 128, S], BF16, kind=skind).ap()
    v_s = nc.dram_tensor("v_s", [32, 128, 1024], BF16, kind=skind).ap()
    x1_s = nc.dram_tensor("x1_s", [NQT, 128, D], F32, kind=skind).ap()
    mT_s = nc.dram_tensor("mT_s", [8, 128, QW], BF16, kind=skind).ap()
    hn2T_s = nc.dram_tensor("hn2T_s", [16, 128, QW], BF16, kind=skind).ap()
    if debug:
        mix_dbg = nc.dram_tensor("mix_dbg", [128, 16, QW], BF16, kind="ExternalOutput").ap()

    from contextlib import ExitStack
    top = ExitStack()
    with top:
        uid = [0]

        def sb(name, shape, dt, stack=top):
            uid[0] += 1
            return stack.enter_context(nc.sbuf_tensor("sb%d_%s" % (uid[0], name), list(shape), dt))

        def sem(name):
            return top.enter_context(nc.semaphore(name))

        sch = Sched(nc)
        PE = Eng("pe", nc.tensor, sem("s_pe"), is_pe=True)
        ACT = Eng("act", nc.scalar, sem("s_act"))
        DVE = Eng("dve", nc.vector, sem("s_dve"))
        POOL = Eng("pool", nc.gpsimd, sem("s_pool"))
        SP = Eng("sp", nc.sync, sem("s_sp"))
        for e in (PE, ACT, DVE, POOL, SP):
            sch.add_engine(e)
        for e, nd in ((SP, 16), (POOL, 12)):
            for i in range(nd):
                e.dsems.append(sem("d_%s%d" % (e.name, i)))
                e.dvals.append(0)

        ps = [top.enter_context(nc.psum_tensor("ps%d" % i, [128, 512], F32)) for i in range(8)]
        psb = [p.bitcast(BF16) for p in ps]

        ident_b = sb("ident_b", [128, 128], BF16)
        ident_f = sb("ident_f", [128, 128], F32)
        ones_b = sb("ones_b", [128, 128], BF16)
        gqk = sb("gqk", [128, 2], F32)
        gqs = sb("gqs", [128, 1], F32)
        pscale = sb("pscale", [128, 8], F32)
        bias31 = sb("bias31", [128, 8], F32)
        flag = sb("flag", [128, 1], F32)
        epsc = sb("epsc", [128, 1], F32)
        zcol = sb("zcol", [128, 1], F32)
        esel = sb("esel", [128, 16, 128], BF16)
        gmask = sb("gmask", [128, NQT, 16], F32)
        notown = sb("notown", [128, NQT, 16], F32)

        sch.dma(POOL, ident_b[:], ident_d, wr=["ident_b"])
        sch.dma(SP, ident_f[:], ident_d, wr=["ident_f"])
        sch.dma(SP, gqk[:], gqk_d, wr=["gqk"])
        sch.dma(SP, pscale[:], pscale_d, wr=["pscale"])
        sch.dma(SP, bias31[:], bias31_d, wr=["bias31"])
        sch.dma(SP, flag[:], flag_d, wr=["flag"])
        sch.dma(POOL, esel[:], esel_d, wr=["esel"])
        sch.dma(SP, gmask[:], gmask_d, wr=["gmask"])
        sch.dma(SP, notown[:], notown_d, wr=["notown"])
        sch.op(DVE, lambda: nc.vector.memset(ones_b[:], 1.0), wr=["ones_b"])
        sch.op(DVE, lambda: nc.vector.memset(epsc[:], EPS), wr=["epsc"])
        sch.op(DVE, lambda: nc.vector.memset(zcol[:], 0.0), wr=["zcol"])
        sch.op(DVE, lambda: nc.vector.tensor_scalar(out=gqs[:], in0=gqk[:, 0:1], scalar1=float(HD ** -0.5),
                                                    scalar2=None, op0=ALU.mult),
               rd=["gqk"], wr=["gqs"])

        pm = ExitStack()

        def rmsnorm_tile(xt, xkey, gbc, hn, hnkey, ss, rt1, rstd, idx):
            k = "n%d" % (idx % 2)
            sch.op(ACT, lambda: nc.scalar.activation(out=hn, in_=xt, func=AF.Square, accum_out=ss[:]),
                   rd=[xkey], wr=[hnkey, "ss" + k])
            sch.op(ACT, lambda: nc.scalar.activation(out=rt1[:], in_=ss[:], func=AF.Ln, bias=epsc[:],
                                                     scale=1.0 / D),
                   rd=["ss" + k, "epsc"], wr=["rt1" + k])
            sch.op(ACT, lambda: nc.scalar.activation(out=rstd[:], in_=rt1[:], func=AF.Exp, bias=zcol[:],
                                                     scale=-0.5),
                   rd=["rt1" + k, "zcol"], wr=["rstd" + k])
            sch.op(DVE, lambda: nc.vector.scalar_tensor_tensor(out=hn, in0=xt, scalar=rstd[:], in1=gbc,
                                                               op0=ALU.mult, op1=ALU.mult),
                   rd=[xkey, "rstd" + k, "gbc"], wr=[hnkey])

        with ExitStack() as p1:
            def s1(name, shape, dt):
                return sb(name, shape, dt, p1)
            g1bc = s1("g1bc", [128, D], F32)
            sch.dma(SP, g1bc[:], g1bc_d, wr=["gbc"])
            band_b = s1("band_b", [128, 3, 4, 128], BF16)
            poolw_b = s1("poolw_b", [128, 4, 2, 256], BF16)
            xbuf = [s1("xbuf%d" % i, [128, D], F32) for i in range(4)]
            hnb = [s1("hnb%d" % i, [128, D], BF16) for i in range(2)]
            ssb = [s1("ss%d" % i, [128, 1], F32) for i in range(2)]
            rt1b = [s1("rt1%d" % i, [128, 1], F32) for i in range(2)]
            rstdb = [s1("rstd%d" % i, [128, 1], F32) for i in range(2)]
            hnTb = [s1("hnT%d" % i, [128, 16, 1024], BF16) for i in range(2)]
            mst = [s1("mst%d" % i, [128, 8, 128], BF16) for i in range(2)]
            wsl = [s1("wsl%d" % i, [128, 16, 512], BF16) for i in range(2)]
            pall = s1("pall", [128, 9, 1024], BF16)
            sqb = [s1("sqb%d" % i, [128, 512], BF16) for i in range(2)]
            rtb = [s1("rtb%d" % i, [128, 512], F32) for i in range(2)]
            rrb = [s1("rrb%d" % i, [128, 512], F32) for i in range(2)]
            stg = [s1("stg%d" % i, [128, 512], BF16) for i in range(3)]
            vst = [s1("vst%d" % i, [128, 512], BF16) for i in range(3)]
            mixT = [s1("mixT%d" % i, [128, 8, 128], BF16) for i in range(2)]

            sch.dma(POOL, band_b[:], band_d, wr=["band"])
            sch.dma(POOL, poolw_b[:], pool_w.rearrange("g (kh p) c -> p g kh c", p=128), wr=["poolw"])
            sch.op(DVE, lambda: nc.vector.memset(pall[:, 0, :], 0.0), wr=["pall0"])

            w_in_v = w_in.rearrange("(dc p) c -> p dc c", p=128)
            cnt = {"qk": 0, "v": 0, "slab": 0, "x": 0}

            def load_x(gt):
                if gt >= 32:
                    return
                i = gt % 4
                sch.dma(SP if gt < 8 else POOL, xbuf[i][:], xs[gt * 128:(gt + 1) * 128, :], wr=["xbuf%d" % i])

            def load_slab(s):
                i = cnt["slab"] % 2
                cnt["slab"] += 1
                for a in range(4):
                    sch.dma(POOL, wsl[i][:, 4 * a:4 * a + 4, :], w_in_v[:, 4 * a:4 * a + 4, s * 512:(s + 1) * 512],
                            wr=["wsl%d_%d" % (i, a)])
                return i

            def slab_keys(i):
                return ["wsl%d_%d" % (i, a) for a in range(4)]

            post_q = []

            def flush_post():
                while post_q:
                    norm_post(*post_q.pop(0))

            def norm_pre(G, ti):
                gt = 8 * G + ti
                hi = gt % 2
                if any((8 * g + t) % 2 == hi for (g, t) in post_q):
                    flush_post()
                load_x(gt + 3)
                xi = gt % 4
                rmsnorm_tile(xbuf[xi][:], "xbuf%d" % xi, g1bc[:], hnb[hi][:], "hnb%d" % hi,
                             ssb[hi], rt1b[hi], rstdb[hi], hi)

            def norm_post(G, ti):
                gt = 8 * G + ti
                hi = gt % 2
                hb = G % 2
                for a in range(4):
                    bank = 5 + (a % 2)
                    fns = []
                    for b in range(4):
                        dc = 4 * a + b
                        fns.append(lambda dc=dc, b=b, bank=bank, hi=hi: nc.tensor.transpose(
                            psb[bank][:, b * 128:(b + 1) * 128], hnb[hi][:, dc * 128:(dc + 1) * 128], ident_b[:]))
                    sch.ops(PE, fns, rd=["hnb%d" % hi, "ident_b"], wr=["ps%d" % bank])
                    if a % 2 == 0:
                        sch.op(ACT, lambda a=a, bank=bank, ti=ti, hb=hb: nc.scalar.activation(
                            out=hnTb[hb][:, 4 * a:4 * a + 4, ti * 128:(ti + 1) * 128],
                            in_=psb[bank][:, 0:512].rearrange("p (a b) -> p a b", a=4), func=AF.Copy),
                            rd=["ps%d" % bank], wr=["hnT%d_%d" % (hb, ti // 4)])
                    else:
                        sch.op(DVE, lambda a=a, bank=bank, ti=ti, hb=hb: nc.vector.tensor_copy(
                            out=hnTb[hb][:, 4 * a:4 * a + 4, ti * 128:(ti + 1) * 128],
                            in_=psb[bank][:, 0:512].rearrange("p (a b) -> p a b", a=4)),
                            rd=["ps%d" % bank], wr=["hnT%d_%d" % (hb, ti // 4)])

            for gt in range(3):
                load_x(gt)
            pre_slab = load_slab(2)
            norm_pre(0, 0)
            for ti in range(8):
                if ti + 1 < 8:
                    norm_pre(0, ti + 1)
                norm_post(0, ti)
            nmt = 0
            for G in range(4):
                slabs = [2, 3, 4, 5] if G == 0 else list(range(8))
                hnT = hnTb[G % 2]
                hk = "hnT%d" % (G % 2)
                tps = 8 // len(slabs)
                for si, s in enumerate(slabs):
                    wi = pre_slab
                    if G + 1 < 4:
                        for ti in range(si * tps, (si + 1) * tps):
                            norm_pre(G + 1, ti)
                    if si + 1 < len(slabs):
                        pre_slab = load_slab(slabs[si + 1])
                    elif G + 1 < 4:
                        pre_slab = load_slab(0)
                    W = wsl[wi]
                    wk = slab_keys(wi)
                    if s < 4:
                        isq = s < 2
                        dst = qT_s if isq else kT_s
                        gcol = gqs[:] if isq else gqk[:, 1:2]
                        its = [(hh, c) for c in range(2) for hh in range(4)
                               if not (isq and G == 1 and c == 0)]

                        halo_q = isq and G == 1
                        nw = 128 if halo_q else 512
                        cofs = 384 if halo_q else 0

                        def qk_a(n, its=its, W=W, wk=wk, nw=nw, cofs=cofs):
                            hh, c = its[n]
                            if c == 1:
                                flush_post()
                            k = cnt["qk"] + n
                            pa, i2 = k % 3, k % 2
                            t0 = c * 512 + cofs
                            fns = [lambda dc=dc: nc.tensor.matmul(
                                ps[pa][:, 0:nw], W[:, dc, hh * 128:(hh + 1) * 128], hnT[:, dc, t0:t0 + nw],
                                start=(dc == 0), stop=(dc == 15)) for dc in range(16)]
                            sch.ops(PE, fns, rd=wk + [hk + "_%d" % c], wr=["ps%d" % pa])
                            sch.op(ACT, lambda: nc.scalar.activation(out=sqb[i2][:, 0:nw], in_=ps[pa][:, 0:nw],
                                                                     func=AF.Square),
                                   rd=["ps%d" % pa], wr=["sqb%d" % i2])

                        def qk_b(n, its=its, s=s, isq=isq, dst=dst, gcol=gcol, nw=nw, cofs=cofs):
                            hh, c = its[n]
                            head = (s % 2) * 4 + hh
                            k = cnt["qk"] + n
                            pa, i2, i3 = k % 3, k % 2, k % 3
                            pb = 3 + i2
                            sch.op(PE, lambda: nc.tensor.matmul(ps[pb][:, 0:nw], ones_b[:], sqb[i2][:, 0:nw],
                                                                start=True, stop=True),
                                   rd=["sqb%d" % i2, "ones_b"], wr=["ps%d" % pb])
                            sch.op(ACT, lambda: nc.scalar.activation(
                                out=rtb[i2][:, 0:nw], in_=ps[pb][:, 0:nw], func=AF.Ln, bias=epsc[:], scale=1.0 / HD),
                                rd=["ps%d" % pb, "epsc"], wr=["rtb%d" % i2])
                            sch.op(ACT, lambda: nc.scalar.activation(
                                out=rrb[i2][:, 0:nw], in_=rtb[i2][:, 0:nw], func=AF.Exp, bias=zcol[:], scale=-0.5),
                                rd=["rtb%d" % i2, "zcol"], wr=["rrb%d" % i2])
                            sch.op(DVE, lambda: nc.vector.scalar_tensor_tensor(
                                out=stg[i3][:, 0:nw], in0=ps[pa][:, 0:nw], scalar=gcol, in1=rrb[i2][:, 0:nw],
                                op0=ALU.mult, op1=ALU.mult),
                                rd=["ps%d" % pa, "rrb%d" % i2, "gqs", "gqk"], wr=["stg%d" % i3])
                            t0 = G * 1024 + c * 512 + cofs
                            sch.dma(SP, dst[head, :, t0:t0 + nw], stg[i3][:, 0:nw], rd=["stg%d" % i3],
                                    wr=[("qs" if isq else "ks", head, G, c)])

                        qk_a(0)
                        for n in range(len(its)):
                            if n + 1 < len(its):
                                qk_a(n + 1)
                            if n == 1:
                                flush_post()
                            qk_b(n)
                        flush_post()
                        cnt["qk"] += len(its)
                    else:
                        isv = s < 6
                        for ti in range(8):
                            gt = 8 * G + ti
                            if (not isv) and G == 1 and ti < 6:
                                continue
                            i2 = cnt["v"] % 2
                            i3 = cnt["v"] % 3
                            cnt["v"] += 1
                            pv = i3
                            if ti >= 4:
                                flush_post()
                            fns = [lambda dc=dc, ti=ti, pv=pv: nc.tensor.matmul(
                                ps[pv][:], hnT[:, dc, ti * 128:(ti + 1) * 128], W[:, dc, :],
                                start=(dc == 0), stop=(dc == 15)) for dc in range(16)]
                            sch.ops(PE, fns, rd=wk + [hk + "_%d" % (ti // 4)], wr=["ps%d" % pv])
                            if ti == 1:
                                flush_post()
                            if isv:
                                sch.op(DVE, lambda pv=pv, i3=i3: nc.vector.tensor_copy(out=vst[i3][:], in_=ps[pv][:]),
                                       rd=["ps%d" % pv], wr=["vst%d" % i3])
                                c0 = (s - 4) * 512
                                sch.dma(SP, v_s[gt, :, c0:c0 + 512], vst[i3][:], rd=["vst%d" % i3],
                                        wr=[("vs", gt, s)])
                            else:
                                c0 = (s - 6) * 512
                                sch.op(DVE, lambda pv=pv, ti=ti, c0=c0: nc.vector.tensor_copy(
                                    out=pall[:, 1 + ti, c0:c0 + 512], in_=ps[pv][:]),
                                    rd=["ps%d" % pv], wr=["pall%d" % (1 + ti)])
                    if G + 1 < 4:
                        for ti in range(si * tps, (si + 1) * tps):
                            post_q.append((G + 1, ti))
                ptiles = [ti for ti in range(8) if 8 * G + ti >= 15]

                def pool_x(ti, G=G):
                    gt = 8 * G + ti
                    kind = 2 if gt == 16 else 1
                    mx = gt % 2
                    for a in range(2):
                        bank = 0 + a
                        fns = []
                        for b in range(4):
                            c8 = 4 * a + b
                            g = c8 // 2
                            fns.append(lambda b=b, c8=c8, g=g: nc.tensor.matmul(
                                ps[bank][:, b * 128:(b + 1) * 128], pall[:, ti, c8 * 128:(c8 + 1) * 128],
                                band_b[:, 0, g, :], start=True, stop=False))
                            fns.append(lambda b=b, c8=c8, g=g: nc.tensor.matmul(
                                ps[bank][:, b * 128:(b + 1) * 128], pall[:, 1 + ti, c8 * 128:(c8 + 1) * 128],
                                band_b[:, kind, g, :], start=False, stop=True))
                        sch.ops(PE, fns, rd=["pall%d" % ti, "pall%d" % (1 + ti), "band"], wr=["ps%d" % bank])
                        sch.op(DVE, lambda a=a: nc.vector.tensor_copy(
                            out=mixT[mx][:, 4 * a:4 * a + 4, :], in_=ps[bank][:].rearrange("p (a b) -> p a b", a=4)),
                            rd=["ps%d" % bank], wr=["mixT%d_%d" % (mx, a)])

                def pool_y(ti, G=G):
                    gt = 8 * G + ti
                    qc = (gt - 15) * 128
                    mx = gt % 2
                    mi = gt % 2
                    for a in range(2):
                        bank = 2 + a
                        fns = []
                        for b in range(4):
                            c8o = 4 * a + b
                            g = c8o // 2
                            half = c8o % 2
                            for kh in range(2):
                                fns.append(lambda b=b, g=g, half=half, kh=kh: nc.tensor.matmul(
                                    ps[bank][:, b * 128:(b + 1) * 128],
                                    poolw_b[:, g, kh, half * 128:(half + 1) * 128], mixT[mx][:, 2 * g + kh, :],
                                    start=(kh == 0), stop=(kh == 1)))
                        sch.ops(PE, fns, rd=["mixT%d_0" % mx, "mixT%d_1" % mx, "poolw"], wr=["ps%d" % bank])
                        for b in range(4):
                            c8o = 4 * a + b
                            if a == 0:
                                sch.op(ACT, lambda b=b, c8o=c8o: nc.scalar.activation(
                                    out=mst[mi][:, c8o, :], in_=ps[bank][:, b * 128:(b + 1) * 128],
                                    func=AF.Identity, bias=zcol[:], scale=pscale[:, c8o:c8o + 1]),
                                    rd=["ps%d" % bank, "pscale", "zcol"], wr=["mst%d_%d" % (mi, a)])
                            else:
                                sch.op(DVE, lambda b=b, c8o=c8o: nc.vector.tensor_scalar(
                                    out=mst[mi][:, c8o, :], in0=ps[bank][:, b * 128:(b + 1) * 128],
                                    scalar1=pscale[:, c8o:c8o + 1], scalar2=None, op0=ALU.mult),
                                    rd=["ps%d" % bank, "pscale"], wr=["mst%d_%d" % (mi, a)])
                    sch.dma(SP, mT_s[:, :, qc:qc + 128].rearrange("c p t -> p c t"), mst[mi][:],
                            rd=["mst%d_0" % mi, "mst%d_1" % mi], wr=[("mTs", gt)])

                flush_post()
                if ptiles:
                    pool_x(ptiles[0])
                    for k, ti in enumerate(ptiles):
                        if k + 1 < len(ptiles):
                            pool_x(ptiles[k + 1])
                        pool_y(ti)
                if G <= 1:
                    flush_post()
                if G >= 1:
                    sch.op(DVE, lambda: nc.vector.tensor_copy(out=pall[:, 0, :], in_=pall[:, 8, :]),
                           rd=["pall8"], wr=["pall0"])
            sch.barrier()

        aT = sb("aT", [128, 8, QW], BF16, pm)
        wo_sb = sb("wo_sb", [128, 16, D], BF16, pm)
        w_out_v = w_out.rearrange("(cc p) d -> p cc d", p=128)
        with ExitStack() as p2:
            def s2(name, shape, dt):
                return sb(name, shape, dt, p2)
            kTh = [s2("kTh%d" % i, [128, S], BF16) for i in range(2)]
            vh = [s2("vh%d" % i, [128, 32, 128], BF16) for i in range(2)]
            qTh = [s2("qTh%d" % i, [128, QW], BF16) for i in range(2)]
            bTh = [s2("bTh%d" % i, [128, NBT, 512], BF16) for i in range(2)]
            MT = [s2("MT%d" % i, [128, QW], BF16) for i in range(2)]
            PT = [s2("PT%d" % i, [128, 512], BF16) for i in range(4)]
            rec = [s2("rec%d" % i, [128, 512], F32) for i in range(2)]
            lnd = [s2("lnd%d" % i, [128, 512], F32) for i in range(2)]
            accP = [s2("accP%d" % i, [128, 512], F32) for i in range(2)]
            accPb = [s2("accPb%d" % i, [128, 512], BF16) for i in range(2)]
            kmf = s2("kmf", [128, 16], F32)
            kmb = [s2("kmb%d" % i, [128, 16], BF16) for i in range(2)]
            gmA = [s2("gmA%d" % i, [128, NQT, 16], F32) for i in range(2)]
            top8 = [s2("top8%d" % i, [128, NQT, 8], F32) for i in range(2)]
            thrA = [s2("thrA%d" % i, [128, NQT], F32) for i in range(2)]
            nselA = [s2("nselA%d" % i, [128, NQT, 16], F32) for i in range(2)]
            maddA = [s2("maddA%d" % i, [128, NQT, 16], F32) for i in range(2)]

            def load_head(h):
                i = h % 2
                sch.dma(SP, kTh[i][:], kT_s[h], wr=["kTh%d" % i])
                sch.dma(SP, qTh[i][:], qT_s[h, :, 1920:S], wr=["qTh%d" % i])
                for a in range(4):
                    sch.dma(POOL, vh[i][:, 8 * a:8 * a + 8, :],
                            v_s[8 * a:8 * a + 8, :, h * 128:(h + 1) * 128].rearrange("t p c -> p t c"),
                            wr=["vh%d_%d" % (i, a)])
                for a in range(NBT):
                    sch.dma(POOL, bTh[i][:, a, :], biasT_d[h, :, a, :], wr=["bTh%d_%d" % (i, a)])

            def gate_stage1a(h, part):
                i = h % 2
                sch.op(DVE, lambda: nc.vector.tensor_reduce(
                    out=kmf[:, 4 * part:4 * part + 4],
                    in_=kTh[i][:, 1024 * part:1024 * (part + 1)].rearrange("p (n k) -> p n k", n=4),
                    axis=AX.X, op=ALU.add), rd=["kTh%d" % i], wr=["kmf%d" % part])
                if part == 3:
                    sch.op(DVE, lambda: nc.vector.tensor_scalar(out=kmb[i][:], in0=kmf[:], scalar1=1.0 / 256.0,
                                                                scalar2=None, op0=ALU.mult),
                           rd=["kmf%d" % p for p in range(4)], wr=["kmb%d" % i])

            def gate_stage1b(h, piece=None):
                i = h % 2
                pcs = range(5) if piece is None else [piece]
                for pc in pcs:
                    if pc == 0:
                        fns = [lambda qi=qi: nc.tensor.matmul(
                            ps[7][:, qi * 16:(qi + 1) * 16], qTh[i][:, qi * 128:(qi + 1) * 128], kmb[i][:],
                            start=True, stop=True) for qi in range(NQT)]
                        sch.ops(PE, fns, rd=["qTh%d" % i, "kmb%d" % i], wr=["ps7"])
                        sch.op(DVE, lambda: nc.vector.tensor_tensor(
                            out=gmA[i][:], in0=ps[7][:, 0:NQT * 16].rearrange("p (a b) -> p a b", a=NQT),
                            in1=gmask[:], op=ALU.add), rd=["ps7", "gmask"], wr=["gmA%d" % i])
                    elif pc in (1, 2, 3):
                        qs = [range(0, 6), range(6, 12), range(12, NQT)][pc - 1]
                        fns = [lambda qi=qi: nc.vector.max(out=top8[i][:, qi, :], in_=gmA[i][:, qi, :]) for qi in qs]
                        sch.ops(DVE, fns, rd=["gmA%d" % i], wr=["top8%d_%d" % (i, pc)])
                    else:
                        sch.op(DVE, lambda: nc.vector.tensor_scalar(
                            out=thrA[i][:], in0=top8[i][:, :, 2], scalar1=-1e29, scalar2=None, op0=ALU.max),
                            rd=["top8%d_%d" % (i, p) for p in (1, 2, 3)], wr=["thrA%d" % i])
                        sch.op(DVE, lambda: nc.vector.tensor_tensor(
                            out=nselA[i][:], in0=gmA[i][:],
                            in1=thrA[i][:].unsqueeze(2).to_broadcast([128, NQT, 16]), op=ALU.is_lt),
                            rd=["gmA%d" % i, "thrA%d" % i], wr=["nselA%d" % i])
                        sch.op(DVE, lambda: nc.vector.scalar_tensor_tensor(
                            out=maddA[i][:], in0=nselA[i][:], scalar=-BIG, in1=notown[:], op0=ALU.mult, op1=ALU.mult),
                            rd=["nselA%d" % i, "notown"], wr=["maddA%d" % i])

            def gate_stage2_pe(h, rnd):
                i = h % 2
                qis = list(range(4 * rnd, min(4 * rnd + 4, NQT)))
                fns = [lambda k=k, qi=qi: nc.tensor.transpose(
                    ps[5][0:16, k * 128:(k + 1) * 128], maddA[i][:, qi, :], ident_f[:])
                    for k, qi in enumerate(qis)]
                sch.ops(PE, fns, rd=["maddA%d" % i, "ident_f"], wr=["ps5"])

            def gate_stage2_act(h, rnd):
                i = h % 2
                qis = list(range(4 * rnd, min(4 * rnd + 4, NQT)))
                n = len(qis)
                q0 = qis[0]
                sch.op(ACT, lambda: nc.scalar.activation(
                    out=MT[i][0:16, q0 * 128:(q0 + n) * 128], in_=ps[5][0:16, 0:n * 128], func=AF.Copy),
                    rd=["ps5"], wr=["MT%d" % i])

            accPh = s2("accPh", [128, 512], F32)
            accPbh = s2("accPbh", [128, 512], BF16)
            for i in range(2):
                sch.op(DVE, lambda i=i: nc.vector.memset(MT[i][:], 0.0), wr=["MT%d" % i])
            load_head(0)
            for part in range(4):
                gate_stage1a(0, part)
            gate_stage1b(0)
            for r in range(5):
                gate_stage2_pe(0, r)
                gate_stage2_act(0, r)

            SB = [0, 1, 2, 6]
            PTK = {0: 0, 1: 1, 2: 2, 6: 3}
            U = []
            nreg = 0
            for h in range(NH):
                hu = []
                halo = dict(h=h, halo=True, q0=0, nq=128, Q0=1920, nkt=16, pOap=ps[7][:, 384:512], pOk="ps7h",
                            acc=accPh, accb=accPbh, acck="accPh", accbk="accPbh", ab=0)
                hunits = [dict(ctx=halo, kts=list(range(4 * g, 4 * g + 4))) for g in range(4)]
                for c in range(4):
                    ab = nreg % 2
                    nreg += 1
                    ctx = dict(h=h, halo=False, q0=128 + 512 * c, nq=512, Q0=2048 + 512 * c,
                               nkt=(2048 + 512 * c + 512) // 128, pOap=ps[3 + ab][:, 0:512], pOk="ps%d" % (3 + ab),
                               acc=accP[ab], accb=accPb[ab], acck="accP%d" % ab, accbk="accPb%d" % ab, ab=ab)
                    for kt in range(ctx["nkt"]):
                        hu.append(dict(ctx=ctx, kts=[kt]))
                        if c == 0 and kt % 4 == 3 and kt // 4 < 4:
                            hu.append(hunits[kt // 4])
                for k, u in enumerate(hu):
                    u["hidx"] = k
                U += hu
            NU = len(U)
            cnt2 = {"s": 0}
            later = {}

            def at(n, fn):
                later.setdefault(min(n, NU - 1), []).append(fn)

            def scores(n):
                u = U[n]
                c = u["ctx"]
                i = c["h"] % 2
                nq, q0, Q0 = c["nq"], c["q0"], c["Q0"]
                sbk = SB[cnt2["s"] % 4]
                cnt2["s"] += 1
                u["sbk"] = sbk
                fns = []
                bdeps = []
                for idx, kt in enumerate(u["kts"]):
                    D0 = Q0 - 128 * kt
                    near = D0 <= 896
                    assert u.setdefault("near", near) == near
                    if near:
                        bdeps.append("bTh%d_%d" % (i, (D0 + 384) // 128))
                    cs = slice(idx * nq, (idx + 1) * nq)
                    opnds = [(kTh[i][:, kt * 128:(kt + 1) * 128], qTh[i][:, q0:q0 + nq])]
                    if near:
                        j = (D0 + 384) // 128
                        opnds.append((ident_b[:], bTh[i][:, j, 0:nq]))
                    if kt < c["nkt"] - 2:
                        opnds.append((esel[:, kt // 2, :], MT[i][:, q0:q0 + nq]))
                    for oi, (lh, rh) in enumerate(opnds):
                        fns.append(lambda lh=lh, rh=rh, cs=cs, oi=oi, no=len(opnds): nc.tensor.matmul(
                            ps[sbk][:, cs], lh, rh, start=(oi == 0), stop=(oi == no - 1)))
                sch.ops(PE, fns, rd=["kTh%d" % i, "qTh%d" % i, "MT%d" % i, "esel", "ident_b"] + bdeps,
                        wr=["ps%d" % sbk])

            def finalizeA0(c):
                sch.op(DVE, lambda: nc.vector.tensor_copy(out=c["accb"][:, 0:512], in_=c["acc"][:, 0:512]),
                       rd=[c["acck"]], wr=[c["accbk"]])

            def finalizeA(c):
                nq = c["nq"]
                grp = 512 // nq
                fns = [lambda g=g: nc.tensor.matmul(ps[5][:, 0:nq], ones_b[:], c["accb"][:, g * nq:(g + 1) * nq],
                                                    start=(g == 0), stop=(g == grp - 1)) for g in range(grp)]
                sch.ops(PE, fns, rd=[c["accbk"], "ones_b"], wr=["ps5"])

            def finalizeB(c):
                nq, ab, h, q0 = c["nq"], c["ab"], c["h"], c["q0"]
                sch.op(ACT, lambda: nc.scalar.activation(out=lnd[ab][:, 0:nq], in_=ps[5][:, 0:nq],
                                                         func=AF.Ln, bias=zcol[:], scale=1.0),
                       rd=["ps5", "zcol"], wr=["lnd%d" % ab])

            def finalizeC(c):
                nq, ab, h, q0 = c["nq"], c["ab"], c["h"], c["q0"]
                sch.op(ACT, lambda: nc.scalar.activation(out=rec[ab][:, 0:nq], in_=lnd[ab][:, 0:nq],
                                                         func=AF.Exp, bias=zcol[:], scale=-1.0),
                       rd=["lnd%d" % ab, "zcol"], wr=["rec%d" % ab])
                sch.op(DVE, lambda: nc.vector.tensor_tensor(
                    out=aT[:, h, q0:q0 + nq], in0=c["pOap"][:, 0:nq], in1=rec[ab][:, 0:nq], op=ALU.mult),
                    rd=[c["pOk"], "rec%d" % ab], wr=["aT"])

            scores(0)
            scores(1)
            for n in range(NU):
                u = U[n]
                c = u["ctx"]
                h = c["h"]
                i = h % 2
                nq = c["nq"]
                k = u["hidx"]
                W = nq * len(u["kts"])
                if k == 0:
                    if h + 1 < NH:
                        load_head(h + 1)
                    if h == 0:
                        for cc in range(16):
                            sch.dma(POOL, wo_sb[:, cc, :], w_out_v[:, cc, :], wr=["wo%d" % cc])
                if h + 1 < NH:
                    if k in (34, 36, 38, 40):
                        gate_stage1a(h + 1, (k - 34) // 2)
                    if 42 <= k <= 46:
                        gate_stage1b(h + 1, k - 42)
                    if k >= 60 and (k - 60) % 3 == 0 and (k - 60) // 3 < 5:
                        gate_stage2_pe(h + 1, (k - 60) // 3)
                if n + 2 < NU:
                    scores(n + 2)
                sbk = u["sbk"]
                pk = PTK[sbk]
                bcol = zcol[:] if u["near"] else bias31[:, h:h + 1]
                sch.op(ACT, lambda: nc.scalar.activation(
                    out=PT[pk][:, 0:W], in_=ps[sbk][:, 0:W], func=AF.Exp, bias=bcol, scale=1.0),
                    rd=["ps%d" % sbk, "bias31", "zcol"], wr=["PT%d" % pk])
                if h + 1 < NH and k >= 60 and (k - 60) % 3 == 0 and (k - 60) // 3 < 5:
                    gate_stage2_act(h + 1, (k - 60) // 3)
                fns = []
                for idx, kt in enumerate(u["kts"]):
                    fns.append(lambda idx=idx, kt=kt: nc.tensor.matmul(
                        c["pOap"][:, 0:nq], vh[i][:, kt, :], PT[pk][:, idx * nq:(idx + 1) * nq],
                        start=(kt == 0), stop=(kt == c["nkt"] - 1)))
                sch.ops(PE, fns, rd=["PT%d" % pk] + ["vh%d_%d" % (i, a) for a in range(4)], wr=[c["pOk"]])
                if u["kts"][0] == 0:
                    sch.op(DVE, lambda: nc.vector.tensor_copy(out=c["acc"][:, 0:W], in_=PT[pk][:, 0:W]),
                           rd=["PT%d" % pk], wr=[c["acck"]])
                else:
                    sch.op(DVE, lambda: nc.vector.tensor_tensor(
                        out=c["acc"][:, 0:W], in0=c["acc"][:, 0:W], in1=PT[pk][:, 0:W], op=ALU.add),
                        rd=["PT%d" % pk, c["acck"]], wr=[c["acck"]])
                if u["kts"][-1] == c["nkt"] - 1:
                    at(n + 3, lambda c=c: finalizeA0(c))
                    at(n + 6, lambda c=c: finalizeA(c))
                    at(n + 8, lambda c=c: finalizeB(c))
                    at(n + 10, lambda c=c: finalizeC(c))
                for fn in later.pop(n, []):
                    fn()
            sch.barrier()

        with ExitStack() as p3:
            def s3(name, shape, dt):
                return sb(name, shape, dt, p3)
            mT = s3("mT", [128, 8, QW], BF16)
            g2bc = s3("g2bc", [128, D], F32)
            xo = [s3("xo%d" % i, [128, D], F32) for i in range(2)]
            x1t = [s3("x1t%d" % i, [128, D], F32) for i in range(2)]
            hn2 = [s3("hn2_%d" % i, [128, D], BF16) for i in range(2)]
            ss2 = [s3("ss2_%d" % i, [128, 1], F32) for i in range(2)]
            rt2 = [s3("rt2_%d" % i, [128, 1], F32) for i in range(2)]
            rs2 = [s3("rs2_%d" % i, [128, 1], F32) for i in range(2)]
            hst = [s3("hst%d" % i, [128, 16, 128], BF16) for i in range(2)]
            sch.dma(POOL, g2bc[:], g2bc_d, wr=["gbc"])
            for cc in range(8):
                sch.dma(POOL, mT[:, cc, :], mT_s[cc], wr=["mT%d" % cc])
            if debug:
                sch.dma(SP, mix_dbg[:, 0:8, :], aT[:], rd=["aT"])
                sch.dma(SP, mix_dbg[:, 8:16, :], mT[:], rd=["mT%d" % cc for cc in range(8)])
            wokeys = ["wo%d" % cc for cc in range(16)]

            def load_xo(qi):
                gt = 15 + qi
                sch.dma(POOL, xo[qi % 2][:], xs[gt * 128:(gt + 1) * 128, :], wr=["xo%d" % (qi % 2)])

            def o_mm(qi):
                i = qi % 2
                if qi + 1 < NQT:
                    load_xo(qi + 1)
                for sl in range(4):
                    bank = (4 * qi + sl) % 6
                    fns = []
                    for cc in range(16):
                        src = aT if cc < 8 else mT
                        fns.append(lambda cc=cc, src=src: nc.tensor.matmul(
                            ps[bank][:], src[:, cc % 8, qi * 128:(qi + 1) * 128], wo_sb[:, cc, sl * 512:(sl + 1) * 512],
                            start=(cc == 0), stop=(cc == 15)))
                    sch.ops(PE, fns, rd=["aT"] + ["mT%d" % cc for cc in range(8)] + wokeys, wr=["ps%d" % bank])
                    sch.op(DVE, lambda: nc.vector.tensor_tensor(
                        out=x1t[i][:, sl * 512:(sl + 1) * 512], in0=ps[bank][:], in1=xo[i][:, sl * 512:(sl + 1) * 512],
                        op=ALU.add), rd=["ps%d" % bank, "xo%d" % i], wr=["x1t%d" % i])
                sch.dma(SP, x1_s[qi], x1t[i][:], rd=["x1t%d" % i], wr=[("x1s", qi)])

            def o_norm(qi):
                i = qi % 2
                rmsnorm_tile(x1t[i][:], "x1t%d" % i, g2bc[:], hn2[i][:], "hn2_%d" % i, ss2[i], rt2[i], rs2[i], i)

            def o_tr(qi):
                i = qi % 2
                for a in range(4):
                    bank = 6 + (a % 2)
                    fns = []
                    for b in range(4):
                        dc = 4 * a + b
                        fns.append(lambda dc=dc, b=b: nc.tensor.transpose(
                            psb[bank][:, b * 128:(b + 1) * 128], hn2[i][:, dc * 128:(dc + 1) * 128], ident_b[:]))
                    sch.ops(PE, fns, rd=["hn2_%d" % i, "ident_b"], wr=["ps%d" % bank])
                    sch.op(ACT, lambda a=a: nc.scalar.activation(
                        out=hst[i][:, 4 * a:4 * a + 4, :],
                        in_=psb[bank][:, 0:512].rearrange("p (a b) -> p a b", a=4), func=AF.Copy),
                        rd=["ps%d" % bank], wr=["hst%d" % i])
                sch.dma(SP, hn2T_s[:, :, qi * 128:(qi + 1) * 128].rearrange("dc p t -> p dc t"), hst[i][:],
                        rd=["hst%d" % i], wr=[("hn2Ts", qi)])

            load_xo(0)
            o_mm(0)
            o_norm(0)
            for qi in range(NQT):
                if qi + 1 < NQT:
                    o_mm(qi + 1)
                o_tr(qi)
                if qi + 1 < NQT:
                    o_norm(qi + 1)
            sch.barrier()

        pm.close()
        with ExitStack() as p4:
            def s4(name, shape, dt):
                return sb(name, shape, dt, p4)
            gT = s4("gT", [128, NJ, 1024], BF16)
            halo_st = s4("halo_st", [128, NJ, 2, 2], F32)
            cw = s4("cw", [128, 3, 88], F32)
            cb = s4("cb", [128, 88], F32)
            sch.dma(SP, cw[:], cw_d, wr=["cw"])
            sch.dma(SP, cb[:], cb_d, wr=["cb"])
            w_up_v = w_up.rearrange("(dc p) f -> p dc f", p=128)
            w_down_v = w_down.rearrange("(j p) d -> p j d", p=128)
            wub = [s4("wub%d" % i, [128, 2, 16, 128], BF16) for i in range(2)]
            wdb = [s4("wdb%d" % i, [128, NJ, 128], BF16) for i in range(2)]

            def load_wu(j):
                i = j % 2
                for gv in range(2):
                    c0 = gv * DFF + j * 128
                    sch.dma(POOL, wub[i][:, gv, :, :], w_up_v[:, :, c0:c0 + 128], wr=["wub%d_%d" % (i, gv)])

            def load_wd(m):
                i = m % 2
                for a in range(4):
                    sch.dma(POOL, wdb[i][:, 11 * a:11 * a + 11, :],
                            w_down_v[:, 11 * a:11 * a + 11, m * 128:(m + 1) * 128], wr=["wdb%d_%d" % (i, a)])
            load_wu(0)
            out_v = out_d.rearrange("(t p) c -> p t c", p=128)
            for Gf in range(2):
                with ExitStack() as pu:
                    def su(name, shape, dt):
                        return sb(name, shape, dt, pu)
                    hn2T = su("hn2T", [128, 16, 1024], BF16)
                    hn2Th = su("hn2Th", [128, 16, 2], BF16)
                    c0 = 128 + 1024 * Gf
                    for cch in range(2):
                        for a in range(4):
                            sch.dma(SP if a % 2 == 0 else POOL,
                                    hn2T[:, 4 * a:4 * a + 4, cch * 512:(cch + 1) * 512],
                                    hn2T_s[4 * a:4 * a + 4, :, c0 + cch * 512:c0 + (cch + 1) * 512].rearrange(
                                        "dc p t -> p dc t"),
                                    wr=["hn2T_%d_%d" % (a, cch)])
                    if Gf == 0:
                        sch.dma(POOL, hn2Th[:], hn2T_s[:, :, 126:128].rearrange("dc p t -> p dc t"), wr=["hn2Th"])
                    ugb = [su("ugb%d" % i, [128, 1026], F32) for i in range(2)]
                    uvb = [su("uvb%d" % i, [128, 1026], F32) for i in range(2)]
                    ygb = [su("ygb%d" % i, [128, 512], F32) for i in range(2)]
                    yvb = [su("yvb%d" % i, [128, 512], F32) for i in range(2)]
                    sgb = [su("sgb%d" % i, [128, 512], F32) for i in range(2)]

                    ne = 0
                    for j in range(NJ):
                        if j + 1 < NJ:
                            load_wu(j + 1)
                        else:
                            load_wd(0)
                        wi = j % 2
                        ug, uv = ugb[wi], uvb[wi]
                        ugk, uvk = "ug%d" % wi, "uv%d" % wi
                        wk = ["wub%d_0" % wi, "wub%d_1" % wi]
                        def emit_halo(ug=ug, uv=uv, ugk=ugk, uvk=uvk, wk=wk, wi=wi, j=j):
                          if Gf == 0:
                            fns = []
                            for gv in range(2):
                                for dc in range(16):
                                    fns.append(lambda gv=gv, dc=dc, wi=wi: nc.tensor.matmul(
                                        ps[7][:, gv * 2:gv * 2 + 2], wub[wi][:, gv, dc, :], hn2Th[:, dc, :],
                                        start=(dc == 0), stop=(dc == 15)))
                            sch.ops(PE, fns, rd=wk + ["hn2Th"], wr=["ps7"])
                            sch.op(DVE, lambda ug=ug: nc.vector.tensor_scalar(
                                out=ug[:, 0:2], in0=ps[7][:, 0:2], scalar1=flag[:], scalar2=None, op0=ALU.mult),
                                rd=["ps7", "flag"], wr=[ugk + "h"])
                            sch.op(DVE, lambda uv=uv: nc.vector.tensor_scalar(
                                out=uv[:, 0:2], in0=ps[7][:, 2:4], scalar1=flag[:], scalar2=None, op0=ALU.mult),
                                rd=["ps7", "flag"], wr=[uvk + "h"])
                          else:
                            sch.op(DVE, lambda ug=ug, j=j: nc.vector.tensor_copy(out=ug[:, 0:2],
                                                                              in_=halo_st[:, j, 0, :]),
                                   rd=["halo_st"], wr=[ugk + "h"])
                            sch.op(DVE, lambda uv=uv, j=j: nc.vector.tensor_copy(out=uv[:, 0:2],
                                                                              in_=halo_st[:, j, 1, :]),
                                   rd=["halo_st"], wr=[uvk + "h"])
                        for c in range(2):
                            e2 = ne % 2
                            ne += 1
                            pg, pv = 0 + e2, 2 + e2
                            for gv, pbank in ((0, pg), (1, pv)):
                                fns = [lambda gv=gv, dc=dc, pbank=pbank, c=c, wi=wi: nc.tensor.matmul(
                                    ps[pbank][:], wub[wi][:, gv, dc, :], hn2T[:, dc, c * 512:(c + 1) * 512],
                                    start=(dc == 0), stop=(dc == 15)) for dc in range(16)]
                                sch.ops(PE, fns, rd=wk + ["hn2T_%d_%d" % (a, c) for a in range(4)], wr=["ps%d" % pbank])
                            if c == 0:
                                emit_halo()
                            lo = 2 + 512 * c
                            sch.op(ACT, lambda ug=ug, pg=pg, lo=lo: nc.scalar.activation(
                                out=ug[:, lo:lo + 512], in_=ps[pg][:], func=AF.Copy),
                                rd=["ps%d" % pg], wr=[ugk + "c%d" % c])
                            sch.op(ACT, lambda uv=uv, pv=pv, lo=lo: nc.scalar.activation(
                                out=uv[:, lo:lo + 512], in_=ps[pv][:], func=AF.Copy),
                                rd=["ps%d" % pv], wr=[uvk + "c%d" % c])
                            for (u, uk, y, yk, jj) in ((ug, ugk, ygb[e2], "yg%d" % e2, j),
                                                       (uv, uvk, yvb[e2], "yv%d" % e2, NJ + j)):
                                urd = [uk + "h", uk + "c0", uk + "c1"] if c == 1 else [uk + "h", uk + "c0"]
                                sch.op(DVE, lambda u=u, y=y, jj=jj, lo=lo: nc.vector.tensor_scalar(
                                    out=y[:], in0=u[:, lo:lo + 512], scalar1=cw[:, 2, jj:jj + 1],
                                    scalar2=cb[:, jj:jj + 1], op0=ALU.mult, op1=ALU.add),
                                    rd=urd + ["cw", "cb"], wr=[yk])
                                sch.op(DVE, lambda u=u, y=y, jj=jj, lo=lo: nc.vector.scalar_tensor_tensor(
                                    out=y[:], in0=u[:, lo - 1:lo + 511], scalar=cw[:, 1, jj:jj + 1], in1=y[:],
                                    op0=ALU.mult, op1=ALU.add), rd=urd + ["cw", yk], wr=[yk])
                                sch.op(DVE, lambda u=u, y=y, jj=jj, lo=lo: nc.vector.scalar_tensor_tensor(
                                    out=y[:], in0=u[:, lo - 2:lo + 510], scalar=cw[:, 0, jj:jj + 1], in1=y[:],
                                    op0=ALU.mult, op1=ALU.add), rd=urd + ["cw", yk], wr=[yk])
                            sch.op(ACT, lambda e2=e2: nc.scalar.activation(out=sgb[e2][:], in_=ygb[e2][:],
                                                                            func=AF.Silu),
                                   rd=["yg%d" % e2], wr=["sg%d" % e2])
                            sch.op(DVE, lambda e2=e2, j=j, c=c: nc.vector.tensor_tensor(
                                out=gT[:, j, c * 512:(c + 1) * 512], in0=sgb[e2][:], in1=yvb[e2][:], op=ALU.mult),
                                rd=["sg%d" % e2, "yv%d" % e2], wr=["gT"])
                        if Gf == 0:
                            sch.op(DVE, lambda ug=ug, j=j: nc.vector.tensor_copy(out=halo_st[:, j, 0, :],
                                                                              in_=ug[:, 1024:1026]),
                                   rd=[ugk + "c1"], wr=["halo_st"])
                            sch.op(DVE, lambda uv=uv, j=j: nc.vector.tensor_copy(out=halo_st[:, j, 1, :],
                                                                              in_=uv[:, 1024:1026]),
                                   rd=[uvk + "c1"], wr=["halo_st"])
                    sch.barrier(keep=["wdb0_%d" % a for a in range(4)])
                with ExitStack() as pd:
                    def sd(name, shape, dt):
                        return sb(name, shape, dt, pd)
                    x1g = [sd("x1g%d" % i, [128, 8, 512], F32) for i in range(2)]
                    obuf = [sd("obuf%d" % i, [128, 8, 512], F32) for i in range(2)]
                    yT = [sd("yT%d" % i, [128, 512], F32) for i in range(2)]

                    its = [(mg, mm, c) for mg in range(4) for mm in range(4) for c in range(2)]

                    def d_mm(n):
                        mg, mm, c = its[n]
                        m = 4 * mg + mm
                        if c == 0 and m + 1 < 16:
                            load_wd(m + 1)
                        if c == 0 and m == 15 and Gf == 0:
                            load_wu(0)
                        if mm == 0 and c == 0:
                            gi = mg % 2
                            qa = 1 + 8 * Gf
                            sch.dma(POOL, x1g[gi][:],
                                    x1_s[qa:qa + 8, :, mg * 512:(mg + 1) * 512].rearrange("t p c -> p t c"),
                                    wr=["x1g%d" % gi])
                        wi = m % 2
                        wk = ["wdb%d_%d" % (wi, a) for a in range(4)]
                        pa = n % 2
                        fns = [lambda j=j: nc.tensor.matmul(
                            ps[pa][:], wdb[wi][:, j, :], gT[:, j, c * 512:(c + 1) * 512],
                            start=(j == 0), stop=(j == NJ - 1)) for j in range(NJ)]
                        sch.ops(PE, fns, rd=wk + ["gT"], wr=["ps%d" % pa])

                    def d_post(n):
                        mg, mm, c = its[n]
                        gi = mg % 2
                        e2 = n % 2
                        pa, pt = e2, 2 + e2
                        sch.op(ACT, lambda: nc.scalar.activation(out=yT[e2][:], in_=ps[pa][:], func=AF.Copy),
                               rd=["ps%d" % pa], wr=["yT%d" % e2])
                        if n + 1 < len(its):
                            d_mm(n + 1)
                        fns = [lambda k=k: nc.tensor.transpose(
                            ps[pt][:, k * 128:(k + 1) * 128], yT[e2][:, k * 128:(k + 1) * 128], ident_f[:])
                            for k in range(4)]
                        sch.ops(PE, fns, rd=["yT%d" % e2, "ident_f"], wr=["ps%d" % pt])
                        sch.op(DVE, lambda: nc.vector.tensor_tensor(
                            out=obuf[gi][:, 4 * c:4 * c + 4, mm * 128:(mm + 1) * 128],
                            in0=ps[pt][:].rearrange("p (a b) -> p a b", a=4),
                            in1=x1g[gi][:, 4 * c:4 * c + 4, mm * 128:(mm + 1) * 128], op=ALU.add),
                            rd=["ps%d" % pt, "x1g%d" % gi], wr=["obuf%d" % gi])
                        if mm == 3 and c == 1:
                            sch.dma(SP, out_v[:, 8 * Gf:8 * Gf + 8, mg * 512:(mg + 1) * 512], obuf[gi][:],
                                    rd=["obuf%d" % gi], wr=[("out", Gf, mg)])

                    d_mm(0)
                    for n in range(len(its)):
                        d_post(n)
                    sch.barrier(keep=["wub0_0", "wub0_1"])
        sch.barrier()
    return nc


def _rel_bucket_np(n):
    n = np.maximum(n, 0)
    max_exact = 16
    nf = np.maximum(n, max_exact).astype(np.float32)
    large = max_exact + (np.log(nf / np.float32(max_exact)) / np.float32(math.log(1024 / max_exact))
                         * np.float32(32 - max_exact)).astype(np.int32)
    large = np.minimum(large, 31)
    return np.where(n < max_exact, n, large)


def _static_consts():
    c = {}
    c["ident"] = np.eye(128, dtype=np.float32)
    es = np.zeros((128, 16, 128), np.float32)
    for n in range(16):
        es[n, n, :] = 1.0
    c["esel"] = es
    k = np.arange(128)[:, None]
    q = np.arange(512)[None, :]
    dist = np.stack([(-384 + 128 * j) + q - k for j in range(NBT)], 0)
    c["dist"] = dist
    c["bucket"] = _rel_bucket_np(dist)
    tp = np.arange(128)[:, None]
    t = np.arange(128)[None, :]
    band = np.zeros((3, 4, 128, 128), np.float32)
    for g, w in enumerate((2, 4, 8, 16)):
        band[0, g] = np.where(tp >= t + 129 - w, 1.0 / w, 0.0)
        incl = (tp <= t) & (tp > t - w)
        band[1, g] = np.where(incl, 1.0 / w, 0.0) - (tp == t)
        cntf = np.minimum(t + 1, w).astype(np.float32)
        band[2, g] = np.where(incl, 1.0 / cntf, 0.0) - (tp == t)
    c["band"] = band
    no = np.ones((NQT, 16), np.float32)
    for qi in range(NQT):
        no[qi, (15 + qi) // 2] = 0.0
    c["notown"] = np.ascontiguousarray(np.broadcast_to(no[None], (128, NQT, 16)))
    return c


def _prep_inputs(inp):
    f32 = np.float32
    x = np.asarray(inp["x"], f32)
    cst = _static_consts()
    rel_bias = np.asarray(inp["rel_bias"], f32)
    bt = rel_bias[:, cst["bucket"]]
    bt = np.where(cst["dist"][None] >= 0, bt, f32(-BIG)).astype(f32)
    biasT = np.ascontiguousarray(bt.transpose(0, 2, 1, 3))
    rep = lambda v: np.ascontiguousarray(np.broadcast_to(np.asarray(v, f32).reshape(1, -1), (128, v.size)))
    common = {
        "w_in": np.ascontiguousarray(inp["w_in"][0], f32),
        "w_out": np.ascontiguousarray(inp["w_out"][0], f32),
        "w_up": np.ascontiguousarray(inp["w_up"][0], f32),
        "w_down": np.ascontiguousarray(inp["w_down"][0], f32),
        "pool_w": np.ascontiguousarray(inp["pool_w"][0], f32),
        "g1bc": rep(np.asarray(inp["attn_norm_g"][0])),
        "g2bc": rep(np.asarray(inp["ffn_norm_g"][0])),
        "gqk": np.ascontiguousarray(np.stack([np.asarray(inp["q_norm_g"][0], f32),
                                              np.asarray(inp["k_norm_g"][0], f32)], 1)),
        "pscale": np.ascontiguousarray(np.asarray(inp["pool_scale"][0], f32).reshape(8, 128).T),
        "cw": np.ascontiguousarray(np.asarray(inp["conv_w"][0], f32).reshape(3, 88, 128).transpose(2, 0, 1)),
        "cb": np.ascontiguousarray(np.asarray(inp["conv_b"][0], f32).reshape(88, 128).T),
        "bias31": rep(rel_bias[:, 31]),
        "biasT": biasT,
        "notown": cst["notown"],
        "ident": cst["ident"],
        "esel": cst["esel"],
    }
    maps = []
    for core in range(8):
        b, r = core // 2, core % 2
        m = dict(common)
        if r == 1:
            m["xs"] = np.ascontiguousarray(x[b])
        else:
            m["xs"] = np.ascontiguousarray(np.concatenate([np.zeros((2048, D), f32), x[b, :2048]], 0))
        band = cst["band"].copy()
        if r == 1:
            band[2] = band[1]
        m["band"] = np.ascontiguousarray(band.transpose(2, 0, 1, 3))
        first = 0 if r == 1 else 8
        gmk = np.full((NQT, 16), -1e30, f32)
        for qi in range(NQT):
            for n in range(16):
                if first <= n < (15 + qi) // 2:
                    gmk[qi, n] = 0.0
        m["gmask"] = np.ascontiguousarray(np.broadcast_to(gmk[None], (128, NQT, 16)))
        m["flag"] = np.full((128, 1), float(r), f32)
        maps.append(m)
    return maps


_NC_CACHE = {}


def kernel(**inputs):
    maps = _prep_inputs(inputs)
    if "nc" not in _NC_CACHE:
        _NC_CACHE["nc"] = build_program(DEBUG)
    nc = _NC_CACHE["nc"]
    res = run_bass_kernel_spmd(nc, maps, core_ids=list(range(8)))
    out = np.empty((4, S, D), np.float32)
    for core in range(8):
        b, r = core // 2, core % 2
        out[b, r * 2048:(r + 1) * 2048] = res.results[core]["out"]
    if DEBUG:
        kernel.last = res
    return out
```

```python
import math
import numpy as np
import concourse.bass as bass
import concourse.mybir as mybir
from concourse.bass_utils import run_bass_kernel_spmd

F32 = mybir.dt.float32
BF16 = mybir.dt.bfloat16
AF = mybir.ActivationFunctionType
ALU = mybir.AluOpType
AX = mybir.AxisListType

D = 2048
S = 4096
HD = 128
NH = 8
INW = 4096
DFF = 5632
NJ = DFF // 128
EPS = 1e-6
BIG = 32768.0
NQT = 17
QW = NQT * 128
NBT = 11
DEBUG = False


class Eng:
    def __init__(self, name, h, sem, is_pe=False):
        self.name, self.h, self.sem, self.n, self.is_pe = name, h, sem, 0, is_pe
        self.seen = {}
        self.dsems = []
        self.dvals = []
        self.dnext = 0


class Sched:
    def __init__(self, nc):
        self.nc = nc
        self.w = {}
        self.r = {}
        self.engs = []

    def add_engine(self, e):
        self.engs.append(e)

    def _wait(self, q, t):
        if t[0] == 'c':
            _, e, n = t
            if e is q and q.is_pe:
                return
            key = e.name
            if q.seen.get(key, 0) >= n:
                return
            q.h.wait_ge(e.sem, n)
            q.seen[key] = n
        else:
            _, sem, val, key = t
            if q.seen.get(key, 0) >= val:
                return
            q.h.wait_ge(sem, val)
            q.seen[key] = val

    def _deps(self, q, rd, wr):
        for b in rd:
            t = self.w.get(b)
            if t is not None:
                self._wait(q, t)
        for b in wr:
            t = self.w.get(b)
            if t is not None:
                self._wait(q, t)
            for t2 in self.r.get(b, {}).values():
                self._wait(q, t2)

    def _record(self, tk, rk, rd, wr):
        for b in rd:
            self.r.setdefault(b, {})[rk] = tk
        for b in wr:
            self.w[b] = tk
            self.r[b] = {}

    def op(self, q, fn, rd=(), wr=()):
        self._deps(q, rd, wr)
        ins = fn()
        ins.then_inc(q.sem, 1)
        q.n += 1
        tk = ('c', q, q.n)
        self._record(tk, q.name, rd, wr)
        return tk

    def ops(self, q, fns, rd=(), wr=()):
        self._deps(q, rd, wr)
        ins = None
        for fn in fns:
            ins = fn()
        ins.then_inc(q.sem, 1)
        q.n += 1
        tk = ('c', q, q.n)
        self._record(tk, q.name, rd, wr)
        return tk

    def dma(self, q, out, in_, rd=(), wr=()):
        self._deps(q, rd, wr)
        i = q.dnext % len(q.dsems)
        q.dnext += 1
        sem = q.dsems[i]
        key = q.name + "_d%d" % i
        if q.dvals[i] > 0 and q.seen.get(key, 0) < q.dvals[i]:
            q.h.wait_ge(sem, q.dvals[i])
            q.seen[key] = q.dvals[i]
        q.h.dma_start(out=out, in_=in_).then_inc(sem, 16)
        q.dvals[i] += 16
        tk = ('d', sem, q.dvals[i], key)
        self._record(tk, key, rd, wr)
        return tk

    def barrier(self, keep=()):
        kept = {k: self.w[k] for k in keep if k in self.w}
        skip = {}
        for t in kept.values():
            if t[0] == 'd':
                skip[t[3]] = t[2]
        for q in self.engs:
            for e in self.engs:
                if e is q or e.n == 0:
                    continue
                if q.seen.get(e.name, 0) < e.n:
                    q.h.wait_ge(e.sem, e.n)
                    q.seen[e.name] = e.n
            for e in self.engs:
                for i, sem in enumerate(e.dsems):
                    key = e.name + "_d%d" % i
                    val = e.dvals[i]
                    if key in skip and skip[key] == val:
                        val -= 16
                    if val > 0 and q.seen.get(key, 0) < val:
                        q.h.wait_ge(sem, val)
                        q.seen[key] = val
        self.w = dict(kept)
        self.r = {}


def build_program(debug=False):
    nc = bass.Bass("TRN2", target_bir_lowering=False)

    def din(name, shape, dt=F32):
        return nc.dram_tensor(name, list(shape), dt, kind="ExternalInput").ap()

    xs = din("xs", [S, D])
    w_in = din("w_in", [D, INW])
    w_out = din("w_out", [D, D])
    w_up = din("w_up", [D, 2 * DFF])
    w_down = din("w_down", [DFF, D])
    pool_w = din("pool_w", [4, 256, 256])
    g1bc_d = din("g1bc", [128, D])
    g2bc_d = din("g2bc", [128, D])
    gqk_d = din("gqk", [128, 2])
    pscale_d = din("pscale", [128, 8])
    cw_d = din("cw", [128, 3, 88])
    cb_d = din("cb", [128, 88])
    bias31_d = din("bias31", [128, 8])
    biasT_d = din("biasT", [NH, 128, NBT, 512])
    band_d = din("band", [128, 3, 4, 128])
    gmask_d = din("gmask", [128, NQT, 16])
    notown_d = din("notown", [128, NQT, 16])
    flag_d = din("flag", [128, 1])
    ident_d = din("ident", [128, 128])
    esel_d = din("esel", [128, 16, 128])

    out_d = nc.dram_tensor("out", [2048, D], F32, kind="ExternalOutput").ap()
    skind = "ExternalOutput" if debug else "Internal"
    kT_s = nc.dram_tensor("kT_s", [NH, 128, S], BF16, kind=skind).ap()
    qT_s = nc.dram_tensor("qT_s", [NH, 128, S], BF16, kind=skind).ap()
    v_s = nc.dram_tensor("v_s", [32, 128, 1024], BF16, kind=skind).ap()
    x1_s = nc.dram_tensor("x1_s", [NQT, 128, D], F32, kind=skind).ap()
    mT_s = nc.dram_tensor("mT_s", [8, 128, QW], BF16, kind=skind).ap()
    hn2T_s = nc.dram_tensor("hn2T_s", [16, 128, QW], BF16, kind=skind).ap()
    if debug:
        mix_dbg = nc.dram_tensor("mix_dbg", [128, 16, QW], BF16, kind="ExternalOutput").ap()

    from contextlib import ExitStack
    top = ExitStack()
    with top:
        uid = [0]

        def sb(name, shape, dt, stack=top):
            uid[0] += 1
            return stack.enter_context(nc.sbuf_tensor("sb%d_%s" % (uid[0], name), list(shape), dt))

        def sem(name):
            return top.enter_context(nc.semaphore(name))

        sch = Sched(nc)
        PE = Eng("pe", nc.tensor, sem("s_pe"), is_pe=True)
        ACT = Eng("act", nc.scalar, sem("s_act"))
        DVE = Eng("dve", nc.vector, sem("s_dve"))
        POOL = Eng("pool", nc.gpsimd, sem("s_pool"))
        SP = Eng("sp", nc.sync, sem("s_sp"))
        for e in (PE, ACT, DVE, POOL, SP):
            sch.add_engine(e)
        for e, nd in ((SP, 16), (POOL, 12)):
            for i in range(nd):
                e.dsems.append(sem("d_%s%d" % (e.name, i)))
                e.dvals.append(0)

        ps = [top.enter_context(nc.psum_tensor("ps%d" % i, [128, 512], F32)) for i in range(8)]
        psb = [p.bitcast(BF16) for p in ps]

        ident_b = sb("ident_b", [128, 128], BF16)
        ident_f = sb("ident_f", [128, 128], F32)
        ones_b = sb("ones_b", [128, 128], BF16)
        gqk = sb("gqk", [128, 2], F32)
        gqs = sb("gqs", [128, 1], F32)
        pscale = sb("pscale", [128, 8], F32)
        bias31 = sb("bias31", [128, 8], F32)
        flag = sb("flag", [128, 1], F32)
        epsc = sb("epsc", [128, 1], F32)
        zcol = sb("zcol", [128, 1], F32)
        esel = sb("esel", [128, 16, 128], BF16)
        gmask = sb("gmask", [128, NQT, 16], F32)
        notown = sb("notown", [128, NQT, 16], F32)

        sch.dma(POOL, ident_b[:], ident_d, wr=["ident_b"])
        sch.dma(SP, ident_f[:], ident_d, wr=["ident_f"])
        sch.dma(SP, gqk[:], gqk_d, wr=["gqk"])
        sch.dma(SP, pscale[:], pscale_d, wr=["pscale"])
        sch.dma(SP, bias31[:], bias31_d, wr=["bias31"])
        sch.dma(SP, flag[:], flag_d, wr=["flag"])
        sch.dma(POOL, esel[:], esel_d, wr=["esel"])
        sch.dma(SP, gmask[:], gmask_d, wr=["gmask"])
        sch.dma(SP, notown[:], notown_d, wr=["notown"])
        sch.op(DVE, lambda: nc.vector.memset(ones_b[:], 1.0), wr=["ones_b"])
        sch.op(DVE, lambda: nc.vector.memset(epsc[:], EPS), wr=["epsc"])
        sch.op(DVE, lambda: nc.vector.memset(zcol[:], 0.0), wr=["zcol"])
        sch.op(DVE, lambda: nc.vector.tensor_scalar(out=gqs[:], in0=gqk[:, 0:1], scalar1=float(HD ** -0.5),
                                                    scalar2=None, op0=ALU.mult),
               rd=["gqk"], wr=["gqs"])

        pm = ExitStack()

        def rmsnorm_tile(xt, xkey, gbc, hn, hnkey, ss, rt1, rstd, idx):
            k = "n%d" % (idx % 2)
            sch.op(ACT, lambda: nc.scalar.activation(out=hn, in_=xt, func=AF.Square, accum_out=ss[:]),
                   rd=[xkey], wr=[hnkey, "ss" + k])
            sch.op(ACT, lambda: nc.scalar.activation(out=rt1[:], in_=ss[:], func=AF.Ln, bias=epsc[:],
                                                     scale=1.0 / D),
                   rd=["ss" + k, "epsc"], wr=["rt1" + k])
            sch.op(ACT, lambda: nc.scalar.activation(out=rstd[:], in_=rt1[:], func=AF.Exp, bias=zcol[:],
                                                     scale=-0.5),
                   rd=["rt1" + k, "zcol"], wr=["rstd" + k])
            sch.op(DVE, lambda: nc.vector.scalar_tensor_tensor(out=hn, in0=xt, scalar=rstd[:], in1=gbc,
                                                               op0=ALU.mult, op1=ALU.mult),
                   rd=[xkey, "rstd" + k, "gbc"], wr=[hnkey])

        with ExitStack() as p1:
            def s1(name, shape, dt):
                return sb(name, shape, dt, p1)
            g1bc = s1("g1bc", [128, D], F32)
            sch.dma(SP, g1bc[:], g1bc_d, wr=["gbc"])
            band_b = s1("band_b", [128, 3, 4, 128], BF16)
            poolw_b = s1("poolw_b", [128, 4, 2, 256], BF16)
            xbuf = [s1("xbuf%d" % i, [128, D], F32) for i in range(4)]
            hnb = [s1("hnb%d" % i, [128, D], BF16) for i in range(2)]
            ssb = [s1("ss%d" % i, [128, 1], F32) for i in range(2)]
            rt1b = [s1("rt1%d" % i, [128, 1], F32) for i in range(2)]
            rstdb = [s1("rstd%d" % i, [128, 1], F32) for i in range(2)]
            hnTb = [s1("hnT%d" % i, [128, 16, 1024], BF16) for i in range(2)]
            mst = [s1("mst%d" % i, [128, 8, 128], BF16) for i in range(2)]
            wsl = [s1("wsl%d" % i, [128, 16, 512], BF16) for i in range(2)]
            pall = s1("pall", [128, 9, 1024], BF16)
            sqb = [s1("sqb%d" % i, [128, 512], BF16) for i in range(2)]
            rtb = [s1("rtb%d" % i, [128, 512], F32) for i in range(2)]
            rrb = [s1("rrb%d" % i, [128, 512], F32) for i in range(2)]
            stg = [s1("stg%d" % i, [128, 512], BF16) for i in range(3)]
            vst = [s1("vst%d" % i, [128, 512], BF16) for i in range(3)]
            mixT = [s1("mixT%d" % i, [128, 8, 128], BF16) for i in range(2)]

            sch.dma(POOL, band_b[:], band_d, wr=["band"])
            sch.dma(POOL, poolw_b[:], pool_w.rearrange("g (kh p) c -> p g kh c", p=128), wr=["poolw"])
            sch.op(DVE, lambda: nc.vector.memset(pall[:, 0, :], 0.0), wr=["pall0"])

            w_in_v = w_in.rearrange("(dc p) c -> p dc c", p=128)
            cnt = {"qk": 0, "v": 0, "slab": 0, "x": 0}

            def load_x(gt):
                if gt >= 32:
                    return
                i = gt % 4
                sch.dma(SP if gt < 8 else POOL, xbuf[i][:], xs[gt * 128:(gt + 1) * 128, :], wr=["xbuf%d" % i])

            def load_slab(s):
                i = cnt["slab"] % 2
                cnt["slab"] += 1
                for a in range(4):
                    sch.dma(POOL, wsl[i][:, 4 * a:4 * a + 4, :], w_in_v[:, 4 * a:4 * a + 4, s * 512:(s + 1) * 512],
                            wr=["wsl%d_%d" % (i, a)])
                return i

            def slab_keys(i):
                return ["wsl%d_%d" % (i, a) for a in range(4)]

            post_q = []

            def flush_post():
                while post_q:
                    norm_post(*post_q.pop(0))

            def norm_pre(G, ti):
                gt = 8 * G + ti
                hi = gt % 2
                if any((8 * g + t) % 2 == hi for (g, t) in post_q):
                    flush_post()
                load_x(gt + 3)
                xi = gt % 4
                rmsnorm_tile(xbuf[xi][:], "xbuf%d" % xi, g1bc[:], hnb[hi][:], "hnb%d" % hi,
                             ssb[hi], rt1b[hi], rstdb[hi], hi)

            def norm_post(G, ti):
                gt = 8 * G + ti
                hi = gt % 2
                hb = G % 2
                for a in range(4):
                    bank = 5 + (a % 2)
                    fns = []
                    for b in range(4):
                        dc = 4 * a + b
                        fns.append(lambda dc=dc, b=b, bank=bank, hi=hi: nc.tensor.transpose(
                            psb[bank][:, b * 128:(b + 1) * 128], hnb[hi][:, dc * 128:(dc + 1) * 128], ident_b[:]))
                    sch.ops(PE, fns, rd=["hnb%d" % hi, "ident_b"], wr=["ps%d" % bank])
                    if a % 2 == 0:
                        sch.op(ACT, lambda a=a, bank=bank, ti=ti, hb=hb: nc.scalar.activation(
                            out=hnTb[hb][:, 4 * a:4 * a + 4, ti * 128:(ti + 1) * 128],
                            in_=psb[bank][:, 0:512].rearrange("p (a b) -> p a b", a=4), func=AF.Copy),
                            rd=["ps%d" % bank], wr=["hnT%d_%d" % (hb, ti // 4)])
                    else:
                        sch.op(DVE, lambda a=a, bank=bank, ti=ti, hb=hb: nc.vector.tensor_copy(
                            out=hnTb[hb][:, 4 * a:4 * a + 4, ti * 128:(ti + 1) * 128],
                            in_=psb[bank][:, 0:512].rearrange("p (a b) -> p a b", a=4)),
                            rd=["ps%d" % bank], wr=["hnT%d_%d" % (hb, ti // 4)])

            for gt in range(3):
                load_x(gt)
            pre_slab = load_slab(2)
            norm_pre(0, 0)
            for ti in range(8):
                if ti + 1 < 8:
                    norm_pre(0, ti + 1)
                norm_post(0, ti)
            nmt = 0
            for G in range(4):
                slabs = [2, 3, 4, 5] if G == 0 else list(range(8))
                hnT = hnTb[G % 2]
                hk = "hnT%d" % (G % 2)
                tps = 8 // len(slabs)
                for si, s in enumerate(slabs):
                    wi = pre_slab
                    if G + 1 < 4:
                        for ti in range(si * tps, (si + 1) * tps):
                            norm_pre(G + 1, ti)
                    if si + 1 < len(slabs):
                        pre_slab = load_slab(slabs[si + 1])
                    elif G + 1 < 4:
                        pre_slab = load_slab(0)
                    W = wsl[wi]
                    wk = slab_keys(wi)
                    if s < 4:
                        isq = s < 2
                        dst = qT_s if isq else kT_s
                        gcol = gqs[:] if isq else gqk[:, 1:2]
                        its = [(hh, c) for c in range(2) for hh in range(4)
                               if not (isq and G == 1 and c == 0)]

                        halo_q = isq and G == 1
                        nw = 128 if halo_q else 512
                        cofs = 384 if halo_q else 0

                        def qk_a(n, its=its, W=W, wk=wk, nw=nw, cofs=cofs):
                            hh, c = its[n]
                            if c == 1:
                                flush_post()
                            k = cnt["qk"] + n
                            pa, i2 = k % 3, k % 2
                            t0 = c * 512 + cofs
                            fns = [lambda dc=dc: nc.tensor.matmul(
                                ps[pa][:, 0:nw], W[:, dc, hh * 128:(hh + 1) * 128], hnT[:, dc, t0:t0 + nw],
                                start=(dc == 0), stop=(dc == 15)) for dc in range(16)]
                            sch.ops(PE, fns, rd=wk + [hk + "_%d" % c], wr=["ps%d" % pa])
                            sch.op(ACT, lambda: nc.scalar.activation(out=sqb[i2][:, 0:nw], in_=ps[pa][:, 0:nw],
                                                                     func=AF.Square),
                                   rd=["ps%d" % pa], wr=["sqb%d" % i2])

                        def qk_b(n, its=its, s=s, isq=isq, dst=dst, gcol=gcol, nw=nw, cofs=cofs):
                            hh, c = its[n]
                            head = (s % 2) * 4 + hh
                            k = cnt["qk"] + n
                            pa, i2, i3 = k % 3, k % 2, k % 3
                            pb = 3 + i2
                            sch.op(PE, lambda: nc.tensor.matmul(ps[pb][:, 0:nw], ones_b[:], sqb[i2][:, 0:nw],
                                                                start=True, stop=True),
                                   rd=["sqb%d" % i2, "ones_b"], wr=["ps%d" % pb])
                            sch.op(ACT, lambda: nc.scalar.activation(
                                out=rtb[i2][:, 0:nw], in_=ps[pb][:, 0:nw], func=AF.Ln, bias=epsc[:], scale=1.0 / HD),
                                rd=["ps%d" % pb, "epsc"], wr=["rtb%d" % i2])
                            sch.op(ACT, lambda: nc.scalar.activation(
                                out=rrb[i2][:, 0:nw], in_=rtb[i2][:, 0:nw], func=AF.Exp, bias=zcol[:], scale=-0.5),
                                rd=["rtb%d" % i2, "zcol"], wr=["rrb%d" % i2])
                            sch.op(DVE, lambda: nc.vector.scalar_tensor_tensor(
                                out=stg[i3][:, 0:nw], in0=ps[pa][:, 0:nw], scalar=gcol, in1=rrb[i2][:, 0:nw],
                                op0=ALU.mult, op1=ALU.mult),
                                rd=["ps%d" % pa, "rrb%d" % i2, "gqs", "gqk"], wr=["stg%d" % i3])
                            t0 = G * 1024 + c * 512 + cofs
                            sch.dma(SP, dst[head, :, t0:t0 + nw], stg[i3][:, 0:nw], rd=["stg%d" % i3],
                                    wr=[("qs" if isq else "ks", head, G, c)])

                        qk_a(0)
                        for n in range(len(its)):
                            if n + 1 < len(its):
                                qk_a(n + 1)
                            if n == 1:
                                flush_post()
                            qk_b(n)
                        flush_post()
                        cnt["qk"] += len(its)
                    else:
                        isv = s < 6
                        for ti in range(8):
                            gt = 8 * G + ti
                            if (not isv) and G == 1 and ti < 6:
                                continue
                            i2 = cnt["v"] % 2
                            i3 = cnt["v"] % 3
                            cnt["v"] += 1
                            pv = i3
                            if ti >= 4:
                                flush_post()
                            fns = [lambda dc=dc, ti=ti, pv=pv: nc.tensor.matmul(
                                ps[pv][:], hnT[:, dc, ti * 128:(ti + 1) * 128], W[:, dc, :],
                                start=(dc == 0), stop=(dc == 15)) for dc in range(16)]
                            sch.ops(PE, fns, rd=wk + [hk + "_%d" % (ti // 4)], wr=["ps%d" % pv])
                            if ti == 1:
                                flush_post()
                            if isv:
                                sch.op(DVE, lambda pv=pv, i3=i3: nc.vector.tensor_copy(out=vst[i3][:], in_=ps[pv][:]),
                                       rd=["ps%d" % pv], wr=["vst%d" % i3])
                                c0 = (s - 4) * 512
                                sch.dma(SP, v_s[gt, :, c0:c0 + 512], vst[i3][:], rd=["vst%d" % i3],
                                        wr=[("vs", gt, s)])
                            else:
                                c0 = (s - 6) * 512
                                sch.op(DVE, lambda pv=pv, ti=ti, c0=c0: nc.vector.tensor_copy(
                                    out=pall[:, 1 + ti, c0:c0 + 512], in_=ps[pv][:]),
                                    rd=["ps%d" % pv], wr=["pall%d" % (1 + ti)])
                    if G + 1 < 4:
                        for ti in range(si * tps, (si + 1) * tps):
                            post_q.append((G + 1, ti))
                ptiles = [ti for ti in range(8) if 8 * G + ti >= 15]

                def pool_x(ti, G=G):
                    gt = 8 * G + ti
                    kind = 2 if gt == 16 else 1
                    mx = gt % 2
                    for a in range(2):
                        bank = 0 + a
                        fns = []
                        for b in range(4):
                            c8 = 4 * a + b
                            g = c8 // 2
                            fns.append(lambda b=b, c8=c8, g=g: nc.tensor.matmul(
                                ps[bank][:, b * 128:(b + 1) * 128], pall[:, ti, c8 * 128:(c8 + 1) * 128],
                                band_b[:, 0, g, :], start=True, stop=False))
                            fns.append(lambda b=b, c8=c8, g=g: nc.tensor.matmul(
                                ps[bank][:, b * 128:(b + 1) * 128], pall[:, 1 + ti, c8 * 128:(c8 + 1) * 128],
                                band_b[:, kind, g, :], start=False, stop=True))
                        sch.ops(PE, fns, rd=["pall%d" % ti, "pall%d" % (1 + ti), "band"], wr=["ps%d" % bank])
                        sch.op(DVE, lambda a=a: nc.vector.tensor_copy(
                            out=mixT[mx][:, 4 * a:4 * a + 4, :], in_=ps[bank][:].rearrange("p (a b) -> p a b", a=4)),
                            rd=["ps%d" % bank], wr=["mixT%d_%d" % (mx, a)])

                def pool_y(ti, G=G):
                    gt = 8 * G + ti
                    qc = (gt - 15) * 128
                    mx = gt % 2
                    mi = gt % 2
                    for a in range(2):
                        bank = 2 + a
                        fns = []
                        for b in range(4):
                            c8o = 4 * a + b
                            g = c8o // 2
                            half = c8o % 2
                            for kh in range(2):
                                fns.append(lambda b=b, g=g, half=half, kh=kh: nc.tensor.matmul(
                                    ps[bank][:, b * 128:(b + 1) * 128],
                                    poolw_b[:, g, kh, half * 128:(half + 1) * 128], mixT[mx][:, 2 * g + kh, :],
                                    start=(kh == 0), stop=(kh == 1)))
                        sch.ops(PE, fns, rd=["mixT%d_0" % mx, "mixT%d_1" % mx, "poolw"], wr=["ps%d" % bank])
                        for b in range(4):
                            c8o = 4 * a + b
                            if a == 0:
                                sch.op(ACT, lambda b=b, c8o=c8o: nc.scalar.activation(
                                    out=mst[mi][:, c8o, :], in_=ps[bank][:, b * 128:(b + 1) * 128],
                                    func=AF.Identity, bias=zcol[:], scale=pscale[:, c8o:c8o + 1]),
                                    rd=["ps%d" % bank, "pscale", "zcol"], wr=["mst%d_%d" % (mi, a)])
                            else:
                                sch.op(DVE, lambda b=b, c8o=c8o: nc.vector.tensor_scalar(
                                    out=mst[mi][:, c8o, :], in0=ps[bank][:, b * 128:(b + 1) * 128],
                                    scalar1=pscale[:, c8o:c8o + 1], scalar2=None, op0=ALU.mult),
                                    rd=["ps%d" % bank, "pscale"], wr=["mst%d_%d" % (mi, a)])
                    sch.dma(SP, mT_s[:, :, qc:qc + 128].rearrange("c p t -> p c t"), mst[mi][:],
                            rd=["mst%d_0" % mi, "mst%d_1" % mi], wr=[("mTs", gt)])

                flush_post()
                if ptiles:
                    pool_x(ptiles[0])
                    for k, ti in enumerate(ptiles):
                        if k + 1 < len(ptiles):
                            pool_x(ptiles[k + 1])
                        pool_y(ti)
                if G <= 1:
                    flush_post()
                if G >= 1:
                    sch.op(DVE, lambda: nc.vector.tensor_copy(out=pall[:, 0, :], in_=pall[:, 8, :]),
                           rd=["pall8"], wr=["pall0"])
            sch.barrier()

        aT = sb("aT", [128, 8, QW], BF16, pm)
        wo_sb = sb("wo_sb", [128, 16, D], BF16, pm)
        w_out_v = w_out.rearrange("(cc p) d -> p cc d", p=128)
        with ExitStack() as p2:
            def s2(name, shape, dt):
                return sb(name, shape, dt, p2)
            kTh = [s2("kTh%d" % i, [128, S], BF16) for i in range(2)]
            vh = [s2("vh%d" % i, [128, 32, 128], BF16) for i in range(2)]
            qTh = [s2("qTh%d" % i, [128, QW], BF16) for i in range(2)]
            bTh = [s2("bTh%d" % i, [128, NBT, 512], BF16) for i in range(2)]
            MT = [s2("MT%d" % i, [128, QW], BF16) for i in range(2)]
            PT = [s2("PT%d" % i, [128, 512], BF16) for i in range(4)]
            rec = [s2("rec%d" % i, [128, 512], F32) for i in range(2)]
            lnd = [s2("lnd%d" % i, [128, 512], F32) for i in range(2)]
            accP = [s2("accP%d" % i, [128, 512], F32) for i in range(2)]
            accPb = [s2("accPb%d" % i, [128, 512], BF16) for i in range(2)]
            kmf = s2("kmf", [128, 16], F32)
            kmb = [s2("kmb%d" % i, [128, 16], BF16) for i in range(2)]
            gmA = [s2("gmA%d" % i, [128, NQT, 16], F32) for i in range(2)]
            top8 = [s2("top8%d" % i, [128, NQT, 8], F32) for i in range(2)]
            thrA = [s2("thrA%d" % i, [128, NQT], F32) for i in range(2)]
            nselA = [s2("nselA%d" % i, [128, NQT, 16], F32) for i in range(2)]
            maddA = [s2("maddA%d" % i, [128, NQT, 16], F32) for i in range(2)]

            def load_head(h):
                i = h % 2
                sch.dma(SP, kTh[i][:], kT_s[h], wr=["kTh%d" % i])
                sch.dma(SP, qTh[i][:], qT_s[h, :, 1920:S], wr=["qTh%d" % i])
                for a in range(4):
                    sch.dma(POOL, vh[i][:, 8 * a:8 * a + 8, :],
                            v_s[8 * a:8 * a + 8, :, h * 128:(h + 1) * 128].rearrange("t p c -> p t c"),
                            wr=["vh%d_%d" % (i, a)])
                for a in range(NBT):
                    sch.dma(POOL, bTh[i][:, a, :], biasT_d[h, :, a, :], wr=["bTh%d_%d" % (i, a)])

            def gate_stage1a(h, part):
                i = h % 2
                sch.op(DVE, lambda: nc.vector.tensor_reduce(
                    out=kmf[:, 4 * part:4 * part + 4],
                    in_=kTh[i][:, 1024 * part:1024 * (part + 1)].rearrange("p (n k) -> p n k", n=4),
                    axis=AX.X, op=ALU.add), rd=["kTh%d" % i], wr=["kmf%d" % part])
                if part == 3:
                    sch.op(DVE, lambda: nc.vector.tensor_scalar(out=kmb[i][:], in0=kmf[:], scalar1=1.0 / 256.0,
                                                                scalar2=None, op0=ALU.mult),
                           rd=["kmf%d" % p for p in range(4)], wr=["kmb%d" % i])

            def gate_stage1b(h, piece=None):
                i = h % 2
                pcs = range(5) if piece is None else [piece]
                for pc in pcs:
                    if pc == 0:
                        fns = [lambda qi=qi: nc.tensor.matmul(
                            ps[7][:, qi * 16:(qi + 1) * 16], qTh[i][:, qi * 128:(qi + 1) * 128], kmb[i][:],
                            start=True, stop=True) for qi in range(NQT)]
                        sch.ops(PE, fns, rd=["qTh%d" % i, "kmb%d" % i], wr=["ps7"])
                        sch.op(DVE, lambda: nc.vector.tensor_tensor(
                            out=gmA[i][:], in0=ps[7][:, 0:NQT * 16].rearrange("p (a b) -> p a b", a=NQT),
                            in1=gmask[:], op=ALU.add), rd=["ps7", "gmask"], wr=["gmA%d" % i])
                    elif pc in (1, 2, 3):
                        qs = [range(0, 6), range(6, 12), range(12, NQT)][pc - 1]
                        fns = [lambda qi=qi: nc.vector.max(out=top8[i][:, qi, :], in_=gmA[i][:, qi, :]) for qi in qs]
                        sch.ops(DVE, fns, rd=["gmA%d" % i], wr=["top8%d_%d" % (i, pc)])
                    else:
                        sch.op(DVE, lambda: nc.vector.tensor_scalar(
                            out=thrA[i][:], in0=top8[i][:, :, 2], scalar1=-1e29, scalar2=None, op0=ALU.max),
                            rd=["top8%d_%d" % (i, p) for p in (1, 2, 3)], wr=["thrA%d" % i])
                        sch.op(DVE, lambda: nc.vector.tensor_tensor(
                            out=nselA[i][:], in0=gmA[i][:],
                            in1=thrA[i][:].unsqueeze(2).to_broadcast([128, NQT, 16]), op=ALU.is_lt),
                            rd=["gmA%d" % i, "thrA%d" % i], wr=["nselA%d" % i])
                        sch.op(DVE, lambda: nc.vector.scalar_tensor_tensor(
                            out=maddA[i][:], in0=nselA[i][:], scalar=-BIG, in1=notown[:], op0=ALU.mult, op1=ALU.mult),
                            rd=["nselA%d" % i, "notown"], wr=["maddA%d" % i])

            def gate_stage2_pe(h, rnd):
                i = h % 2
                qis = list(range(4 * rnd, min(4 * rnd + 4, NQT)))
                fns = [lambda k=k, qi=qi: nc.tensor.transpose(
                    ps[5][0:16, k * 128:(k + 1) * 128], maddA[i][:, qi, :], ident_f[:])
                    for k, qi in enumerate(qis)]
                sch.ops(PE, fns, rd=["maddA%d" % i, "ident_f"], wr=["ps5"])

            def gate_stage2_act(h, rnd):
                i = h % 2
                qis = list(range(4 * rnd, min(4 * rnd + 4, NQT)))
                n = len(qis)
                q0 = qis[0]
                sch.op(ACT, lambda: nc.scalar.activation(
                    out=MT[i][0:16, q0 * 128:(q0 + n) * 128], in_=ps[5][0:16, 0:n * 128], func=AF.Copy),
                    rd=["ps5"], wr=["MT%d" % i])

            accPh = s2("accPh", [128, 512], F32)
            accPbh = s2("accPbh", [128, 512], BF16)
            for i in range(2):
                sch.op(DVE, lambda i=i: nc.vector.memset(MT[i][:], 0.0), wr=["MT%d" % i])
            load_head(0)
            for part in range(4):
                gate_stage1a(0, part)
            gate_stage1b(0)
            for r in range(5):
                gate_stage2_pe(0, r)
                gate_stage2_act(0, r)

            SB = [0, 1, 2, 6]
            PTK = {0: 0, 1: 1, 2: 2, 6: 3}
            U = []
            nreg = 0
            for h in range(NH):
                hu = []
                halo = dict(h=h, halo=True, q0=0, nq=128, Q0=1920, nkt=16, pOap=ps[7][:, 384:512], pOk="ps7h",
                            acc=accPh, accb=accPbh, acck="accPh", accbk="accPbh", ab=0)
                hunits = [dict(ctx=halo, kts=list(range(4 * g, 4 * g + 4))) for g in range(4)]
                for c in range(4):
                    ab = nreg % 2
                    nreg += 1
                    ctx = dict(h=h, halo=False, q0=128 + 512 * c, nq=512, Q0=2048 + 512 * c,
                               nkt=(2048 + 512 * c + 512) // 128, pOap=ps[3 + ab][:, 0:512], pOk="ps%d" % (3 + ab),
                               acc=accP[ab], accb=accPb[ab], acck="accP%d" % ab, accbk="accPb%d" % ab, ab=ab)
                    for kt in range(ctx["nkt"]):
                        hu.append(dict(ctx=ctx, kts=[kt]))
                        if c == 0 and kt % 4 == 3 and kt // 4 < 4:
                            hu.append(hunits[kt // 4])
                for k, u in enumerate(hu):
                    u["hidx"] = k
                U += hu
            NU = len(U)
            cnt2 = {"s": 0}
            later = {}

            def at(n, fn):
                later.setdefault(min(n, NU - 1), []).append(fn)

            def scores(n):
                u = U[n]
                c = u["ctx"]
                i = c["h"] % 2
                nq, q0, Q0 = c["nq"], c["q0"], c["Q0"]
                sbk = SB[cnt2["s"] % 4]
                cnt2["s"] += 1
                u["sbk"] = sbk
                fns = []
                bdeps = []
                for idx, kt in enumerate(u["kts"]):
                    D0 = Q0 - 128 * kt
                    near = D0 <= 896
                    assert u.setdefault("near", near) == near
                    if near:
                        bdeps.append("bTh%d_%d" % (i, (D0 + 384) // 128))
                    cs = slice(idx * nq, (idx + 1) * nq)
                    opnds = [(kTh[i][:, kt * 128:(kt + 1) * 128], qTh[i][:, q0:q0 + nq])]
                    if near:
                        j = (D0 + 384) // 128
                        opnds.append((ident_b[:], bTh[i][:, j, 0:nq]))
                    if kt < c["nkt"] - 2:
                        opnds.append((esel[:, kt // 2, :], MT[i][:, q0:q0 + nq]))
                    for oi, (lh, rh) in enumerate(opnds):
                        fns.append(lambda lh=lh, rh=rh, cs=cs, oi=oi, no=len(opnds): nc.tensor.matmul(
                            ps[sbk][:, cs], lh, rh, start=(oi == 0), stop=(oi == no - 1)))
                sch.ops(PE, fns, rd=["kTh%d" % i, "qTh%d" % i, "MT%d" % i, "esel", "ident_b"] + bdeps,
                        wr=["ps%d" % sbk])

            def finalizeA0(c):
                sch.op(DVE, lambda: nc.vector.tensor_copy(out=c["accb"][:, 0:512], in_=c["acc"][:, 0:512]),
                       rd=[c["acck"]], wr=[c["accbk"]])

            def finalizeA(c):
                nq = c["nq"]
                grp = 512 // nq
                fns = [lambda g=g: nc.tensor.matmul(ps[5][:, 0:nq], ones_b[:], c["accb"][:, g * nq:(g + 1) * nq],
                                                    start=(g == 0), stop=(g == grp - 1)) for g in range(grp)]
                sch.ops(PE, fns, rd=[c["accbk"], "ones_b"], wr=["ps5"])

            def finalizeB(c):
                nq, ab, h, q0 = c["nq"], c["ab"], c["h"], c["q0"]
                sch.op(ACT, lambda: nc.scalar.activation(out=lnd[ab][:, 0:nq], in_=ps[5][:, 0:nq],
                                                         func=AF.Ln, bias=zcol[:], scale=1.0),
                       rd=["ps5", "zcol"], wr=["lnd%d" % ab])

            def finalizeC(c):
                nq, ab, h, q0 = c["nq"], c["ab"], c["h"], c["q0"]
                sch.op(ACT, lambda: nc.scalar.activation(out=rec[ab][:, 0:nq], in_=lnd[ab][:, 0:nq],
                                                         func=AF.Exp, bias=zcol[:], scale=-1.0),
                       rd=["lnd%d" % ab, "zcol"], wr=["rec%d" % ab])
                sch.op(DVE, lambda: nc.vector.tensor_tensor(
                    out=aT[:, h, q0:q0 + nq], in0=c["pOap"][:, 0:nq], in1=rec[ab][:, 0:nq], op=ALU.mult),
                    rd=[c["pOk"], "rec%d" % ab], wr=["aT"])

            scores(0)
            scores(1)
            for n in range(NU):
                u = U[n]
                c = u["ctx"]
                h = c["h"]
                i = h % 2
                nq = c["nq"]
                k = u["hidx"]
                W = nq * len(u["kts"])
                if k == 0:
                    if h + 1 < NH:
                        load_head(h + 1)
                    if h == 0:
                        for cc in range(16):
                            sch.dma(POOL, wo_sb[:, cc, :], w_out_v[:, cc, :], wr=["wo%d" % cc])
                if h + 1 < NH:
                    if k in (34, 36, 38, 40):
                        gate_stage1a(h + 1, (k - 34) // 2)
                    if 42 <= k <= 46:
                        gate_stage1b(h + 1, k - 42)
                    if k >= 60 and (k - 60) % 3 == 0 and (k - 60) // 3 < 5:
                        gate_stage2_pe(h + 1, (k - 60) // 3)
                if n + 2 < NU:
                    scores(n + 2)
                sbk = u["sbk"]
                pk = PTK[sbk]
                bcol = zcol[:] if u["near"] else bias31[:, h:h + 1]
                sch.op(ACT, lambda: nc.scalar.activation(
                    out=PT[pk][:, 0:W], in_=ps[sbk][:, 0:W], func=AF.Exp, bias=bcol, scale=1.0),
                    rd=["ps%d" % sbk, "bias31", "zcol"], wr=["PT%d" % pk])
                if h + 1 < NH and k >= 60 and (k - 60) % 3 == 0 and (k - 60) // 3 < 5:
                    gate_stage2_act(h + 1, (k - 60) // 3)
                fns = []
                for idx, kt in enumerate(u["kts"]):
                    fns.append(lambda idx=idx, kt=kt: nc.tensor.matmul(
                        c["pOap"][:, 0:nq], vh[i][:, kt, :], PT[pk][:, idx * nq:(idx + 1) * nq],
                        start=(kt == 0), stop=(kt == c["nkt"] - 1)))
                sch.ops(PE, fns, rd=["PT%d" % pk] + ["vh%d_%d" % (i, a) for a in range(4)], wr=[c["pOk"]])
                if u["kts"][0] == 0:
                    sch.op(DVE, lambda: nc.vector.tensor_copy(out=c["acc"][:, 0:W], in_=PT[pk][:, 0:W]),
                           rd=["PT%d" % pk], wr=[c["acck"]])
                else:
                    sch.op(DVE, lambda: nc.vector.tensor_tensor(
                        out=c["acc"][:, 0:W], in0=c["acc"][:, 0:W], in1=PT[pk][:, 0:W], op=ALU.add),
                        rd=["PT%d" % pk, c["acck"]], wr=[c["acck"]])
                if u["kts"][-1] == c["nkt"] - 1:
                    at(n + 3, lambda c=c: finalizeA0(c))
                    at(n + 6, lambda c=c: finalizeA(c))
                    at(n + 8, lambda c=c: finalizeB(c))
                    at(n + 10, lambda c=c: finalizeC(c))
                for fn in later.pop(n, []):
                    fn()
            sch.barrier()

        with ExitStack() as p3:
            def s3(name, shape, dt):
                return sb(name, shape, dt, p3)
            mT = s3("mT", [128, 8, QW], BF16)
            g2bc = s3("g2bc", [128, D], F32)
            xo = [s3("xo%d" % i, [128, D], F32) for i in range(2)]
            x1t = [s3("x1t%d" % i, [128, D], F32) for i in range(2)]
            hn2 = [s3("hn2_%d" % i, [128, D], BF16) for i in range(2)]
            ss2 = [s3("ss2_%d" % i, [128, 1], F32) for i in range(2)]
            rt2 = [s3("rt2_%d" % i, [128, 1], F32) for i in range(2)]
            rs2 = [s3("rs2_%d" % i, [128, 1], F32) for i in range(2)]
            hst = [s3("hst%d" % i, [128, 16, 128], BF16) for i in range(2)]
            for cc in range(8):
                sch.dma(SP if cc % 2 == 0 else POOL, mT[:, cc, :], mT_s[cc], wr=["mT%d" % cc])
            sch.dma(POOL, g2bc[:], g2bc_d, wr=["gbc"])
            if debug:
                sch.dma(SP, mix_dbg[:, 0:8, :], aT[:], rd=["aT"])
                sch.dma(SP, mix_dbg[:, 8:16, :], mT[:], rd=["mT%d" % cc for cc in range(8)])
            wokeys = ["wo%d" % cc for cc in range(16)]

            def load_xo(qi):
                gt = 15 + qi
                sch.dma(POOL, xo[qi % 2][:], xs[gt * 128:(gt + 1) * 128, :], wr=["xo%d" % (qi % 2)])

            def o_mm(qi):
                i = qi % 2
                if qi + 1 < NQT:
                    load_xo(qi + 1)
                for sl in range(4):
                    bank = (4 * qi + sl) % 6
                    fns = []
                    for cc in range(16):
                        src = aT if cc < 8 else mT
                        fns.append(lambda cc=cc, src=src: nc.tensor.matmul(
                            ps[bank][:], src[:, cc % 8, qi * 128:(qi + 1) * 128], wo_sb[:, cc, sl * 512:(sl + 1) * 512],
                            start=(cc == 0), stop=(cc == 15)))
                    sch.ops(PE, fns, rd=["aT"] + ["mT%d" % cc for cc in range(8)] + wokeys, wr=["ps%d" % bank])
                    sch.op(DVE, lambda: nc.vector.tensor_tensor(
                        out=x1t[i][:, sl * 512:(sl + 1) * 512], in0=ps[bank][:], in1=xo[i][:, sl * 512:(sl + 1) * 512],
                        op=ALU.add), rd=["ps%d" % bank, "xo%d" % i], wr=["x1t%d" % i])
                sch.dma(SP, x1_s[qi], x1t[i][:], rd=["x1t%d" % i], wr=[("x1s", qi)])

            def o_norm(qi):
                i = qi % 2
                rmsnorm_tile(x1t[i][:], "x1t%d" % i, g2bc[:], hn2[i][:], "hn2_%d" % i, ss2[i], rt2[i], rs2[i], i)

            def o_tr(qi):
                i = qi % 2
                for a in range(4):
                    bank = 6 + (a % 2)
                    fns = []
                    for b in range(4):
                        dc = 4 * a + b
                        fns.append(lambda dc=dc, b=b: nc.tensor.transpose(
                            psb[bank][:, b * 128:(b + 1) * 128], hn2[i][:, dc * 128:(dc + 1) * 128], ident_b[:]))
                    sch.ops(PE, fns, rd=["hn2_%d" % i, "ident_b"], wr=["ps%d" % bank])
                    if a % 2 == 0:
                        sch.op(ACT, lambda a=a: nc.scalar.activation(
                            out=hst[i][:, 4 * a:4 * a + 4, :],
                            in_=psb[bank][:, 0:512].rearrange("p (a b) -> p a b", a=4), func=AF.Copy),
                            rd=["ps%d" % bank], wr=["hst%d_%d" % (i, a)])
                    else:
                        sch.op(DVE, lambda a=a: nc.vector.tensor_copy(
                            out=hst[i][:, 4 * a:4 * a + 4, :],
                            in_=psb[bank][:, 0:512].rearrange("p (a b) -> p a b", a=4)),
                            rd=["ps%d" % bank], wr=["hst%d_%d" % (i, a)])
                sch.dma(SP, hn2T_s[:, :, qi * 128:(qi + 1) * 128].rearrange("dc p t -> p dc t"), hst[i][:],
                        rd=["hst%d_%d" % (i, a) for a in range(4)], wr=[("hn2Ts", qi)])

            load_xo(0)
            o_mm(0)
            o_norm(0)
            for qi in range(NQT):
                if qi + 1 < NQT:
                    o_mm(qi + 1)
                o_tr(qi)
                if qi + 1 < NQT:
                    o_norm(qi + 1)
            sch.barrier()

        pm.close()
        with ExitStack() as p4:
            def s4(name, shape, dt):
                return sb(name, shape, dt, p4)
            gT = s4("gT", [128, NJ, 1024], BF16)
            halo_st = s4("halo_st", [128, NJ, 2, 2], F32)
            cw = s4("cw", [128, 3, 88], F32)
            cb = s4("cb", [128, 88], F32)
            sch.dma(SP, cw[:], cw_d, wr=["cw"])
            sch.dma(SP, cb[:], cb_d, wr=["cb"])
            w_up_v = w_up.rearrange("(dc p) f -> p dc f", p=128)
            w_down_v = w_down.rearrange("(j p) d -> p j d", p=128)
            wub = [s4("wub%d" % i, [128, 2, 16, 128], BF16) for i in range(2)]
            wdb = [s4("wdb%d" % i, [128, NJ, 128], BF16) for i in range(2)]

            def load_wu(j):
                i = j % 2
                for gv in range(2):
                    c0 = gv * DFF + j * 128
                    sch.dma(POOL, wub[i][:, gv, :, :], w_up_v[:, :, c0:c0 + 128], wr=["wub%d_%d" % (i, gv)])

            def load_wd(m):
                i = m % 2
                for a in range(4):
                    sch.dma(POOL, wdb[i][:, 11 * a:11 * a + 11, :],
                            w_down_v[:, 11 * a:11 * a + 11, m * 128:(m + 1) * 128], wr=["wdb%d_%d" % (i, a)])
            load_wu(0)
            out_v = out_d.rearrange("(t p) c -> p t c", p=128)
            for Gf in range(2):
                with ExitStack() as pu:
                    def su(name, shape, dt):
                        return sb(name, shape, dt, pu)
                    hn2T = su("hn2T", [128, 16, 1024], BF16)
                    hn2Th = su("hn2Th", [128, 16, 2], BF16)
                    c0 = 128 + 1024 * Gf
                    for cch in range(2):
                        for a in range(4):
                            sch.dma(SP if a % 2 == 0 else POOL,
                                    hn2T[:, 4 * a:4 * a + 4, cch * 512:(cch + 1) * 512],
                                    hn2T_s[4 * a:4 * a + 4, :, c0 + cch * 512:c0 + (cch + 1) * 512].rearrange(
                                        "dc p t -> p dc t"),
                                    wr=["hn2T_%d_%d" % (a, cch)])
                    if Gf == 0:
                        sch.dma(POOL, hn2Th[:], hn2T_s[:, :, 126:128].rearrange("dc p t -> p dc t"), wr=["hn2Th"])
                    ugb = [su("ugb%d" % i, [128, 1026], F32) for i in range(2)]
                    uvb = [su("uvb%d" % i, [128, 1026], F32) for i in range(2)]
                    ygb = [su("ygb%d" % i, [128, 512], F32) for i in range(2)]
                    yvb = [su("yvb%d" % i, [128, 512], F32) for i in range(2)]
                    sgb = [su("sgb%d" % i, [128, 512], F32) for i in range(2)]

                    ne = 0
                    for j in range(NJ):
                        if j + 1 < NJ:
                            load_wu(j + 1)
                        else:
                            load_wd(0)
                        wi = j % 2
                        ug, uv = ugb[wi], uvb[wi]
                        ugk, uvk = "ug%d" % wi, "uv%d" % wi
                        wk = ["wub%d_0" % wi, "wub%d_1" % wi]
                        def emit_halo(ug=ug, uv=uv, ugk=ugk, uvk=uvk, wk=wk, wi=wi, j=j):
                          if Gf == 0:
                            fns = []
                            for gv in range(2):
                                for dc in range(16):
                                    fns.append(lambda gv=gv, dc=dc, wi=wi: nc.tensor.matmul(
                                        ps[7][:, gv * 2:gv * 2 + 2], wub[wi][:, gv, dc, :], hn2Th[:, dc, :],
                                        start=(dc == 0), stop=(dc == 15)))
                            sch.ops(PE, fns, rd=wk + ["hn2Th"], wr=["ps7"])
                            sch.op(DVE, lambda ug=ug: nc.vector.tensor_scalar(
                                out=ug[:, 0:2], in0=ps[7][:, 0:2], scalar1=flag[:], scalar2=None, op0=ALU.mult),
                                rd=["ps7", "flag"], wr=[ugk + "h"])
                            sch.op(DVE, lambda uv=uv: nc.vector.tensor_scalar(
                                out=uv[:, 0:2], in0=ps[7][:, 2:4], scalar1=flag[:], scalar2=None, op0=ALU.mult),
                                rd=["ps7", "flag"], wr=[uvk + "h"])
                          else:
                            sch.op(DVE, lambda ug=ug, j=j: nc.vector.tensor_copy(out=ug[:, 0:2],
                                                                              in_=halo_st[:, j, 0, :]),
                                   rd=["halo_st"], wr=[ugk + "h"])
                            sch.op(DVE, lambda uv=uv, j=j: nc.vector.tensor_copy(out=uv[:, 0:2],
                                                                              in_=halo_st[:, j, 1, :]),
                                   rd=["halo_st"], wr=[uvk + "h"])
                        for c in range(2):
                            e2 = ne % 2
                            ne += 1
                            pg, pv = 0 + e2, 2 + e2
                            for gv, pbank in ((0, pg), (1, pv)):
                                fns = [lambda gv=gv, dc=dc, pbank=pbank, c=c, wi=wi: nc.tensor.matmul(
                                    ps[pbank][:], wub[wi][:, gv, dc, :], hn2T[:, dc, c * 512:(c + 1) * 512],
                                    start=(dc == 0), stop=(dc == 15)) for dc in range(16)]
                                sch.ops(PE, fns, rd=wk + ["hn2T_%d_%d" % (a, c) for a in range(4)], wr=["ps%d" % pbank])
                            if c == 0:
                                emit_halo()
                            lo = 2 + 512 * c
                            sch.op(ACT, lambda ug=ug, pg=pg, lo=lo: nc.scalar.activation(
                                out=ug[:, lo:lo + 512], in_=ps[pg][:], func=AF.Copy),
                                rd=["ps%d" % pg], wr=[ugk + "c%d" % c])
                            sch.op(ACT, lambda uv=uv, pv=pv, lo=lo: nc.scalar.activation(
                                out=uv[:, lo:lo + 512], in_=ps[pv][:], func=AF.Copy),
                                rd=["ps%d" % pv], wr=[uvk + "c%d" % c])
                            for (u, uk, y, yk, jj) in ((ug, ugk, ygb[e2], "yg%d" % e2, j),
                                                       (uv, uvk, yvb[e2], "yv%d" % e2, NJ + j)):
                                urd = [uk + "h", uk + "c0", uk + "c1"] if c == 1 else [uk + "h", uk + "c0"]
                                sch.op(DVE, lambda u=u, y=y, jj=jj, lo=lo: nc.vector.tensor_scalar(
                                    out=y[:], in0=u[:, lo:lo + 512], scalar1=cw[:, 2, jj:jj + 1],
                                    scalar2=cb[:, jj:jj + 1], op0=ALU.mult, op1=ALU.add),
                                    rd=urd + ["cw", "cb"], wr=[yk])
                                sch.op(DVE, lambda u=u, y=y, jj=jj, lo=lo: nc.vector.scalar_tensor_tensor(
                                    out=y[:], in0=u[:, lo - 1:lo + 511], scalar=cw[:, 1, jj:jj + 1], in1=y[:],
                                    op0=ALU.mult, op1=ALU.add), rd=urd + ["cw", yk], wr=[yk])
                                sch.op(DVE, lambda u=u, y=y, jj=jj, lo=lo: nc.vector.scalar_tensor_tensor(
                                    out=y[:], in0=u[:, lo - 2:lo + 510], scalar=cw[:, 0, jj:jj + 1], in1=y[:],
                                    op0=ALU.mult, op1=ALU.add), rd=urd + ["cw", yk], wr=[yk])
                            sch.op(ACT, lambda e2=e2: nc.scalar.activation(out=sgb[e2][:], in_=ygb[e2][:],
                                                                            func=AF.Silu),
                                   rd=["yg%d" % e2], wr=["sg%d" % e2])
                            sch.op(DVE, lambda e2=e2, j=j, c=c: nc.vector.tensor_tensor(
                                out=gT[:, j, c * 512:(c + 1) * 512], in0=sgb[e2][:], in1=yvb[e2][:], op=ALU.mult),
                                rd=["sg%d" % e2, "yv%d" % e2], wr=["gT"])
                        if Gf == 0:
                            sch.op(DVE, lambda ug=ug, j=j: nc.vector.tensor_copy(out=halo_st[:, j, 0, :],
                                                                              in_=ug[:, 1024:1026]),
                                   rd=[ugk + "c1"], wr=["halo_st"])
                            sch.op(DVE, lambda uv=uv, j=j: nc.vector.tensor_copy(out=halo_st[:, j, 1, :],
                                                                              in_=uv[:, 1024:1026]),
                                   rd=[uvk + "c1"], wr=["halo_st"])
                    sch.barrier(keep=["wdb0_%d" % a for a in range(4)])
                with ExitStack() as pd:
                    def sd(name, shape, dt):
                        return sb(name, shape, dt, pd)
                    x1g = [sd("x1g%d" % i, [128, 8, 512], F32) for i in range(2)]
                    obuf = [sd("obuf%d" % i, [128, 8, 512], F32) for i in range(2)]
                    yT = [sd("yT%d" % i, [128, 512], F32) for i in range(2)]

                    its = [(mg, mm, c) for mg in range(4) for mm in range(4) for c in range(2)]

                    def d_mm(n):
                        mg, mm, c = its[n]
                        m = 4 * mg + mm
                        if c == 0 and m + 1 < 16:
                            load_wd(m + 1)
                        if c == 0 and m == 15 and Gf == 0:
                            load_wu(0)
                        if mm == 0 and c == 0:
                            gi = mg % 2
                            qa = 1 + 8 * Gf
                            sch.dma(POOL, x1g[gi][:],
                                    x1_s[qa:qa + 8, :, mg * 512:(mg + 1) * 512].rearrange("t p c -> p t c"),
                                    wr=["x1g%d" % gi])
                        wi = m % 2
                        wk = ["wdb%d_%d" % (wi, a) for a in range(4)]
                        pa = n % 2
                        fns = [lambda j=j: nc.tensor.matmul(
                            ps[pa][:], wdb[wi][:, j, :], gT[:, j, c * 512:(c + 1) * 512],
                            start=(j == 0), stop=(j == NJ - 1)) for j in range(NJ)]
                        sch.ops(PE, fns, rd=wk + ["gT"], wr=["ps%d" % pa])

                    def d_post(n):
                        mg, mm, c = its[n]
                        gi = mg % 2
                        e2 = n % 2
                        pa, pt = e2, 2 + e2
                        sch.op(ACT, lambda: nc.scalar.activation(out=yT[e2][:], in_=ps[pa][:], func=AF.Copy),
                               rd=["ps%d" % pa], wr=["yT%d" % e2])
                        if n + 1 < len(its):
                            d_mm(n + 1)
                        fns = [lambda k=k: nc.tensor.transpose(
                            ps[pt][:, k * 128:(k + 1) * 128], yT[e2][:, k * 128:(k + 1) * 128], ident_f[:])
                            for k in range(4)]
                        sch.ops(PE, fns, rd=["yT%d" % e2, "ident_f"], wr=["ps%d" % pt])
                        sch.op(DVE, lambda: nc.vector.tensor_tensor(
                            out=obuf[gi][:, 4 * c:4 * c + 4, mm * 128:(mm + 1) * 128],
                            in0=ps[pt][:].rearrange("p (a b) -> p a b", a=4),
                            in1=x1g[gi][:, 4 * c:4 * c + 4, mm * 128:(mm + 1) * 128], op=ALU.add),
                            rd=["ps%d" % pt, "x1g%d" % gi], wr=["obuf%d" % gi])
                        if mm == 3 and c == 1:
                            sch.dma(SP, out_v[:, 8 * Gf:8 * Gf + 8, mg * 512:(mg + 1) * 512], obuf[gi][:],
                                    rd=["obuf%d" % gi], wr=[("out", Gf, mg)])

                    d_mm(0)
                    for n in range(len(its)):
                        d_post(n)
                    sch.barrier(keep=["wub0_0", "wub0_1"])
        sch.barrier()
    return nc


def _rel_bucket_np(n):
    n = np.maximum(n, 0)
    max_exact = 16
    nf = np.maximum(n, max_exact).astype(np.float32)
    large = max_exact + (np.log(nf / np.float32(max_exact)) / np.float32(math.log(1024 / max_exact))
                         * np.float32(32 - max_exact)).astype(np.int32)
    large = np.minimum(large, 31)
    return np.where(n < max_exact, n, large)


def _static_consts():
    c = {}
    c["ident"] = np.eye(128, dtype=np.float32)
    es = np.zeros((128, 16, 128), np.float32)
    for n in range(16):
        es[n, n, :] = 1.0
    c["esel"] = es
    k = np.arange(128)[:, None]
    q = np.arange(512)[None, :]
    dist = np.stack([(-384 + 128 * j) + q - k for j in range(NBT)], 0)
    c["dist"] = dist
    c["bucket"] = _rel_bucket_np(dist)
    tp = np.arange(128)[:, None]
    t = np.arange(128)[None, :]
    band = np.zeros((3, 4, 128, 128), np.float32)
    for g, w in enumerate((2, 4, 8, 16)):
        band[0, g] = np.where(tp >= t + 129 - w, 1.0 / w, 0.0)
        incl = (tp <= t) & (tp > t - w)
        band[1, g] = np.where(incl, 1.0 / w, 0.0) - (tp == t)
        cntf = np.minimum(t + 1, w).astype(np.float32)
        band[2, g] = np.where(incl, 1.0 / cntf, 0.0) - (tp == t)
    c["band"] = band
    no = np.ones((NQT, 16), np.float32)
    for qi in range(NQT):
        no[qi, (15 + qi) // 2] = 0.0
    c["notown"] = np.ascontiguousarray(np.broadcast_to(no[None], (128, NQT, 16)))
    return c


def _prep_inputs(inp):
    f32 = np.float32
    x = np.asarray(inp["x"], f32)
    cst = _static_consts()
    rel_bias = np.asarray(inp["rel_bias"], f32)
    bt = rel_bias[:, cst["bucket"]]
    bt = np.where(cst["dist"][None] >= 0, bt, f32(-BIG)).astype(f32)
    biasT = np.ascontiguousarray(bt.transpose(0, 2, 1, 3))
    rep = lambda v: np.ascontiguousarray(np.broadcast_to(np.asarray(v, f32).reshape(1, -1), (128, v.size)))
    common = {
        "w_in": np.ascontiguousarray(inp["w_in"][0], f32),
        "w_out": np.ascontiguousarray(inp["w_out"][0], f32),
        "w_up": np.ascontiguousarray(inp["w_up"][0], f32),
        "w_down": np.ascontiguousarray(inp["w_down"][0], f32),
        "pool_w": np.ascontiguousarray(inp["pool_w"][0], f32),
        "g1bc": rep(np.asarray(inp["attn_norm_g"][0])),
        "g2bc": rep(np.asarray(inp["ffn_norm_g"][0])),
        "gqk": np.ascontiguousarray(np.stack([np.asarray(inp["q_norm_g"][0], f32),
                                              np.asarray(inp["k_norm_g"][0], f32)], 1)),
        "pscale": np.ascontiguousarray(np.asarray(inp["pool_scale"][0], f32).reshape(8, 128).T),
        "cw": np.ascontiguousarray(np.asarray(inp["conv_w"][0], f32).reshape(3, 88, 128).transpose(2, 0, 1)),
        "cb": np.ascontiguousarray(np.asarray(inp["conv_b"][0], f32).reshape(88, 128).T),
        "bias31": rep(rel_bias[:, 31]),
        "biasT": biasT,
        "notown": cst["notown"],
        "ident": cst["ident"],
        "esel": cst["esel"],
    }
    maps = []
    for core in range(8):
        b, r = core // 2, core % 2
        m = dict(common)
        if r == 1:
            m["xs"] = np.ascontiguousarray(x[b])
        else:
            m["xs"] = np.ascontiguousarray(np.concatenate([np.zeros((2048, D), f32), x[b, :2048]], 0))
        band = cst["band"].copy()
        if r == 1:
            band[2] = band[1]
        m["band"] = np.ascontiguousarray(band.transpose(2, 0, 1, 3))
        first = 0 if r == 1 else 8
        gmk = np.full((NQT, 16), -1e30, f32)
        for qi in range(NQT):
            for n in range(16):
                if first <= n < (15 + qi) // 2:
                    gmk[qi, n] = 0.0
        m["gmask"] = np.ascontiguousarray(np.broadcast_to(gmk[None], (128, NQT, 16)))
        m["flag"] = np.full((128, 1), float(r), f32)
        maps.append(m)
    return maps


_NC_CACHE = {}


def kernel(**inputs):
    maps = _prep_inputs(inputs)
    if "nc" not in _NC_CACHE:
        _NC_CACHE["nc"] = build_program(DEBUG)
    nc = _NC_CACHE["nc"]
    res = run_bass_kernel_spmd(nc, maps, core_ids=list(range(8)))
    out = np.empty((4, S, D), np.float32)
    for core in range(8):
        b, r = core // 2, core % 2
        out[b, r * 2048:(r + 1) * 2048] = res.results[core]["out"]
    if DEBUG:
        kernel.last = res
    return out
```

```python
import math
import numpy as np
import concourse.bass as bass
import concourse.mybir as mybir
from concourse.bass_utils import run_bass_kernel_spmd

F32 = mybir.dt.float32
BF16 = mybir.dt.bfloat16
AF = mybir.ActivationFunctionType
ALU = mybir.AluOpType
AX = mybir.AxisListType

D = 2048
S = 4096
HD = 128
NH = 8
INW = 4096
DFF = 5632
NJ = DFF // 128
EPS = 1e-6
BIG = 32768.0
NQT = 17
QW = NQT * 128
NBT = 11
DEBUG = False


class Eng:
    def __init__(self, name, h, sem, is_pe=False):
        self.name, self.h, self.sem, self.n, self.is_pe = name, h, sem, 0, is_pe
        self.seen = {}
        self.dsems = []
        self.dvals = []
        self.dnext = 0


class Sched:
    def __init__(self, nc):
        self.nc = nc
        self.w = {}
        self.r = {}
        self.engs = []

    def add_engine(self, e):
        self.engs.append(e)

    def _wait(self, q, t):
        if t[0] == 'c':
            _, e, n = t
            if e is q and q.is_pe:
                return
            key = e.name
            if q.seen.get(key, 0) >= n:
                return
            q.h.wait_ge(e.sem, n)
            q.seen[key] = n
        else:
            _, sem, val, key = t
            if q.seen.get(key, 0) >= val:
                return
            q.h.wait_ge(sem, val)
            q.seen[key] = val

    def _deps(self, q, rd, wr):
        for b in rd:
            t = self.w.get(b)
            if t is not None:
                self._wait(q, t)
        for b in wr:
            t = self.w.get(b)
            if t is not None:
                self._wait(q, t)
            for t2 in self.r.get(b, {}).values():
                self._wait(q, t2)

    def _record(self, tk, rk, rd, wr):
        for b in rd:
            self.r.setdefault(b, {})[rk] = tk
        for b in wr:
            self.w[b] = tk
            self.r[b] = {}

    def op(self, q, fn, rd=(), wr=()):
        self._deps(q, rd, wr)
        ins = fn()
        ins.then_inc(q.sem, 1)
        q.n += 1
        tk = ('c', q, q.n)
        self._record(tk, q.name, rd, wr)
        return tk

    def ops(self, q, fns, rd=(), wr=()):
        self._deps(q, rd, wr)
        ins = None
        for fn in fns:
            ins = fn()
        ins.then_inc(q.sem, 1)
        q.n += 1
        tk = ('c', q, q.n)
        self._record(tk, q.name, rd, wr)
        return tk

    def dma(self, q, out, in_, rd=(), wr=()):
        self._deps(q, rd, wr)
        i = q.dnext % len(q.dsems)
        q.dnext += 1
        sem = q.dsems[i]
        key = q.name + "_d%d" % i
        if q.dvals[i] > 0 and q.seen.get(key, 0) < q.dvals[i]:
            q.h.wait_ge(sem, q.dvals[i])
            q.seen[key] = q.dvals[i]
        q.h.dma_start(out=out, in_=in_).then_inc(sem, 16)
        q.dvals[i] += 16
        tk = ('d', sem, q.dvals[i], key)
        self._record(tk, key, rd, wr)
        return tk

    def barrier(self, keep=()):
        kept = {k: self.w[k] for k in keep if k in self.w}
        skip = {}
        for t in kept.values():
            if t[0] == 'd':
                skip[t[3]] = t[2]
        for q in self.engs:
            for e in self.engs:
                if e is q or e.n == 0:
                    continue
                if q.seen.get(e.name, 0) < e.n:
                    q.h.wait_ge(e.sem, e.n)
                    q.seen[e.name] = e.n
            for e in self.engs:
                for i, sem in enumerate(e.dsems):
                    key = e.name + "_d%d" % i
                    val = e.dvals[i]
                    if key in skip and skip[key] == val:
                        val -= 16
                    if val > 0 and q.seen.get(key, 0) < val:
                        q.h.wait_ge(sem, val)
                        q.seen[key] = val
        self.w = dict(kept)
        self.r = {}


def build_program(debug=False):
    nc = bass.Bass("TRN2", target_bir_lowering=False)

    def din(name, shape, dt=F32):
        return nc.dram_tensor(name, list(shape), dt, kind="ExternalInput").ap()

    xs = din("xs", [S, D])
    w_in = din("w_in", [D, INW])
    w_out = din("w_out", [D, D])
    w_up = din("w_up", [D, 2 * DFF])
    w_down = din("w_down", [DFF, D])
    pool_w = din("pool_w", [4, 256, 256])
    g1bc_d = din("g1bc", [128, D])
    g2bc_d = din("g2bc", [128, D])
    gqk_d = din("gqk", [128, 2])
    pscale_d = din("pscale", [128, 8])
    cw_d = din("cw", [128, 3, 88])
    cb_d = din("cb", [128, 88])
    bias31_d = din("bias31", [128, 8])
    biasT_d = din("biasT", [NH, 128, NBT, 512])
    band_d = din("band", [128, 3, 4, 128])
    gmask_d = din("gmask", [128, NQT, 16])
    notown_d = din("notown", [128, NQT, 16])
    flag_d = din("flag", [128, 1])
    ident_d = din("ident", [128, 128])
    esel_d = din("esel", [128, 16, 128])

    out_d = nc.dram_tensor("out", [2048, D], F32, kind="ExternalOutput").ap()
    skind = "ExternalOutput" if debug else "Internal"
    kT_s = nc.dram_tensor("kT_s", [NH, 128, S], BF16, kind=skind).ap()
    qT_s = nc.dram_tensor("qT_s", [NH, 128, S], BF16, kind=skind).ap()
    v_s = nc.dram_tensor("v_s", [32, 128, 1024], BF16, kind=skind).ap()
    x1_s = nc.dram_tensor("x1_s", [NQT, 128, D], F32, kind=skind).ap()
    mT_s = nc.dram_tensor("mT_s", [8, 128, QW], BF16, kind=skind).ap()
    hn2T_s = nc.dram_tensor("hn2T_s", [16, 128, QW], BF16, kind=skind).ap()
    if debug:
        mix_dbg = nc.dram_tensor("mix_dbg", [128, 16, QW], BF16, kind="ExternalOutput").ap()

    from contextlib import ExitStack
    top = ExitStack()
    with top:
        uid = [0]

        def sb(name, shape, dt, stack=top):
            uid[0] += 1
            return stack.enter_context(nc.sbuf_tensor("sb%d_%s" % (uid[0], name), list(shape), dt))

        def sem(name):
            return top.enter_context(nc.semaphore(name))

        sch = Sched(nc)
        PE = Eng("pe", nc.tensor, sem("s_pe"), is_pe=True)
        ACT = Eng("act", nc.scalar, sem("s_act"))
        DVE = Eng("dve", nc.vector, sem("s_dve"))
        POOL = Eng("pool", nc.gpsimd, sem("s_pool"))
        SP = Eng("sp", nc.sync, sem("s_sp"))
        for e in (PE, ACT, DVE, POOL, SP):
            sch.add_engine(e)
        for e, nd in ((SP, 16), (POOL, 12)):
            for i in range(nd):
                e.dsems.append(sem("d_%s%d" % (e.name, i)))
                e.dvals.append(0)

        ps = [top.enter_context(nc.psum_tensor("ps%d" % i, [128, 512], F32)) for i in range(8)]
        psb = [p.bitcast(BF16) for p in ps]

        ident_b = sb("ident_b", [128, 128], BF16)
        ident_f = sb("ident_f", [128, 128], F32)
        ones_b = sb("ones_b", [128, 128], BF16)
        gqk = sb("gqk", [128, 2], F32)
        gqs = sb("gqs", [128, 1], F32)
        pscale = sb("pscale", [128, 8], F32)
        bias31 = sb("bias31", [128, 8], F32)
        flag = sb("flag", [128, 1], F32)
        epsc = sb("epsc", [128, 1], F32)
        zcol = sb("zcol", [128, 1], F32)
        esel = sb("esel", [128, 16, 128], BF16)
        gmask = sb("gmask", [128, NQT, 16], F32)
        notown = sb("notown", [128, NQT, 16], F32)

        sch.dma(POOL, ident_b[:], ident_d, wr=["ident_b"])
        sch.op(DVE, lambda: nc.vector.memset(ones_b[:], 1.0), wr=["ones_b"])
        sch.op(DVE, lambda: nc.vector.memset(epsc[:], EPS), wr=["epsc"])
        sch.op(DVE, lambda: nc.vector.memset(zcol[:], 0.0), wr=["zcol"])

        def late_consts():
            sch.dma(SP, ident_f[:], ident_d, wr=["ident_f"])
            sch.dma(SP, gqk[:], gqk_d, wr=["gqk"])
            sch.dma(SP, pscale[:], pscale_d, wr=["pscale"])
            sch.dma(SP, bias31[:], bias31_d, wr=["bias31"])
            sch.dma(SP, flag[:], flag_d, wr=["flag"])
            sch.dma(POOL, esel[:], esel_d, wr=["esel"])
            sch.dma(SP, gmask[:], gmask_d, wr=["gmask"])
            sch.dma(SP, notown[:], notown_d, wr=["notown"])
            sch.op(DVE, lambda: nc.vector.tensor_scalar(out=gqs[:], in0=gqk[:, 0:1], scalar1=float(HD ** -0.5),
                                                        scalar2=None, op0=ALU.mult),
                   rd=["gqk"], wr=["gqs"])

        pm = ExitStack()

        def rmsnorm_tile(xt, xkey, gbc, hn, hnkey, ss, rt1, rstd, idx):
            k = "n%d" % (idx % 2)
            sch.op(ACT, lambda: nc.scalar.activation(out=hn, in_=xt, func=AF.Square, accum_out=ss[:]),
                   rd=[xkey], wr=[hnkey, "ss" + k])
            sch.op(ACT, lambda: nc.scalar.activation(out=rt1[:], in_=ss[:], func=AF.Ln, bias=epsc[:],
                                                     scale=1.0 / D),
                   rd=["ss" + k, "epsc"], wr=["rt1" + k])
            sch.op(ACT, lambda: nc.scalar.activation(out=rstd[:], in_=rt1[:], func=AF.Exp, bias=zcol[:],
                                                     scale=-0.5),
                   rd=["rt1" + k, "zcol"], wr=["rstd" + k])
            sch.op(DVE, lambda: nc.vector.scalar_tensor_tensor(out=hn, in0=xt, scalar=rstd[:], in1=gbc,
                                                               op0=ALU.mult, op1=ALU.mult),
                   rd=[xkey, "rstd" + k, "gbc"], wr=[hnkey])

        with ExitStack() as p1:
            def s1(name, shape, dt):
                return sb(name, shape, dt, p1)
            g1bc = s1("g1bc", [128, D], F32)
            sch.dma(SP, g1bc[:], g1bc_d, wr=["gbc"])
            band_b = s1("band_b", [128, 3, 4, 128], BF16)
            poolw_b = s1("poolw_b", [128, 4, 2, 256], BF16)
            xbuf = [s1("xbuf%d" % i, [128, D], F32) for i in range(4)]
            hnb = [s1("hnb%d" % i, [128, D], BF16) for i in range(2)]
            ssb = [s1("ss%d" % i, [128, 1], F32) for i in range(2)]
            rt1b = [s1("rt1%d" % i, [128, 1], F32) for i in range(2)]
            rstdb = [s1("rstd%d" % i, [128, 1], F32) for i in range(2)]
            hnTb = [s1("hnT%d" % i, [128, 16, 1024], BF16) for i in range(2)]
            mst = [s1("mst%d" % i, [128, 8, 128], BF16) for i in range(2)]
            wsl = [s1("wsl%d" % i, [128, 16, 512], BF16) for i in range(2)]
            pall = s1("pall", [128, 9, 1024], BF16)
            sqb = [s1("sqb%d" % i, [128, 512], BF16) for i in range(2)]
            rtb = [s1("rtb%d" % i, [128, 512], F32) for i in range(2)]
            rrb = [s1("rrb%d" % i, [128, 512], F32) for i in range(2)]
            stg = [s1("stg%d" % i, [128, 512], BF16) for i in range(3)]
            vst = [s1("vst%d" % i, [128, 512], BF16) for i in range(3)]
            mixT = [s1("mixT%d" % i, [128, 8, 128], BF16) for i in range(2)]

            sch.dma(POOL, band_b[:], band_d, wr=["band"])
            sch.dma(POOL, poolw_b[:], pool_w.rearrange("g (kh p) c -> p g kh c", p=128), wr=["poolw"])
            sch.op(DVE, lambda: nc.vector.memset(pall[:, 0, :], 0.0), wr=["pall0"])

            w_in_v = w_in.rearrange("(dc p) c -> p dc c", p=128)
            cnt = {"qk": 0, "v": 0, "slab": 0, "x": 0}

            def load_x(gt):
                if gt >= 32:
                    return
                i = gt % 4
                sch.dma(SP if gt < 8 else POOL, xbuf[i][:], xs[gt * 128:(gt + 1) * 128, :], wr=["xbuf%d" % i])

            def load_slab(s):
                i = cnt["slab"] % 2
                cnt["slab"] += 1
                for a in range(4):
                    sch.dma(POOL, wsl[i][:, 4 * a:4 * a + 4, :], w_in_v[:, 4 * a:4 * a + 4, s * 512:(s + 1) * 512],
                            wr=["wsl%d_%d" % (i, a)])
                return i

            def slab_keys(i):
                return ["wsl%d_%d" % (i, a) for a in range(4)]

            post_q = []

            def flush_post():
                while post_q:
                    norm_post(*post_q.pop(0))

            def norm_pre(G, ti):
                gt = 8 * G + ti
                hi = gt % 2
                if any((8 * g + t) % 2 == hi for (g, t) in post_q):
                    flush_post()
                load_x(gt + 3)
                xi = gt % 4
                rmsnorm_tile(xbuf[xi][:], "xbuf%d" % xi, g1bc[:], hnb[hi][:], "hnb%d" % hi,
                             ssb[hi], rt1b[hi], rstdb[hi], hi)

            def norm_post(G, ti):
                gt = 8 * G + ti
                hi = gt % 2
                hb = G % 2
                for a in range(4):
                    bank = 5 + (a % 2)
                    fns = []
                    for b in range(4):
                        dc = 4 * a + b
                        fns.append(lambda dc=dc, b=b, bank=bank, hi=hi: nc.tensor.transpose(
                            psb[bank][:, b * 128:(b + 1) * 128], hnb[hi][:, dc * 128:(dc + 1) * 128], ident_b[:]))
                    sch.ops(PE, fns, rd=["hnb%d" % hi, "ident_b"], wr=["ps%d" % bank])
                    if a % 2 == 0:
                        sch.op(ACT, lambda a=a, bank=bank, ti=ti, hb=hb: nc.scalar.activation(
                            out=hnTb[hb][:, 4 * a:4 * a + 4, ti * 128:(ti + 1) * 128],
                            in_=psb[bank][:, 0:512].rearrange("p (a b) -> p a b", a=4), func=AF.Copy),
                            rd=["ps%d" % bank], wr=["hnT%d_%d" % (hb, ti // 4)])
                    else:
                        sch.op(DVE, lambda a=a, bank=bank, ti=ti, hb=hb: nc.vector.tensor_copy(
                            out=hnTb[hb][:, 4 * a:4 * a + 4, ti * 128:(ti + 1) * 128],
                            in_=psb[bank][:, 0:512].rearrange("p (a b) -> p a b", a=4)),
                            rd=["ps%d" % bank], wr=["hnT%d_%d" % (hb, ti // 4)])

            for gt in range(3):
                load_x(gt)
            pre_slab = load_slab(2)
            late_consts()
            norm_pre(0, 0)
            for ti in range(8):
                if ti + 1 < 8:
                    norm_pre(0, ti + 1)
                norm_post(0, ti)
            nmt = 0
            for G in range(4):
                slabs = [2, 3, 4, 5] if G == 0 else list(range(8))
                hnT = hnTb[G % 2]
                hk = "hnT%d" % (G % 2)
                tps = 8 // len(slabs)
                for si, s in enumerate(slabs):
                    wi = pre_slab
                    if G + 1 < 4:
                        for ti in range(si * tps, (si + 1) * tps):
                            norm_pre(G + 1, ti)
                    if si + 1 < len(slabs):
                        pre_slab = load_slab(slabs[si + 1])
                    elif G + 1 < 4:
                        pre_slab = load_slab(0)
                    W = wsl[wi]
                    wk = slab_keys(wi)
                    if s < 4:
                        isq = s < 2
                        dst = qT_s if isq else kT_s
                        gcol = gqs[:] if isq else gqk[:, 1:2]
                        its = [(hh, c) for c in range(2) for hh in range(4)
                               if not (isq and G == 1 and c == 0)]

                        halo_q = isq and G == 1
                        nw = 128 if halo_q else 512
                        cofs = 384 if halo_q else 0

                        def qk_a(n, its=its, W=W, wk=wk, nw=nw, cofs=cofs):
                            hh, c = its[n]
                            if c == 1:
                                flush_post()
                            k = cnt["qk"] + n
                            pa, i2 = k % 3, k % 2
                            t0 = c * 512 + cofs
                            fns = [lambda dc=dc: nc.tensor.matmul(
                                ps[pa][:, 0:nw], W[:, dc, hh * 128:(hh + 1) * 128], hnT[:, dc, t0:t0 + nw],
                                start=(dc == 0), stop=(dc == 15)) for dc in range(16)]
                            sch.ops(PE, fns, rd=wk + [hk + "_%d" % c], wr=["ps%d" % pa])
                            sch.op(ACT, lambda: nc.scalar.activation(out=sqb[i2][:, 0:nw], in_=ps[pa][:, 0:nw],
                                                                     func=AF.Square),
                                   rd=["ps%d" % pa], wr=["sqb%d" % i2])

                        def qk_b(n, its=its, s=s, isq=isq, dst=dst, gcol=gcol, nw=nw, cofs=cofs):
                            hh, c = its[n]
                            head = (s % 2) * 4 + hh
                            k = cnt["qk"] + n
                            pa, i2, i3 = k % 3, k % 2, k % 3
                            pb = 3 + i2
                            sch.op(PE, lambda: nc.tensor.matmul(ps[pb][:, 0:nw], ones_b[:], sqb[i2][:, 0:nw],
                                                                start=True, stop=True),
                                   rd=["sqb%d" % i2, "ones_b"], wr=["ps%d" % pb])
                            sch.op(ACT, lambda: nc.scalar.activation(
                                out=rtb[i2][:, 0:nw], in_=ps[pb][:, 0:nw], func=AF.Ln, bias=epsc[:], scale=1.0 / HD),
                                rd=["ps%d" % pb, "epsc"], wr=["rtb%d" % i2])
                            sch.op(ACT, lambda: nc.scalar.activation(
                                out=rrb[i2][:, 0:nw], in_=rtb[i2][:, 0:nw], func=AF.Exp, bias=zcol[:], scale=-0.5),
                                rd=["rtb%d" % i2, "zcol"], wr=["rrb%d" % i2])
                            sch.op(DVE, lambda: nc.vector.scalar_tensor_tensor(
                                out=stg[i3][:, 0:nw], in0=ps[pa][:, 0:nw], scalar=gcol, in1=rrb[i2][:, 0:nw],
                                op0=ALU.mult, op1=ALU.mult),
                                rd=["ps%d" % pa, "rrb%d" % i2, "gqs", "gqk"], wr=["stg%d" % i3])
                            t0 = G * 1024 + c * 512 + cofs
                            sch.dma(SP, dst[head, :, t0:t0 + nw], stg[i3][:, 0:nw], rd=["stg%d" % i3],
                                    wr=[("qs" if isq else "ks", head, G, c)])

                        qk_a(0)
                        for n in range(len(its)):
                            if n + 1 < len(its):
                                qk_a(n + 1)
                            if n == 1:
                                flush_post()
                            qk_b(n)
                        flush_post()
                        cnt["qk"] += len(its)
                    else:
                        isv = s < 6
                        for ti in range(8):
                            gt = 8 * G + ti
                            if (not isv) and G == 1 and ti < 6:
                                continue
                            i2 = cnt["v"] % 2
                            i3 = cnt["v"] % 3
                            cnt["v"] += 1
                            pv = i3
                            if ti >= 4:
                                flush_post()
                            fns = [lambda dc=dc, ti=ti, pv=pv: nc.tensor.matmul(
                                ps[pv][:], hnT[:, dc, ti * 128:(ti + 1) * 128], W[:, dc, :],
                                start=(dc == 0), stop=(dc == 15)) for dc in range(16)]
                            sch.ops(PE, fns, rd=wk + [hk + "_%d" % (ti // 4)], wr=["ps%d" % pv])
                            if ti == 1:
                                flush_post()
                            if isv:
                                sch.op(DVE, lambda pv=pv, i3=i3: nc.vector.tensor_copy(out=vst[i3][:], in_=ps[pv][:]),
                                       rd=["ps%d" % pv], wr=["vst%d" % i3])
                                c0 = (s - 4) * 512
                                sch.dma(SP, v_s[gt, :, c0:c0 + 512], vst[i3][:], rd=["vst%d" % i3],
                                        wr=[("vs", gt, s)])
                            else:
                                c0 = (s - 6) * 512
                                sch.op(DVE, lambda pv=pv, ti=ti, c0=c0: nc.vector.tensor_copy(
                                    out=pall[:, 1 + ti, c0:c0 + 512], in_=ps[pv][:]),
                                    rd=["ps%d" % pv], wr=["pall%d" % (1 + ti)])
                    if G + 1 < 4:
                        for ti in range(si * tps, (si + 1) * tps):
                            post_q.append((G + 1, ti))
                ptiles = [ti for ti in range(8) if 8 * G + ti >= 15]

                def pool_x(ti, G=G):
                    gt = 8 * G + ti
                    kind = 2 if gt == 16 else 1
                    mx = gt % 2
                    for a in range(2):
                        bank = 0 + a
                        fns = []
                        for b in range(4):
                            c8 = 4 * a + b
                            g = c8 // 2
                            fns.append(lambda b=b, c8=c8, g=g: nc.tensor.matmul(
                                ps[bank][:, b * 128:(b + 1) * 128], pall[:, ti, c8 * 128:(c8 + 1) * 128],
                                band_b[:, 0, g, :], start=True, stop=False))
                            fns.append(lambda b=b, c8=c8, g=g: nc.tensor.matmul(
                                ps[bank][:, b * 128:(b + 1) * 128], pall[:, 1 + ti, c8 * 128:(c8 + 1) * 128],
                                band_b[:, kind, g, :], start=False, stop=True))
                        sch.ops(PE, fns, rd=["pall%d" % ti, "pall%d" % (1 + ti), "band"], wr=["ps%d" % bank])
                        sch.op(DVE, lambda a=a: nc.vector.tensor_copy(
                            out=mixT[mx][:, 4 * a:4 * a + 4, :], in_=ps[bank][:].rearrange("p (a b) -> p a b", a=4)),
                            rd=["ps%d" % bank], wr=["mixT%d_%d" % (mx, a)])

                def pool_y(ti, G=G):
                    gt = 8 * G + ti
                    qc = (gt - 15) * 128
                    mx = gt % 2
                    mi = gt % 2
                    for a in range(2):
                        bank = 2 + a
                        fns = []
                        for b in range(4):
                            c8o = 4 * a + b
                            g = c8o // 2
                            half = c8o % 2
                            for kh in range(2):
                                fns.append(lambda b=b, g=g, half=half, kh=kh: nc.tensor.matmul(
                                    ps[bank][:, b * 128:(b + 1) * 128],
                                    poolw_b[:, g, kh, half * 128:(half + 1) * 128], mixT[mx][:, 2 * g + kh, :],
                                    start=(kh == 0), stop=(kh == 1)))
                        sch.ops(PE, fns, rd=["mixT%d_0" % mx, "mixT%d_1" % mx, "poolw"], wr=["ps%d" % bank])
                        for b in range(4):
                            c8o = 4 * a + b
                            if a == 0:
                                sch.op(ACT, lambda b=b, c8o=c8o: nc.scalar.activation(
                                    out=mst[mi][:, c8o, :], in_=ps[bank][:, b * 128:(b + 1) * 128],
                                    func=AF.Identity, bias=zcol[:], scale=pscale[:, c8o:c8o + 1]),
                                    rd=["ps%d" % bank, "pscale", "zcol"], wr=["mst%d_%d" % (mi, a)])
                            else:
                                sch.op(DVE, lambda b=b, c8o=c8o: nc.vector.tensor_scalar(
                                    out=mst[mi][:, c8o, :], in0=ps[bank][:, b * 128:(b + 1) * 128],
                                    scalar1=pscale[:, c8o:c8o + 1], scalar2=None, op0=ALU.mult),
                                    rd=["ps%d" % bank, "pscale"], wr=["mst%d_%d" % (mi, a)])
                    sch.dma(SP, mT_s[:, :, qc:qc + 128].rearrange("c p t -> p c t"), mst[mi][:],
                            rd=["mst%d_0" % mi, "mst%d_1" % mi], wr=[("mTs", gt)])

                flush_post()
                if ptiles:
                    pool_x(ptiles[0])
                    for k, ti in enumerate(ptiles):
                        if k + 1 < len(ptiles):
                            pool_x(ptiles[k + 1])
                        pool_y(ti)
                if G <= 1:
                    flush_post()
                if G >= 1:
                    sch.op(DVE, lambda: nc.vector.tensor_copy(out=pall[:, 0, :], in_=pall[:, 8, :]),
                           rd=["pall8"], wr=["pall0"])
            sch.barrier()

        aT = sb("aT", [128, 8, QW], BF16, pm)
        wo_sb = sb("wo_sb", [128, 16, D], BF16, pm)
        w_out_v = w_out.rearrange("(cc p) d -> p cc d", p=128)
        with ExitStack() as p2:
            def s2(name, shape, dt):
                return sb(name, shape, dt, p2)
            kTh = [s2("kTh%d" % i, [128, S], BF16) for i in range(2)]
            vh = [s2("vh%d" % i, [128, 32, 128], BF16) for i in range(2)]
            qTh = [s2("qTh%d" % i, [128, QW], BF16) for i in range(2)]
            bTh = [s2("bTh%d" % i, [128, NBT, 512], BF16) for i in range(2)]
            MT = [s2("MT%d" % i, [128, QW], BF16) for i in range(2)]
            PT = [s2("PT%d" % i, [128, 512], BF16) for i in range(4)]
            rec = [s2("rec%d" % i, [128, 512], F32) for i in range(2)]
            lnd = [s2("lnd%d" % i, [128, 512], F32) for i in range(2)]
            accP = [s2("accP%d" % i, [128, 512], F32) for i in range(2)]
            accPb = [s2("accPb%d" % i, [128, 512], BF16) for i in range(2)]
            kmf = s2("kmf", [128, 16], F32)
            kmb = [s2("kmb%d" % i, [128, 16], BF16) for i in range(2)]
            gmA = [s2("gmA%d" % i, [128, NQT, 16], F32) for i in range(2)]
            top8 = [s2("top8%d" % i, [128, NQT, 8], F32) for i in range(2)]
            thrA = [s2("thrA%d" % i, [128, NQT], F32) for i in range(2)]
            nselA = [s2("nselA%d" % i, [128, NQT, 16], F32) for i in range(2)]
            maddA = [s2("maddA%d" % i, [128, NQT, 16], F32) for i in range(2)]

            def load_head(h):
                i = h % 2
                sch.dma(SP, kTh[i][:], kT_s[h], wr=["kTh%d" % i])
                sch.dma(SP, qTh[i][:], qT_s[h, :, 1920:S], wr=["qTh%d" % i])
                for a in range(4):
                    sch.dma(POOL, vh[i][:, 8 * a:8 * a + 8, :],
                            v_s[8 * a:8 * a + 8, :, h * 128:(h + 1) * 128].rearrange("t p c -> p t c"),
                            wr=["vh%d_%d" % (i, a)])
                for a in range(NBT):
                    sch.dma(POOL, bTh[i][:, a, :], biasT_d[h, :, a, :], wr=["bTh%d_%d" % (i, a)])

            def gate_stage1a(h, part):
                i = h % 2
                sch.op(DVE, lambda: nc.vector.tensor_reduce(
                    out=kmf[:, 4 * part:4 * part + 4],
                    in_=kTh[i][:, 1024 * part:1024 * (part + 1)].rearrange("p (n k) -> p n k", n=4),
                    axis=AX.X, op=ALU.add), rd=["kTh%d" % i], wr=["kmf%d" % part])
                if part == 3:
                    sch.op(DVE, lambda: nc.vector.tensor_scalar(out=kmb[i][:], in0=kmf[:], scalar1=1.0 / 256.0,
                                                                scalar2=None, op0=ALU.mult),
                           rd=["kmf%d" % p for p in range(4)], wr=["kmb%d" % i])

            def gate_stage1b(h, piece=None):
                i = h % 2
                pcs = range(5) if piece is None else [piece]
                for pc in pcs:
                    if pc == 0:
                        fns = [lambda qi=qi: nc.tensor.matmul(
                            ps[7][:, qi * 16:(qi + 1) * 16], qTh[i][:, qi * 128:(qi + 1) * 128], kmb[i][:],
                            start=True, stop=True) for qi in range(NQT)]
                        sch.ops(PE, fns, rd=["qTh%d" % i, "kmb%d" % i], wr=["ps7"])
                        sch.op(DVE, lambda: nc.vector.tensor_tensor(
                            out=gmA[i][:], in0=ps[7][:, 0:NQT * 16].rearrange("p (a b) -> p a b", a=NQT),
                            in1=gmask[:], op=ALU.add), rd=["ps7", "gmask"], wr=["gmA%d" % i])
                    elif pc in (1, 2, 3):
                        qs = [range(0, 6), range(6, 12), range(12, NQT)][pc - 1]
                        fns = [lambda qi=qi: nc.vector.max(out=top8[i][:, qi, :], in_=gmA[i][:, qi, :]) for qi in qs]
                        sch.ops(DVE, fns, rd=["gmA%d" % i], wr=["top8%d_%d" % (i, pc)])
                    else:
                        sch.op(DVE, lambda: nc.vector.tensor_scalar(
                            out=thrA[i][:], in0=top8[i][:, :, 2], scalar1=-1e29, scalar2=None, op0=ALU.max),
                            rd=["top8%d_%d" % (i, p) for p in (1, 2, 3)], wr=["thrA%d" % i])
                        sch.op(DVE, lambda: nc.vector.tensor_tensor(
                            out=nselA[i][:], in0=gmA[i][:],
                            in1=thrA[i][:].unsqueeze(2).to_broadcast([128, NQT, 16]), op=ALU.is_lt),
                            rd=["gmA%d" % i, "thrA%d" % i], wr=["nselA%d" % i])
                        sch.op(DVE, lambda: nc.vector.scalar_tensor_tensor(
                            out=maddA[i][:], in0=nselA[i][:], scalar=-BIG, in1=notown[:], op0=ALU.mult, op1=ALU.mult),
                            rd=["nselA%d" % i, "notown"], wr=["maddA%d" % i])

            def gate_stage2_pe(h, rnd, bank=5):
                i = h % 2
                qis = list(range(4 * rnd, min(4 * rnd + 4, NQT)))
                fns = [lambda k=k, qi=qi: nc.tensor.transpose(
                    ps[bank][0:16, k * 128:(k + 1) * 128], maddA[i][:, qi, :], ident_f[:])
                    for k, qi in enumerate(qis)]
                sch.ops(PE, fns, rd=["maddA%d" % i, "ident_f"], wr=["ps%d" % bank])

            def gate_stage2_act(h, rnd, bank=5):
                i = h % 2
                qis = list(range(4 * rnd, min(4 * rnd + 4, NQT)))
                n = len(qis)
                q0 = qis[0]
                sch.op(ACT, lambda: nc.scalar.activation(
                    out=MT[i][0:16, q0 * 128:(q0 + n) * 128], in_=ps[bank][0:16, 0:n * 128], func=AF.Copy),
                    rd=["ps%d" % bank], wr=["MT%d" % i])

            accPh = s2("accPh", [128, 512], F32)
            accPbh = s2("accPbh", [128, 512], BF16)
            for i in range(2):
                sch.op(DVE, lambda i=i: nc.vector.memset(MT[i][:], 0.0), wr=["MT%d" % i])
            load_head(0)
            for part in range(4):
                gate_stage1a(0, part)
            gate_stage1b(0)
            g0banks = [5, 3, 4, 6]
            for r in range(4):
                gate_stage2_pe(0, r, g0banks[r])
            for r in range(4):
                gate_stage2_act(0, r, g0banks[r])
            gate_stage2_pe(0, 4)
            gate_stage2_act(0, 4)

            SB = [0, 1, 2, 6]
            PTK = {0: 0, 1: 1, 2: 2, 6: 3}
            U = []
            nreg = 0
            for h in range(NH):
                hu = []
                halo = dict(h=h, halo=True, q0=0, nq=128, Q0=1920, nkt=16, pOap=ps[7][:, 384:512], pOk="ps7h",
                            acc=accPh, accb=accPbh, acck="accPh", accbk="accPbh", ab=0)
                hunits = [dict(ctx=halo, kts=list(range(4 * g, 4 * g + 4))) for g in range(4)]
                for c in range(4):
                    ab = nreg % 2
                    nreg += 1
                    ctx = dict(h=h, halo=False, q0=128 + 512 * c, nq=512, Q0=2048 + 512 * c,
                               nkt=(2048 + 512 * c + 512) // 128, pOap=ps[3 + ab][:, 0:512], pOk="ps%d" % (3 + ab),
                               acc=accP[ab], accb=accPb[ab], acck="accP%d" % ab, accbk="accPb%d" % ab, ab=ab)
                    for kt in range(ctx["nkt"]):
                        hu.append(dict(ctx=ctx, kts=[kt]))
                        if c == 0 and kt % 4 == 3 and kt // 4 < 4:
                            hu.append(hunits[kt // 4])
                for k, u in enumerate(hu):
                    u["hidx"] = k
                U += hu
            NU = len(U)
            cnt2 = {"s": 0}
            later = {}

            def at(n, fn):
                later.setdefault(min(n, NU - 1), []).append(fn)

            def scores(n):
                u = U[n]
                c = u["ctx"]
                i = c["h"] % 2
                nq, q0, Q0 = c["nq"], c["q0"], c["Q0"]
                sbk = SB[cnt2["s"] % 4]
                cnt2["s"] += 1
                u["sbk"] = sbk
                fns = []
                bdeps = []
                for idx, kt in enumerate(u["kts"]):
                    D0 = Q0 - 128 * kt
                    near = D0 <= 896
                    assert u.setdefault("near", near) == near
                    if near:
                        bdeps.append("bTh%d_%d" % (i, (D0 + 384) // 128))
                    cs = slice(idx * nq, (idx + 1) * nq)
                    opnds = [(kTh[i][:, kt * 128:(kt + 1) * 128], qTh[i][:, q0:q0 + nq])]
                    if near:
                        j = (D0 + 384) // 128
                        opnds.append((ident_b[:], bTh[i][:, j, 0:nq]))
                    if kt < c["nkt"] - 2:
                        opnds.append((esel[:, kt // 2, :], MT[i][:, q0:q0 + nq]))
                    for oi, (lh, rh) in enumerate(opnds):
                        fns.append(lambda lh=lh, rh=rh, cs=cs, oi=oi, no=len(opnds): nc.tensor.matmul(
                            ps[sbk][:, cs], lh, rh, start=(oi == 0), stop=(oi == no - 1)))
                sch.ops(PE, fns, rd=["kTh%d" % i, "qTh%d" % i, "MT%d" % i, "esel", "ident_b"] + bdeps,
                        wr=["ps%d" % sbk])

            def finalizeA0(c):
                sch.op(DVE, lambda: nc.vector.tensor_copy(out=c["accb"][:, 0:512], in_=c["acc"][:, 0:512]),
                       rd=[c["acck"]], wr=[c["accbk"]])

            def finalizeA(c):
                nq = c["nq"]
                grp = 512 // nq
                fns = [lambda g=g: nc.tensor.matmul(ps[5][:, 0:nq], ones_b[:], c["accb"][:, g * nq:(g + 1) * nq],
                                                    start=(g == 0), stop=(g == grp - 1)) for g in range(grp)]
                sch.ops(PE, fns, rd=[c["accbk"], "ones_b"], wr=["ps5"])

            def finalizeB(c):
                nq, ab, h, q0 = c["nq"], c["ab"], c["h"], c["q0"]
                sch.op(ACT, lambda: nc.scalar.activation(out=lnd[ab][:, 0:nq], in_=ps[5][:, 0:nq],
                                                         func=AF.Ln, bias=zcol[:], scale=1.0),
                       rd=["ps5", "zcol"], wr=["lnd%d" % ab])

            def finalizeC(c):
                nq, ab, h, q0 = c["nq"], c["ab"], c["h"], c["q0"]
                sch.op(ACT, lambda: nc.scalar.activation(out=rec[ab][:, 0:nq], in_=lnd[ab][:, 0:nq],
                                                         func=AF.Exp, bias=zcol[:], scale=-1.0),
                       rd=["lnd%d" % ab, "zcol"], wr=["rec%d" % ab])
                sch.op(DVE, lambda: nc.vector.tensor_tensor(
                    out=aT[:, h, q0:q0 + nq], in0=c["pOap"][:, 0:nq], in1=rec[ab][:, 0:nq], op=ALU.mult),
                    rd=[c["pOk"], "rec%d" % ab], wr=["aT"])

            scores(0)
            scores(1)
            for n in range(NU):
                u = U[n]
                c = u["ctx"]
                h = c["h"]
                i = h % 2
                nq = c["nq"]
                k = u["hidx"]
                W = nq * len(u["kts"])
                if k == 0:
                    if h + 1 < NH:
                        load_head(h + 1)
                    if h == 0:
                        for cc in range(16):
                            sch.dma(POOL, wo_sb[:, cc, :], w_out_v[:, cc, :], wr=["wo%d" % cc])
                if h + 1 < NH:
                    if k in (34, 36, 38, 40):
                        gate_stage1a(h + 1, (k - 34) // 2)
                    if 42 <= k <= 46:
                        gate_stage1b(h + 1, k - 42)
                    if k >= 60 and (k - 60) % 3 == 0 and (k - 60) // 3 < 5:
                        gate_stage2_pe(h + 1, (k - 60) // 3)
                if n + 2 < NU:
                    scores(n + 2)
                sbk = u["sbk"]
                pk = PTK[sbk]
                bcol = zcol[:] if u["near"] else bias31[:, h:h + 1]
                sch.op(ACT, lambda: nc.scalar.activation(
                    out=PT[pk][:, 0:W], in_=ps[sbk][:, 0:W], func=AF.Exp, bias=bcol, scale=1.0),
                    rd=["ps%d" % sbk, "bias31", "zcol"], wr=["PT%d" % pk])
                if h + 1 < NH and k >= 60 and (k - 60) % 3 == 0 and (k - 60) // 3 < 5:
                    gate_stage2_act(h + 1, (k - 60) // 3)
                fns = []
                for idx, kt in enumerate(u["kts"]):
                    fns.append(lambda idx=idx, kt=kt: nc.tensor.matmul(
                        c["pOap"][:, 0:nq], vh[i][:, kt, :], PT[pk][:, idx * nq:(idx + 1) * nq],
                        start=(kt == 0), stop=(kt == c["nkt"] - 1)))
                sch.ops(PE, fns, rd=["PT%d" % pk] + ["vh%d_%d" % (i, a) for a in range(4)], wr=[c["pOk"]])
                if u["kts"][0] == 0:
                    sch.op(DVE, lambda: nc.vector.tensor_copy(out=c["acc"][:, 0:W], in_=PT[pk][:, 0:W]),
                           rd=["PT%d" % pk], wr=[c["acck"]])
                else:
                    sch.op(DVE, lambda: nc.vector.tensor_tensor(
                        out=c["acc"][:, 0:W], in0=c["acc"][:, 0:W], in1=PT[pk][:, 0:W], op=ALU.add),
                        rd=["PT%d" % pk, c["acck"]], wr=[c["acck"]])
                if u["kts"][-1] == c["nkt"] - 1:
                    at(n + 3, lambda c=c: finalizeA0(c))
                    at(n + 6, lambda c=c: finalizeA(c))
                    at(n + 8, lambda c=c: finalizeB(c))
                    at(n + 10, lambda c=c: finalizeC(c))
                for fn in later.pop(n, []):
                    fn()
            sch.barrier()

        with ExitStack() as p3:
            def s3(name, shape, dt):
                return sb(name, shape, dt, p3)
            mT = s3("mT", [128, 8, QW], BF16)
            g2bc = s3("g2bc", [128, D], F32)
            xo = [s3("xo%d" % i, [128, D], F32) for i in range(2)]
            x1t = [s3("x1t%d" % i, [128, D], F32) for i in range(2)]
            hn2 = [s3("hn2_%d" % i, [128, D], BF16) for i in range(2)]
            ss2 = [s3("ss2_%d" % i, [128, 1], F32) for i in range(2)]
            rt2 = [s3("rt2_%d" % i, [128, 1], F32) for i in range(2)]
            rs2 = [s3("rs2_%d" % i, [128, 1], F32) for i in range(2)]
            hst = [s3("hst%d" % i, [128, 16, 128], BF16) for i in range(2)]
            for cc in range(8):
                sch.dma(SP if cc % 2 == 0 else POOL, mT[:, cc, :], mT_s[cc], wr=["mT%d" % cc])
            sch.dma(POOL, g2bc[:], g2bc_d, wr=["gbc"])
            if debug:
                sch.dma(SP, mix_dbg[:, 0:8, :], aT[:], rd=["aT"])
                sch.dma(SP, mix_dbg[:, 8:16, :], mT[:], rd=["mT%d" % cc for cc in range(8)])
            wokeys = ["wo%d" % cc for cc in range(16)]

            def load_xo(qi):
                gt = 15 + qi
                sch.dma(POOL, xo[qi % 2][:], xs[gt * 128:(gt + 1) * 128, :], wr=["xo%d" % (qi % 2)])

            def o_mm(qi):
                i = qi % 2
                if qi + 1 < NQT:
                    load_xo(qi + 1)
                for sl in range(4):
                    bank = (4 * qi + sl) % 6
                    fns = []
                    for cc in range(16):
                        src = aT if cc < 8 else mT
                        fns.append(lambda cc=cc, src=src: nc.tensor.matmul(
                            ps[bank][:], src[:, cc % 8, qi * 128:(qi + 1) * 128], wo_sb[:, cc, sl * 512:(sl + 1) * 512],
                            start=(cc == 0), stop=(cc == 15)))
                    sch.ops(PE, fns, rd=["aT"] + ["mT%d" % cc for cc in range(8)] + wokeys, wr=["ps%d" % bank])
                    sch.op(DVE, lambda: nc.vector.tensor_tensor(
                        out=x1t[i][:, sl * 512:(sl + 1) * 512], in0=ps[bank][:], in1=xo[i][:, sl * 512:(sl + 1) * 512],
                        op=ALU.add), rd=["ps%d" % bank, "xo%d" % i], wr=["x1t%d" % i])
                sch.dma(SP, x1_s[qi], x1t[i][:], rd=["x1t%d" % i], wr=[("x1s", qi)])

            def o_norm(qi):
                i = qi % 2
                rmsnorm_tile(x1t[i][:], "x1t%d" % i, g2bc[:], hn2[i][:], "hn2_%d" % i, ss2[i], rt2[i], rs2[i], i)

            def o_tr(qi):
                i = qi % 2
                for a in range(4):
                    bank = 6 + (a % 2)
                    fns = []
                    for b in range(4):
                        dc = 4 * a + b
                        fns.append(lambda dc=dc, b=b: nc.tensor.transpose(
                            psb[bank][:, b * 128:(b + 1) * 128], hn2[i][:, dc * 128:(dc + 1) * 128], ident_b[:]))
                    sch.ops(PE, fns, rd=["hn2_%d" % i, "ident_b"], wr=["ps%d" % bank])
                    if a % 2 == 0:
                        sch.op(ACT, lambda a=a: nc.scalar.activation(
                            out=hst[i][:, 4 * a:4 * a + 4, :],
                            in_=psb[bank][:, 0:512].rearrange("p (a b) -> p a b", a=4), func=AF.Copy),
                            rd=["ps%d" % bank], wr=["hst%d_%d" % (i, a)])
                    else:
                        sch.op(DVE, lambda a=a: nc.vector.tensor_copy(
                            out=hst[i][:, 4 * a:4 * a + 4, :],
                            in_=psb[bank][:, 0:512].rearrange("p (a b) -> p a b", a=4)),
                            rd=["ps%d" % bank], wr=["hst%d_%d" % (i, a)])
                sch.dma(SP, hn2T_s[:, :, qi * 128:(qi + 1) * 128].rearrange("dc p t -> p dc t"), hst[i][:],
                        rd=["hst%d_%d" % (i, a) for a in range(4)], wr=[("hn2Ts", qi)])

            load_xo(0)
            o_mm(0)
            o_norm(0)
            for qi in range(NQT):
                if qi + 1 < NQT:
                    o_mm(qi + 1)
                o_tr(qi)
                if qi + 1 < NQT:
                    o_norm(qi + 1)
            sch.barrier()

        pm.close()
        with ExitStack() as p4:
            def s4(name, shape, dt):
                return sb(name, shape, dt, p4)
            gT = s4("gT", [128, NJ, 1024], BF16)
            halo_st = s4("halo_st", [128, NJ, 2, 2], F32)
            cw = s4("cw", [128, 3, 88], F32)
            cb = s4("cb", [128, 88], F32)
            sch.dma(SP, cw[:], cw_d, wr=["cw"])
            sch.dma(SP, cb[:], cb_d, wr=["cb"])
            w_up_v = w_up.rearrange("(dc p) f -> p dc f", p=128)
            w_down_v = w_down.rearrange("(j p) d -> p j d", p=128)
            wub = [s4("wub%d" % i, [128, 2, 16, 128], BF16) for i in range(2)]
            wdb = [s4("wdb%d" % i, [128, NJ, 128], BF16) for i in range(2)]

            def load_wu(j):
                i = j % 2
                for gv in range(2):
                    c0 = gv * DFF + j * 128
                    sch.dma(POOL, wub[i][:, gv, :, :], w_up_v[:, :, c0:c0 + 128], wr=["wub%d_%d" % (i, gv)])

            def load_wd(m):
                i = m % 2
                for a in range(4):
                    sch.dma(POOL, wdb[i][:, 11 * a:11 * a + 11, :],
                            w_down_v[:, 11 * a:11 * a + 11, m * 128:(m + 1) * 128], wr=["wdb%d_%d" % (i, a)])
            load_wu(0)
            out_v = out_d.rearrange("(t p) c -> p t c", p=128)
            for Gf in range(2):
                with ExitStack() as pu:
                    def su(name, shape, dt):
                        return sb(name, shape, dt, pu)
                    hn2T = su("hn2T", [128, 16, 1024], BF16)
                    hn2Th = su("hn2Th", [128, 16, 2], BF16)
                    c0 = 128 + 1024 * Gf
                    for cch in range(2):
                        for a in range(4):
                            sch.dma(SP if a % 2 == 0 else POOL,
                                    hn2T[:, 4 * a:4 * a + 4, cch * 512:(cch + 1) * 512],
                                    hn2T_s[4 * a:4 * a + 4, :, c0 + cch * 512:c0 + (cch + 1) * 512].rearrange(
                                        "dc p t -> p dc t"),
                                    wr=["hn2T_%d_%d" % (a, cch)])
                    if Gf == 0:
                        sch.dma(POOL, hn2Th[:], hn2T_s[:, :, 126:128].rearrange("dc p t -> p dc t"), wr=["hn2Th"])
                    ugb = [su("ugb%d" % i, [128, 1026], F32) for i in range(2)]
                    uvb = [su("uvb%d" % i, [128, 1026], F32) for i in range(2)]
                    ygb = [su("ygb%d" % i, [128, 512], F32) for i in range(2)]
                    yvb = [su("yvb%d" % i, [128, 512], F32) for i in range(2)]
                    sgb = [su("sgb%d" % i, [128, 512], F32) for i in range(2)]

                    ne = 0
                    for j in range(NJ):
                        if j + 1 < NJ:
                            load_wu(j + 1)
                        else:
                            load_wd(0)
                        wi = j % 2
                        ug, uv = ugb[wi], uvb[wi]
                        ugk, uvk = "ug%d" % wi, "uv%d" % wi
                        wk = ["wub%d_0" % wi, "wub%d_1" % wi]
                        def emit_halo(ug=ug, uv=uv, ugk=ugk, uvk=uvk, wk=wk, wi=wi, j=j):
                          if Gf == 0:
                            fns = []
                            for gv in range(2):
                                for dc in range(16):
                                    fns.append(lambda gv=gv, dc=dc, wi=wi: nc.tensor.matmul(
                                        ps[7][:, gv * 2:gv * 2 + 2], wub[wi][:, gv, dc, :], hn2Th[:, dc, :],
                                        start=(dc == 0), stop=(dc == 15)))
                            sch.ops(PE, fns, rd=wk + ["hn2Th"], wr=["ps7"])
                            sch.op(DVE, lambda ug=ug: nc.vector.tensor_scalar(
                                out=ug[:, 0:2], in0=ps[7][:, 0:2], scalar1=flag[:], scalar2=None, op0=ALU.mult),
                                rd=["ps7", "flag"], wr=[ugk + "h"])
                            sch.op(DVE, lambda uv=uv: nc.vector.tensor_scalar(
                                out=uv[:, 0:2], in0=ps[7][:, 2:4], scalar1=flag[:], scalar2=None, op0=ALU.mult),
                                rd=["ps7", "flag"], wr=[uvk + "h"])
                          else:
                            sch.op(DVE, lambda ug=ug, j=j: nc.vector.tensor_copy(out=ug[:, 0:2],
                                                                              in_=halo_st[:, j, 0, :]),
                                   rd=["halo_st"], wr=[ugk + "h"])
                            sch.op(DVE, lambda uv=uv, j=j: nc.vector.tensor_copy(out=uv[:, 0:2],
                                                                              in_=halo_st[:, j, 1, :]),
                                   rd=["halo_st"], wr=[uvk + "h"])
                        for c in range(2):
                            e2 = ne % 2
                            ne += 1
                            pg, pv = 0 + e2, 2 + e2
                            for gv, pbank in ((0, pg), (1, pv)):
                                fns = [lambda gv=gv, dc=dc, pbank=pbank, c=c, wi=wi: nc.tensor.matmul(
                                    ps[pbank][:], wub[wi][:, gv, dc, :], hn2T[:, dc, c * 512:(c + 1) * 512],
                                    start=(dc == 0), stop=(dc == 15)) for dc in range(16)]
                                sch.ops(PE, fns, rd=wk + ["hn2T_%d_%d" % (a, c) for a in range(4)], wr=["ps%d" % pbank])
                            if c == 0:
                                emit_halo()
                            lo = 2 + 512 * c
                            sch.op(ACT, lambda ug=ug, pg=pg, lo=lo: nc.scalar.activation(
                                out=ug[:, lo:lo + 512], in_=ps[pg][:], func=AF.Copy),
                                rd=["ps%d" % pg], wr=[ugk + "c%d" % c])
                            sch.op(ACT, lambda uv=uv, pv=pv, lo=lo: nc.scalar.activation(
                                out=uv[:, lo:lo + 512], in_=ps[pv][:], func=AF.Copy),
                                rd=["ps%d" % pv], wr=[uvk + "c%d" % c])
                            for (u, uk, y, yk, jj) in ((ug, ugk, ygb[e2], "yg%d" % e2, j),
                                                       (uv, uvk, yvb[e2], "yv%d" % e2, NJ + j)):
                                urd = [uk + "h", uk + "c0", uk + "c1"] if c == 1 else [uk + "h", uk + "c0"]
                                sch.op(DVE, lambda u=u, y=y, jj=jj, lo=lo: nc.vector.tensor_scalar(
                                    out=y[:], in0=u[:, lo:lo + 512], scalar1=cw[:, 2, jj:jj + 1],
                                    scalar2=cb[:, jj:jj + 1], op0=ALU.mult, op1=ALU.add),
                                    rd=urd + ["cw", "cb"], wr=[yk])
                                sch.op(DVE, lambda u=u, y=y, jj=jj, lo=lo: nc.vector.scalar_tensor_tensor(
                                    out=y[:], in0=u[:, lo - 1:lo + 511], scalar=cw[:, 1, jj:jj + 1], in1=y[:],
                                    op0=ALU.mult, op1=ALU.add), rd=urd + ["cw", yk], wr=[yk])
                                sch.op(DVE, lambda u=u, y=y, jj=jj, lo=lo: nc.vector.scalar_tensor_tensor(
                                    out=y[:], in0=u[:, lo - 2:lo + 510], scalar=cw[:, 0, jj:jj + 1], in1=y[:],
                                    op0=ALU.mult, op1=ALU.add), rd=urd + ["cw", yk], wr=[yk])
                            sch.op(ACT, lambda e2=e2: nc.scalar.activation(out=sgb[e2][:], in_=ygb[e2][:],
                                                                            func=AF.Silu),
                                   rd=["yg%d" % e2], wr=["sg%d" % e2])
                            sch.op(DVE, lambda e2=e2, j=j, c=c: nc.vector.tensor_tensor(
                                out=gT[:, j, c * 512:(c + 1) * 512], in0=sgb[e2][:], in1=yvb[e2][:], op=ALU.mult),
                                rd=["sg%d" % e2, "yv%d" % e2], wr=["gT"])
                        if Gf == 0:
                            sch.op(DVE, lambda ug=ug, j=j: nc.vector.tensor_copy(out=halo_st[:, j, 0, :],
                                                                              in_=ug[:, 1024:1026]),
                                   rd=[ugk + "c1"], wr=["halo_st"])
                            sch.op(DVE, lambda uv=uv, j=j: nc.vector.tensor_copy(out=halo_st[:, j, 1, :],
                                                                              in_=uv[:, 1024:1026]),
                                   rd=[uvk + "c1"], wr=["halo_st"])
                    sch.barrier(keep=["wdb0_%d" % a for a in range(4)])
                with ExitStack() as pd:
                    def sd(name, shape, dt):
                        return sb(name, shape, dt, pd)
                    x1g = [sd("x1g%d" % i, [128, 8, 512], F32) for i in range(2)]
                    obuf = [sd("obuf%d" % i, [128, 8, 512], F32) for i in range(2)]
                    yT = [sd("yT%d" % i, [128, 512], F32) for i in range(2)]

                    its = [(mg, mm, c) for mg in range(4) for mm in range(4) for c in range(2)]

                    def d_mm(n):
                        mg, mm, c = its[n]
                        m = 4 * mg + mm
                        if c == 0 and m + 1 < 16:
                            load_wd(m + 1)
                        if c == 0 and m == 15 and Gf == 0:
                            load_wu(0)
                        if mm == 0 and c == 0:
                            gi = mg % 2
                            qa = 1 + 8 * Gf
                            sch.dma(POOL, x1g[gi][:],
                                    x1_s[qa:qa + 8, :, mg * 512:(mg + 1) * 512].rearrange("t p c -> p t c"),
                                    wr=["x1g%d" % gi])
                        wi = m % 2
                        wk = ["wdb%d_%d" % (wi, a) for a in range(4)]
                        pa = n % 2
                        fns = [lambda j=j: nc.tensor.matmul(
                            ps[pa][:], wdb[wi][:, j, :], gT[:, j, c * 512:(c + 1) * 512],
                            start=(j == 0), stop=(j == NJ - 1)) for j in range(NJ)]
                        sch.ops(PE, fns, rd=wk + ["gT"], wr=["ps%d" % pa])

                    def d_post(n):
                        mg, mm, c = its[n]
                        gi = mg % 2
                        e2 = n % 2
                        pa, pt = e2, 2 + e2
                        sch.op(ACT, lambda: nc.scalar.activation(out=yT[e2][:], in_=ps[pa][:], func=AF.Copy),
                               rd=["ps%d" % pa], wr=["yT%d" % e2])
                        if n + 1 < len(its):
                            d_mm(n + 1)
                        fns = [lambda k=k: nc.tensor.transpose(
                            ps[pt][:, k * 128:(k + 1) * 128], yT[e2][:, k * 128:(k + 1) * 128], ident_f[:])
                            for k in range(4)]
                        sch.ops(PE, fns, rd=["yT%d" % e2, "ident_f"], wr=["ps%d" % pt])
                        sch.op(DVE, lambda: nc.vector.tensor_tensor(
                            out=obuf[gi][:, 4 * c:4 * c + 4, mm * 128:(mm + 1) * 128],
                            in0=ps[pt][:].rearrange("p (a b) -> p a b", a=4),
                            in1=x1g[gi][:, 4 * c:4 * c + 4, mm * 128:(mm + 1) * 128], op=ALU.add),
                            rd=["ps%d" % pt, "x1g%d" % gi], wr=["obuf%d" % gi])
                        if mm == 3 and c == 1:
                            sch.dma(SP, out_v[:, 8 * Gf:8 * Gf + 8, mg * 512:(mg + 1) * 512], obuf[gi][:],
                                    rd=["obuf%d" % gi], wr=[("out", Gf, mg)])

                    d_mm(0)
                    for n in range(len(its)):
                        d_post(n)
                    sch.barrier(keep=["wub0_0", "wub0_1"])
        sch.barrier()
    return nc


def _rel_bucket_np(n):
    n = np.maximum(n, 0)
    max_exact = 16
    nf = np.maximum(n, max_exact).astype(np.float32)
    large = max_exact + (np.log(nf / np.float32(max_exact)) / np.float32(math.log(1024 / max_exact))
                         * np.float32(32 - max_exact)).astype(np.int32)
    large = np.minimum(large, 31)
    return np.where(n < max_exact, n, large)


def _static_consts():
    c = {}
    c["ident"] = np.eye(128, dtype=np.float32)
    es = np.zeros((128, 16, 128), np.float32)
    for n in range(16):
        es[n, n, :] = 1.0
    c["esel"] = es
    k = np.arange(128)[:, None]
    q = np.arange(512)[None, :]
    dist = np.stack([(-384 + 128 * j) + q - k for j in range(NBT)], 0)
    c["dist"] = dist
    c["bucket"] = _rel_bucket_np(dist)
    tp = np.arange(128)[:, None]
    t = np.arange(128)[None, :]
    band = np.zeros((3, 4, 128, 128), np.float32)
    for g, w in enumerate((2, 4, 8, 16)):
        band[0, g] = np.where(tp >= t + 129 - w, 1.0 / w, 0.0)
        incl = (tp <= t) & (tp > t - w)
        band[1, g] = np.where(incl, 1.0 / w, 0.0) - (tp == t)
        cntf = np.minimum(t + 1, w).astype(np.float32)
        band[2, g] = np.where(incl, 1.0 / cntf, 0.0) - (tp == t)
    c["band"] = band
    no = np.ones((NQT, 16), np.float32)
    for qi in range(NQT):
        no[qi, (15 + qi) // 2] = 0.0
    c["notown"] = np.ascontiguousarray(np.broadcast_to(no[None], (128, NQT, 16)))
    return c


def _prep_inputs(inp):
    f32 = np.float32
    x = np.asarray(inp["x"], f32)
    cst = _static_consts()
    rel_bias = np.asarray(inp["rel_bias"], f32)
    bt = rel_bias[:, cst["bucket"]]
    bt = np.where(cst["dist"][None] >= 0, bt, f32(-BIG)).astype(f32)
    biasT = np.ascontiguousarray(bt.transpose(0, 2, 1, 3))
    rep = lambda v: np.ascontiguousarray(np.broadcast_to(np.asarray(v, f32).reshape(1, -1), (128, v.size)))
    common = {
        "w_in": np.ascontiguousarray(inp["w_in"][0], f32),
        "w_out": np.ascontiguousarray(inp["w_out"][0], f32),
        "w_up": np.ascontiguousarray(inp["w_up"][0], f32),
        "w_down": np.ascontiguousarray(inp["w_down"][0], f32),
        "pool_w": np.ascontiguousarray(inp["pool_w"][0], f32),
        "g1bc": rep(np.asarray(inp["attn_norm_g"][0])),
        "g2bc": rep(np.asarray(inp["ffn_norm_g"][0])),
        "gqk": np.ascontiguousarray(np.stack([np.asarray(inp["q_norm_g"][0], f32),
                                              np.asarray(inp["k_norm_g"][0], f32)], 1)),
        "pscale": np.ascontiguousarray(np.asarray(inp["pool_scale"][0], f32).reshape(8, 128).T),
        "cw": np.ascontiguousarray(np.asarray(inp["conv_w"][0], f32).reshape(3, 88, 128).transpose(2, 0, 1)),
        "cb": np.ascontiguousarray(np.asarray(inp["conv_b"][0], f32).reshape(88, 128).T),
        "bias31": rep(rel_bias[:, 31]),
        "biasT": biasT,
        "notown": cst["notown"],
        "ident": cst["ident"],
        "esel": cst["esel"],
    }
    maps = []
    for core in range(8):
        b, r = core // 2, core % 2
        m = dict(common)
        if r == 1:
            m["xs"] = np.ascontiguousarray(x[b])
        else:
            m["xs"] = np.ascontiguousarray(np.concatenate([np.zeros((2048, D), f32), x[b, :2048]], 0))
        band = cst["band"].copy()
        if r == 1:
            band[2] = band[1]
        m["band"] = np.ascontiguousarray(band.transpose(2, 0, 1, 3))
        first = 0 if r == 1 else 8
        gmk = np.full((NQT, 16), -1e30, f32)
        for qi in range(NQT):
            for n in range(16):
                if first <= n < (15 + qi) // 2:
                    gmk[qi, n] = 0.0
        m["gmask"] = np.ascontiguousarray(np.broadcast_to(gmk[None], (128, NQT, 16)))
        m["flag"] = np.full((128, 1), float(r), f32)
        maps.append(m)
    return maps


_NC_CACHE = {}


def kernel(**inputs):
    maps = _prep_inputs(inputs)
    if "nc" not in _NC_CACHE:
        _NC_CACHE["nc"] = build_program(DEBUG)
    nc = _NC_CACHE["nc"]
    res = run_bass_kernel_spmd(nc, maps, core_ids=list(range(8)))
    out = np.empty((4, S, D), np.float32)
    for core in range(8):
        b, r = core // 2, core % 2
        out[b, r * 2048:(r + 1) * 2048] = res.results[core]["out"]
    if DEBUG:
        kernel.last = res
    return out
```

```python
import math
import numpy as np
import concourse.bass as bass
import concourse.mybir as mybir
from concourse.bass_utils import run_bass_kernel_spmd

F32 = mybir.dt.float32
BF16 = mybir.dt.bfloat16
AF = mybir.ActivationFunctionType
ALU = mybir.AluOpType
AX = mybir.AxisListType

D = 2048
S = 4096
HD = 128
NH = 8
INW = 4096
DFF = 5632
NJ = DFF // 128
EPS = 1e-6
BIG = 32768.0
NQT = 17
QW = NQT * 128
NBT = 11
DEBUG = False


class Eng:
    def __init__(self, name, h, sem, is_pe=False):
        self.name, self.h, self.sem, self.n, self.is_pe = name, h, sem, 0, is_pe
        self.seen = {}
        self.dsems = []
        self.dvals = []
        self.dnext = 0


class Sched:
    def __init__(self, nc):
        self.nc = nc
        self.w = {}
        self.r = {}
        self.engs = []

    def add_engine(self, e):
        self.engs.append(e)

    def _wait(self, q, t):
        if t[0] == 'c':
            _, e, n = t
            if e is q and q.is_pe:
                return
            key = e.name
            if q.seen.get(key, 0) >= n:
                return
            q.h.wait_ge(e.sem, n)
            q.seen[key] = n
        else:
            _, sem, val, key = t
            if q.seen.get(key, 0) >= val:
                return
            q.h.wait_ge(sem, val)
            q.seen[key] = val

    def _deps(self, q, rd, wr):
        for b in rd:
            t = self.w.get(b)
            if t is not None:
                self._wait(q, t)
        for b in wr:
            t = self.w.get(b)
            if t is not None:
                self._wait(q, t)
            for t2 in self.r.get(b, {}).values():
                self._wait(q, t2)

    def _record(self, tk, rk, rd, wr):
        for b in rd:
            self.r.setdefault(b, {})[rk] = tk
        for b in wr:
            self.w[b] = tk
            self.r[b] = {}

    def op(self, q, fn, rd=(), wr=()):
        self._deps(q, rd, wr)
        ins = fn()
        ins.then_inc(q.sem, 1)
        q.n += 1
        tk = ('c', q, q.n)
        self._record(tk, q.name, rd, wr)
        return tk

    def ops(self, q, fns, rd=(), wr=()):
        self._deps(q, rd, wr)
        ins = None
        for fn in fns:
            ins = fn()
        ins.then_inc(q.sem, 1)
        q.n += 1
        tk = ('c', q, q.n)
        self._record(tk, q.name, rd, wr)
        return tk

    def dma(self, q, out, in_, rd=(), wr=()):
        self._deps(q, rd, wr)
        i = q.dnext % len(q.dsems)
        q.dnext += 1
        sem = q.dsems[i]
        key = q.name + "_d%d" % i
        if q.dvals[i] > 0 and q.seen.get(key, 0) < q.dvals[i]:
            q.h.wait_ge(sem, q.dvals[i])
            q.seen[key] = q.dvals[i]
        q.h.dma_start(out=out, in_=in_).then_inc(sem, 16)
        q.dvals[i] += 16
        tk = ('d', sem, q.dvals[i], key)
        self._record(tk, key, rd, wr)
        return tk

    def barrier(self, keep=()):
        kept = {k: self.w[k] for k in keep if k in self.w}
        skip = {}
        for t in kept.values():
            if t[0] == 'd':
                skip[t[3]] = t[2]
        for q in self.engs:
            for e in self.engs:
                if e is q or e.n == 0:
                    continue
                if q.seen.get(e.name, 0) < e.n:
                    q.h.wait_ge(e.sem, e.n)
                    q.seen[e.name] = e.n
            for e in self.engs:
                for i, sem in enumerate(e.dsems):
                    key = e.name + "_d%d" % i
                    val = e.dvals[i]
                    if key in skip and skip[key] == val:
                        val -= 16
                    if val > 0 and q.seen.get(key, 0) < val:
                        q.h.wait_ge(sem, val)
                        q.seen[key] = val
        self.w = dict(kept)
        self.r = {}


def build_program(debug=False):
    nc = bass.Bass("TRN2", target_bir_lowering=False)

    def din(name, shape, dt=F32):
        return nc.dram_tensor(name, list(shape), dt, kind="ExternalInput").ap()

    xs = din("xs", [S, D])
    w_in = din("w_in", [D, INW])
    w_out = din("w_out", [D, D])
    w_up = din("w_up", [D, 2 * DFF])
    w_down = din("w_down", [DFF, D])
    pool_w = din("pool_w", [4, 256, 256])
    g1bc_d = din("g1bc", [128, D])
    g2bc_d = din("g2bc", [128, D])
    gqk_d = din("gqk", [128, 2])
    pscale_d = din("pscale", [128, 8])
    cw_d = din("cw", [128, 3, 88])
    cb_d = din("cb", [128, 88])
    bias31_d = din("bias31", [128, 8])
    biasT_d = din("biasT", [NH, 128, NBT, 512])
    band_d = din("band", [128, 3, 4, 128])
    gmask_d = din("gmask", [128, NQT, 16])
    notown_d = din("notown", [128, NQT, 16])
    flag_d = din("flag", [128, 1])
    ident_d = din("ident", [128, 128])
    esel_d = din("esel", [128, 16, 128])

    out_d = nc.dram_tensor("out", [2048, D], F32, kind="ExternalOutput").ap()
    skind = "ExternalOutput" if debug else "Internal"
    kT_s = nc.dram_tensor("kT_s", [NH, 128, S], BF16, kind=skind).ap()
    qT_s = nc.dram_tensor("qT_s", [NH, 128, S], BF16, kind=skind).ap()
    v_s = nc.dram_tensor("v_s", [32, 128, 1024], BF16, kind=skind).ap()
    x1_s = nc.dram_tensor("x1_s", [NQT, 128, D], F32, kind=skind).ap()
    mT_s = nc.dram_tensor("mT_s", [8, 128, QW], BF16, kind=skind).ap()
    hn2T_s = nc.dram_tensor("hn2T_s", [16, 128, QW], BF16, kind=skind).ap()
    if debug:
        mix_dbg = nc.dram_tensor("mix_dbg", [128, 16, QW], BF16, kind="ExternalOutput").ap()

    from contextlib import ExitStack
    top = ExitStack()
    with top:
        uid = [0]

        def sb(name, shape, dt, stack=top):
            uid[0] += 1
            return stack.enter_context(nc.sbuf_tensor("sb%d_%s" % (uid[0], name), list(shape), dt))

        def sem(name):
            return top.enter_context(nc.semaphore(name))

        sch = Sched(nc)
        PE = Eng("pe", nc.tensor, sem("s_pe"), is_pe=True)
        ACT = Eng("act", nc.scalar, sem("s_act"))
        DVE = Eng("dve", nc.vector, sem("s_dve"))
        POOL = Eng("pool", nc.gpsimd, sem("s_pool"))
        SP = Eng("sp", nc.sync, sem("s_sp"))
        for e in (PE, ACT, DVE, POOL, SP):
            sch.add_engine(e)
        for e, nd in ((SP, 16), (POOL, 12)):
            for i in range(nd):
                e.dsems.append(sem("d_%s%d" % (e.name, i)))
                e.dvals.append(0)

        ps = [top.enter_context(nc.psum_tensor("ps%d" % i, [128, 512], F32)) for i in range(8)]
        psb = [p.bitcast(BF16) for p in ps]

        ident_b = sb("ident_b", [128, 128], BF16)
        ident_f = sb("ident_f", [128, 128], F32)
        ones_b = sb("ones_b", [128, 128], BF16)
        gqk = sb("gqk", [128, 2], F32)
        gqs = sb("gqs", [128, 1], F32)
        pscale = sb("pscale", [128, 8], F32)
        bias31 = sb("bias31", [128, 8], F32)
        flag = sb("flag", [128, 1], F32)
        epsc = sb("epsc", [128, 1], F32)
        zcol = sb("zcol", [128, 1], F32)
        esel = sb("esel", [128, 16, 128], BF16)
        gmask = sb("gmask", [128, NQT, 16], F32)
        notown = sb("notown", [128, NQT, 16], F32)

        sch.dma(POOL, ident_b[:], ident_d, wr=["ident_b"])
        sch.op(DVE, lambda: nc.vector.memset(ones_b[:], 1.0), wr=["ones_b"])
        sch.op(DVE, lambda: nc.vector.memset(epsc[:], EPS), wr=["epsc"])
        sch.op(DVE, lambda: nc.vector.memset(zcol[:], 0.0), wr=["zcol"])

        def late_consts():
            sch.dma(SP, ident_f[:], ident_d, wr=["ident_f"])
            sch.dma(SP, gqk[:], gqk_d, wr=["gqk"])
            sch.dma(SP, pscale[:], pscale_d, wr=["pscale"])
            sch.dma(SP, bias31[:], bias31_d, wr=["bias31"])
            sch.dma(SP, flag[:], flag_d, wr=["flag"])
            sch.dma(POOL, esel[:], esel_d, wr=["esel"])
            sch.dma(SP, gmask[:], gmask_d, wr=["gmask"])
            sch.dma(SP, notown[:], notown_d, wr=["notown"])
            sch.op(DVE, lambda: nc.vector.tensor_scalar(out=gqs[:], in0=gqk[:, 0:1], scalar1=float(HD ** -0.5),
                                                        scalar2=None, op0=ALU.mult),
                   rd=["gqk"], wr=["gqs"])

        pm = ExitStack()

        def rmsnorm_tile(xt, xkey, gbc, hn, hnkey, ss, rt1, rstd, idx):
            k = "n%d" % (idx % 2)
            sch.op(ACT, lambda: nc.scalar.activation(out=hn, in_=xt, func=AF.Square, accum_out=ss[:]),
                   rd=[xkey], wr=[hnkey, "ss" + k])
            sch.op(ACT, lambda: nc.scalar.activation(out=rt1[:], in_=ss[:], func=AF.Ln, bias=epsc[:],
                                                     scale=1.0 / D),
                   rd=["ss" + k, "epsc"], wr=["rt1" + k])
            sch.op(ACT, lambda: nc.scalar.activation(out=rstd[:], in_=rt1[:], func=AF.Exp, bias=zcol[:],
                                                     scale=-0.5),
                   rd=["rt1" + k, "zcol"], wr=["rstd" + k])
            sch.op(DVE, lambda: nc.vector.scalar_tensor_tensor(out=hn, in0=xt, scalar=rstd[:], in1=gbc,
                                                               op0=ALU.mult, op1=ALU.mult),
                   rd=[xkey, "rstd" + k, "gbc"], wr=[hnkey])

        with ExitStack() as p1:
            def s1(name, shape, dt):
                return sb(name, shape, dt, p1)
            g1bc = s1("g1bc", [128, D], F32)
            sch.dma(SP, g1bc[:], g1bc_d, wr=["gbc"])
            band_b = s1("band_b", [128, 3, 4, 128], BF16)
            poolw_b = s1("poolw_b", [128, 4, 2, 256], BF16)
            xbuf = [s1("xbuf%d" % i, [128, D], F32) for i in range(4)]
            hnb = [s1("hnb%d" % i, [128, D], BF16) for i in range(2)]
            ssb = [s1("ss%d" % i, [128, 1], F32) for i in range(2)]
            rt1b = [s1("rt1%d" % i, [128, 1], F32) for i in range(2)]
            rstdb = [s1("rstd%d" % i, [128, 1], F32) for i in range(2)]
            hnTb = [s1("hnT%d" % i, [128, 16, 1024], BF16) for i in range(2)]
            mst = [s1("mst%d" % i, [128, 8, 128], BF16) for i in range(2)]
            wsl = [s1("wsl%d" % i, [128, 16, 512], BF16) for i in range(2)]
            pall = s1("pall", [128, 9, 1024], BF16)
            sqb = [s1("sqb%d" % i, [128, 512], BF16) for i in range(2)]
            rtb = [s1("rtb%d" % i, [128, 512], F32) for i in range(2)]
            rrb = [s1("rrb%d" % i, [128, 512], F32) for i in range(2)]
            stg = [s1("stg%d" % i, [128, 512], BF16) for i in range(3)]
            vst = [s1("vst%d" % i, [128, 512], BF16) for i in range(3)]
            mixT = [s1("mixT%d" % i, [128, 8, 128], BF16) for i in range(2)]

            sch.dma(POOL, band_b[:], band_d, wr=["band"])
            sch.dma(POOL, poolw_b[:], pool_w.rearrange("g (kh p) c -> p g kh c", p=128), wr=["poolw"])
            sch.op(DVE, lambda: nc.vector.memset(pall[:, 0, :], 0.0), wr=["pall0"])

            w_in_v = w_in.rearrange("(dc p) c -> p dc c", p=128)
            cnt = {"qk": 0, "v": 0, "slab": 0, "x": 0}

            def load_x(gt):
                if gt >= 32:
                    return
                i = gt % 4
                sch.dma(SP if gt < 8 else POOL, xbuf[i][:], xs[gt * 128:(gt + 1) * 128, :], wr=["xbuf%d" % i])

            def load_slab(s):
                i = cnt["slab"] % 2
                cnt["slab"] += 1
                for a in range(4):
                    sch.dma(POOL, wsl[i][:, 4 * a:4 * a + 4, :], w_in_v[:, 4 * a:4 * a + 4, s * 512:(s + 1) * 512],
                            wr=["wsl%d_%d" % (i, a)])
                return i

            def slab_keys(i):
                return ["wsl%d_%d" % (i, a) for a in range(4)]

            post_q = []

            def flush_post():
                while post_q:
                    norm_post(*post_q.pop(0))

            def norm_pre(G, ti):
                gt = 8 * G + ti
                hi = gt % 2
                if any((8 * g + t) % 2 == hi for (g, t) in post_q):
                    flush_post()
                load_x(gt + 3)
                xi = gt % 4
                rmsnorm_tile(xbuf[xi][:], "xbuf%d" % xi, g1bc[:], hnb[hi][:], "hnb%d" % hi,
                             ssb[hi], rt1b[hi], rstdb[hi], hi)

            def norm_post(G, ti):
                gt = 8 * G + ti
                hi = gt % 2
                hb = G % 2
                for a in range(4):
                    bank = 5 + (a % 2)
                    fns = []
                    for b in range(4):
                        dc = 4 * a + b
                        fns.append(lambda dc=dc, b=b, bank=bank, hi=hi: nc.tensor.transpose(
                            psb[bank][:, b * 128:(b + 1) * 128], hnb[hi][:, dc * 128:(dc + 1) * 128], ident_b[:]))
                    sch.ops(PE, fns, rd=["hnb%d" % hi, "ident_b"], wr=["ps%d" % bank])
                    if a % 2 == 0:
                        sch.op(ACT, lambda a=a, bank=bank, ti=ti, hb=hb: nc.scalar.activation(
                            out=hnTb[hb][:, 4 * a:4 * a + 4, ti * 128:(ti + 1) * 128],
                            in_=psb[bank][:, 0:512].rearrange("p (a b) -> p a b", a=4), func=AF.Copy),
                            rd=["ps%d" % bank], wr=["hnT%d_%d" % (hb, ti // 4)])
                    else:
                        sch.op(DVE, lambda a=a, bank=bank, ti=ti, hb=hb: nc.vector.tensor_copy(
                            out=hnTb[hb][:, 4 * a:4 * a + 4, ti * 128:(ti + 1) * 128],
                            in_=psb[bank][:, 0:512].rearrange("p (a b) -> p a b", a=4)),
                            rd=["ps%d" % bank], wr=["hnT%d_%d" % (hb, ti // 4)])

            for gt in range(3):
                load_x(gt)
            pre_slab = load_slab(2)
            late_consts()
            norm_pre(0, 0)
            for ti in range(8):
                if ti + 1 < 8:
                    norm_pre(0, ti + 1)
                norm_post(0, ti)
            nmt = 0
            for G in range(4):
                slabs = [2, 3, 4, 5] if G == 0 else list(range(8))
                hnT = hnTb[G % 2]
                hk = "hnT%d" % (G % 2)
                tps = 8 // len(slabs)
                for si, s in enumerate(slabs):
                    wi = pre_slab
                    if G + 1 < 4:
                        for ti in range(si * tps, (si + 1) * tps):
                            norm_pre(G + 1, ti)
                    if si + 1 < len(slabs):
                        pre_slab = load_slab(slabs[si + 1])
                    elif G + 1 < 4:
                        pre_slab = load_slab(0)
                    W = wsl[wi]
                    wk = slab_keys(wi)
                    if s < 4:
                        isq = s < 2
                        dst = qT_s if isq else kT_s
                        gcol = gqs[:] if isq else gqk[:, 1:2]
                        its = [(hh, c) for c in range(2) for hh in range(4)
                               if not (isq and G == 1 and c == 0)]

                        halo_q = isq and G == 1
                        nw = 128 if halo_q else 512
                        cofs = 384 if halo_q else 0

                        def qk_a(n, its=its, W=W, wk=wk, nw=nw, cofs=cofs):
                            hh, c = its[n]
                            if c == 1:
                                flush_post()
                            k = cnt["qk"] + n
                            pa, i2 = k % 3, k % 2
                            t0 = c * 512 + cofs
                            fns = [lambda dc=dc: nc.tensor.matmul(
                                ps[pa][:, 0:nw], W[:, dc, hh * 128:(hh + 1) * 128], hnT[:, dc, t0:t0 + nw],
                                start=(dc == 0), stop=(dc == 15)) for dc in range(16)]
                            sch.ops(PE, fns, rd=wk + [hk + "_%d" % c], wr=["ps%d" % pa])
                            sch.op(ACT, lambda: nc.scalar.activation(out=sqb[i2][:, 0:nw], in_=ps[pa][:, 0:nw],
                                                                     func=AF.Square),
                                   rd=["ps%d" % pa], wr=["sqb%d" % i2])

                        def qk_b(n, its=its, s=s, isq=isq, dst=dst, gcol=gcol, nw=nw, cofs=cofs):
                            hh, c = its[n]
                            head = (s % 2) * 4 + hh
                            k = cnt["qk"] + n
                            pa, i2, i3 = k % 3, k % 2, k % 3
                            pb = 3 + i2
                            sch.op(PE, lambda: nc.tensor.matmul(ps[pb][:, 0:nw], ones_b[:], sqb[i2][:, 0:nw],
                                                                start=True, stop=True),
                                   rd=["sqb%d" % i2, "ones_b"], wr=["ps%d" % pb])
                            sch.op(ACT, lambda: nc.scalar.activation(
                                out=rtb[i2][:, 0:nw], in_=ps[pb][:, 0:nw], func=AF.Ln, bias=epsc[:], scale=1.0 / HD),
                                rd=["ps%d" % pb, "epsc"], wr=["rtb%d" % i2])
                            sch.op(ACT, lambda: nc.scalar.activation(
                                out=rrb[i2][:, 0:nw], in_=rtb[i2][:, 0:nw], func=AF.Exp, bias=zcol[:], scale=-0.5),
                                rd=["rtb%d" % i2, "zcol"], wr=["rrb%d" % i2])
                            sch.op(DVE, lambda: nc.vector.scalar_tensor_tensor(
                                out=stg[i3][:, 0:nw], in0=ps[pa][:, 0:nw], scalar=gcol, in1=rrb[i2][:, 0:nw],
                                op0=ALU.mult, op1=ALU.mult),
                                rd=["ps%d" % pa, "rrb%d" % i2, "gqs", "gqk"], wr=["stg%d" % i3])
                            t0 = G * 1024 + c * 512 + cofs
                            sch.dma(SP, dst[head, :, t0:t0 + nw], stg[i3][:, 0:nw], rd=["stg%d" % i3],
                                    wr=[("qs" if isq else "ks", head, G, c)])

                        qk_a(0)
                        for n in range(len(its)):
                            if n + 1 < len(its):
                                qk_a(n + 1)
                            if n == 1:
                                flush_post()
                            qk_b(n)
                        flush_post()
                        cnt["qk"] += len(its)
                    else:
                        isv = s < 6
                        for ti in range(8):
                            gt = 8 * G + ti
                            if (not isv) and G == 1 and ti < 6:
                                continue
                            i2 = cnt["v"] % 2
                            i3 = cnt["v"] % 3
                            cnt["v"] += 1
                            pv = i3
                            if ti >= 4:
                                flush_post()
                            fns = [lambda dc=dc, ti=ti, pv=pv: nc.tensor.matmul(
                                ps[pv][:], hnT[:, dc, ti * 128:(ti + 1) * 128], W[:, dc, :],
                                start=(dc == 0), stop=(dc == 15)) for dc in range(16)]
                            sch.ops(PE, fns, rd=wk + [hk + "_%d" % (ti // 4)], wr=["ps%d" % pv])
                            if ti == 1:
                                flush_post()
                            if isv:
                                sch.op(DVE, lambda pv=pv, i3=i3: nc.vector.tensor_copy(out=vst[i3][:], in_=ps[pv][:]),
                                       rd=["ps%d" % pv], wr=["vst%d" % i3])
                                c0 = (s - 4) * 512
                                sch.dma(SP, v_s[gt, :, c0:c0 + 512], vst[i3][:], rd=["vst%d" % i3],
                                        wr=[("vs", gt, s)])
                            else:
                                c0 = (s - 6) * 512
                                sch.op(DVE, lambda pv=pv, ti=ti, c0=c0: nc.vector.tensor_copy(
                                    out=pall[:, 1 + ti, c0:c0 + 512], in_=ps[pv][:]),
                                    rd=["ps%d" % pv], wr=["pall%d" % (1 + ti)])
                    if G + 1 < 4:
                        for ti in range(si * tps, (si + 1) * tps):
                            post_q.append((G + 1, ti))
                ptiles = [ti for ti in range(8) if 8 * G + ti >= 15]

                def pool_x(ti, G=G):
                    gt = 8 * G + ti
                    kind = 2 if gt == 16 else 1
                    mx = gt % 2
                    for a in range(2):
                        bank = 0 + a
                        fns = []
                        for b in range(4):
                            c8 = 4 * a + b
                            g = c8 // 2
                            fns.append(lambda b=b, c8=c8, g=g: nc.tensor.matmul(
                                ps[bank][:, b * 128:(b + 1) * 128], pall[:, ti, c8 * 128:(c8 + 1) * 128],
                                band_b[:, 0, g, :], start=True, stop=False))
                            fns.append(lambda b=b, c8=c8, g=g: nc.tensor.matmul(
                                ps[bank][:, b * 128:(b + 1) * 128], pall[:, 1 + ti, c8 * 128:(c8 + 1) * 128],
                                band_b[:, kind, g, :], start=False, stop=True))
                        sch.ops(PE, fns, rd=["pall%d" % ti, "pall%d" % (1 + ti), "band"], wr=["ps%d" % bank])
                        sch.op(DVE, lambda a=a: nc.vector.tensor_copy(
                            out=mixT[mx][:, 4 * a:4 * a + 4, :], in_=ps[bank][:].rearrange("p (a b) -> p a b", a=4)),
                            rd=["ps%d" % bank], wr=["mixT%d_%d" % (mx, a)])

                def pool_y(ti, G=G):
                    gt = 8 * G + ti
                    qc = (gt - 15) * 128
                    mx = gt % 2
                    mi = gt % 2
                    for a in range(2):
                        bank = 2 + a
                        fns = []
                        for b in range(4):
                            c8o = 4 * a + b
                            g = c8o // 2
                            half = c8o % 2
                            for kh in range(2):
                                fns.append(lambda b=b, g=g, half=half, kh=kh: nc.tensor.matmul(
                                    ps[bank][:, b * 128:(b + 1) * 128],
                                    poolw_b[:, g, kh, half * 128:(half + 1) * 128], mixT[mx][:, 2 * g + kh, :],
                                    start=(kh == 0), stop=(kh == 1)))
                        sch.ops(PE, fns, rd=["mixT%d_0" % mx, "mixT%d_1" % mx, "poolw"], wr=["ps%d" % bank])
                        for b in range(4):
                            c8o = 4 * a + b
                            if a == 0:
                                sch.op(ACT, lambda b=b, c8o=c8o: nc.scalar.activation(
                                    out=mst[mi][:, c8o, :], in_=ps[bank][:, b * 128:(b + 1) * 128],
                                    func=AF.Identity, bias=zcol[:], scale=pscale[:, c8o:c8o + 1]),
                                    rd=["ps%d" % bank, "pscale", "zcol"], wr=["mst%d_%d" % (mi, a)])
                            else:
                                sch.op(DVE, lambda b=b, c8o=c8o: nc.vector.tensor_scalar(
                                    out=mst[mi][:, c8o, :], in0=ps[bank][:, b * 128:(b + 1) * 128],
                                    scalar1=pscale[:, c8o:c8o + 1], scalar2=None, op0=ALU.mult),
                                    rd=["ps%d" % bank, "pscale"], wr=["mst%d_%d" % (mi, a)])
                    sch.dma(SP, mT_s[:, :, qc:qc + 128].rearrange("c p t -> p c t"), mst[mi][:],
                            rd=["mst%d_0" % mi, "mst%d_1" % mi], wr=[("mTs", gt)])

                flush_post()
                if ptiles:
                    pool_x(ptiles[0])
                    for k, ti in enumerate(ptiles):
                        if k + 1 < len(ptiles):
                            pool_x(ptiles[k + 1])
                        pool_y(ti)
                if G <= 1:
                    flush_post()
                if G >= 1:
                    sch.op(DVE, lambda: nc.vector.tensor_copy(out=pall[:, 0, :], in_=pall[:, 8, :]),
                           rd=["pall8"], wr=["pall0"])
            sch.barrier()

        aT = sb("aT", [128, 8, QW], BF16, pm)
        wo_sb = sb("wo_sb", [128, 16, D], BF16, pm)
        w_out_v = w_out.rearrange("(cc p) d -> p cc d", p=128)
        with ExitStack() as p2:
            def s2(name, shape, dt):
                return sb(name, shape, dt, p2)
            kTh = [s2("kTh%d" % i, [128, S], BF16) for i in range(2)]
            vh = [s2("vh%d" % i, [128, 32, 128], BF16) for i in range(2)]
            qTh = [s2("qTh%d" % i, [128, QW], BF16) for i in range(2)]
            bTh = [s2("bTh%d" % i, [128, NBT, 512], BF16) for i in range(2)]
            MT = [s2("MT%d" % i, [128, QW], BF16) for i in range(2)]
            PT = [s2("PT%d" % i, [128, 512], BF16) for i in range(4)]
            rec = [s2("rec%d" % i, [128, 512], F32) for i in range(2)]
            lnd = [s2("lnd%d" % i, [128, 512], F32) for i in range(2)]
            accP = [s2("accP%d" % i, [128, 512], F32) for i in range(2)]
            accPb = [s2("accPb%d" % i, [128, 512], BF16) for i in range(2)]
            kmf = s2("kmf", [128, 16], F32)
            kmb = [s2("kmb%d" % i, [128, 16], BF16) for i in range(2)]
            gmA = [s2("gmA%d" % i, [128, NQT, 16], F32) for i in range(2)]
            top8 = [s2("top8%d" % i, [128, NQT, 8], F32) for i in range(2)]
            thrA = [s2("thrA%d" % i, [128, NQT], F32) for i in range(2)]
            nselA = [s2("nselA%d" % i, [128, NQT, 16], F32) for i in range(2)]
            maddA = [s2("maddA%d" % i, [128, NQT, 16], F32) for i in range(2)]

            def load_head(h):
                i = h % 2
                sch.dma(SP, kTh[i][:], kT_s[h], wr=["kTh%d" % i])
                sch.dma(SP, qTh[i][:], qT_s[h, :, 1920:S], wr=["qTh%d" % i])
                for a in range(4):
                    sch.dma(POOL, vh[i][:, 8 * a:8 * a + 8, :],
                            v_s[8 * a:8 * a + 8, :, h * 128:(h + 1) * 128].rearrange("t p c -> p t c"),
                            wr=["vh%d_%d" % (i, a)])
                for a in range(NBT):
                    sch.dma(POOL, bTh[i][:, a, :], biasT_d[h, :, a, :], wr=["bTh%d_%d" % (i, a)])

            def gate_stage1a(h, part):
                i = h % 2
                sch.op(DVE, lambda: nc.vector.tensor_reduce(
                    out=kmf[:, 4 * part:4 * part + 4],
                    in_=kTh[i][:, 1024 * part:1024 * (part + 1)].rearrange("p (n k) -> p n k", n=4),
                    axis=AX.X, op=ALU.add), rd=["kTh%d" % i], wr=["kmf%d" % part])
                if part == 3:
                    sch.op(DVE, lambda: nc.vector.tensor_scalar(out=kmb[i][:], in0=kmf[:], scalar1=1.0 / 256.0,
                                                                scalar2=None, op0=ALU.mult),
                           rd=["kmf%d" % p for p in range(4)], wr=["kmb%d" % i])

            def gate_stage1b(h, piece=None):
                i = h % 2
                pcs = range(5) if piece is None else [piece]
                for pc in pcs:
                    if pc == 0:
                        fns = [lambda qi=qi: nc.tensor.matmul(
                            ps[7][:, qi * 16:(qi + 1) * 16], qTh[i][:, qi * 128:(qi + 1) * 128], kmb[i][:],
                            start=True, stop=True) for qi in range(NQT)]
                        sch.ops(PE, fns, rd=["qTh%d" % i, "kmb%d" % i], wr=["ps7"])
                        sch.op(DVE, lambda: nc.vector.tensor_tensor(
                            out=gmA[i][:], in0=ps[7][:, 0:NQT * 16].rearrange("p (a b) -> p a b", a=NQT),
                            in1=gmask[:], op=ALU.add), rd=["ps7", "gmask"], wr=["gmA%d" % i])
                    elif pc in (1, 2, 3):
                        qs = [range(0, 6), range(6, 12), range(12, NQT)][pc - 1]
                        fns = [lambda qi=qi: nc.vector.max(out=top8[i][:, qi, :], in_=gmA[i][:, qi, :]) for qi in qs]
                        sch.ops(DVE, fns, rd=["gmA%d" % i], wr=["top8%d_%d" % (i, pc)])
                    else:
                        sch.op(DVE, lambda: nc.vector.tensor_scalar(
                            out=thrA[i][:], in0=top8[i][:, :, 2], scalar1=-1e29, scalar2=None, op0=ALU.max),
                            rd=["top8%d_%d" % (i, p) for p in (1, 2, 3)], wr=["thrA%d" % i])
                        sch.op(DVE, lambda: nc.vector.tensor_tensor(
                            out=nselA[i][:], in0=gmA[i][:],
                            in1=thrA[i][:].unsqueeze(2).to_broadcast([128, NQT, 16]), op=ALU.is_lt),
                            rd=["gmA%d" % i, "thrA%d" % i], wr=["nselA%d" % i])
                        sch.op(DVE, lambda: nc.vector.scalar_tensor_tensor(
                            out=maddA[i][:], in0=nselA[i][:], scalar=-BIG, in1=notown[:], op0=ALU.mult, op1=ALU.mult),
                            rd=["nselA%d" % i, "notown"], wr=["maddA%d" % i])

            def gate_stage2_pe(h, rnd, bank=5):
                i = h % 2
                qis = list(range(4 * rnd, min(4 * rnd + 4, NQT)))
                fns = [lambda k=k, qi=qi: nc.tensor.transpose(
                    ps[bank][0:16, k * 128:(k + 1) * 128], maddA[i][:, qi, :], ident_f[:])
                    for k, qi in enumerate(qis)]
                sch.ops(PE, fns, rd=["maddA%d" % i, "ident_f"], wr=["ps%d" % bank])

            def gate_stage2_act(h, rnd, bank=5):
                i = h % 2
                qis = list(range(4 * rnd, min(4 * rnd + 4, NQT)))
                n = len(qis)
                q0 = qis[0]
                sch.op(ACT, lambda: nc.scalar.activation(
                    out=MT[i][0:16, q0 * 128:(q0 + n) * 128], in_=ps[bank][0:16, 0:n * 128], func=AF.Copy),
                    rd=["ps%d" % bank], wr=["MT%d" % i])

            accPh = s2("accPh", [128, 512], F32)
            accPbh = s2("accPbh", [128, 512], BF16)
            for i in range(2):
                sch.op(DVE, lambda i=i: nc.vector.memset(MT[i][:], 0.0), wr=["MT%d" % i])
            load_head(0)
            for part in range(4):
                gate_stage1a(0, part)
            gate_stage1b(0)
            g0banks = [5, 3, 4, 6]
            for r in range(4):
                gate_stage2_pe(0, r, g0banks[r])
            for r in range(4):
                gate_stage2_act(0, r, g0banks[r])
            gate_stage2_pe(0, 4)
            gate_stage2_act(0, 4)

            SB = [0, 1, 2, 6]
            PTK = {0: 0, 1: 1, 2: 2, 6: 3}
            U = []
            nreg = 0
            for h in range(NH):
                hu = []
                halo = dict(h=h, halo=True, q0=0, nq=128, Q0=1920, nkt=16, pOap=ps[7][:, 384:512], pOk="ps7h",
                            acc=accPh, accb=accPbh, acck="accPh", accbk="accPbh", ab=0)
                hunits = [dict(ctx=halo, kts=list(range(4 * g, 4 * g + 4))) for g in range(4)]
                for c in range(4):
                    ab = nreg % 2
                    nreg += 1
                    ctx = dict(h=h, halo=False, q0=128 + 512 * c, nq=512, Q0=2048 + 512 * c,
                               nkt=(2048 + 512 * c + 512) // 128, pOap=ps[3 + ab][:, 0:512], pOk="ps%d" % (3 + ab),
                               acc=accP[ab], accb=accPb[ab], acck="accP%d" % ab, accbk="accPb%d" % ab, ab=ab)
                    for kt in range(ctx["nkt"]):
                        hu.append(dict(ctx=ctx, kts=[kt]))
                        if c == 0 and kt % 4 == 3 and kt // 4 < 4:
                            hu.append(hunits[kt // 4])
                for k, u in enumerate(hu):
                    u["hidx"] = k
                U += hu
            NU = len(U)
            cnt2 = {"s": 0}
            later = {}

            def at(n, fn):
                later.setdefault(min(n, NU - 1), []).append(fn)

            def scores(n):
                u = U[n]
                c = u["ctx"]
                i = c["h"] % 2
                nq, q0, Q0 = c["nq"], c["q0"], c["Q0"]
                sbk = SB[cnt2["s"] % 4]
                cnt2["s"] += 1
                u["sbk"] = sbk
                fns = []
                bdeps = []
                for idx, kt in enumerate(u["kts"]):
                    D0 = Q0 - 128 * kt
                    near = D0 <= 896
                    assert u.setdefault("near", near) == near
                    if near:
                        bdeps.append("bTh%d_%d" % (i, (D0 + 384) // 128))
                    cs = slice(idx * nq, (idx + 1) * nq)
                    opnds = [(kTh[i][:, kt * 128:(kt + 1) * 128], qTh[i][:, q0:q0 + nq])]
                    if near:
                        j = (D0 + 384) // 128
                        opnds.append((ident_b[:], bTh[i][:, j, 0:nq]))
                    if kt < c["nkt"] - 2:
                        opnds.append((esel[:, kt // 2, :], MT[i][:, q0:q0 + nq]))
                    for oi, (lh, rh) in enumerate(opnds):
                        fns.append(lambda lh=lh, rh=rh, cs=cs, oi=oi, no=len(opnds): nc.tensor.matmul(
                            ps[sbk][:, cs], lh, rh, start=(oi == 0), stop=(oi == no - 1)))
                sch.ops(PE, fns, rd=["kTh%d" % i, "qTh%d" % i, "MT%d" % i, "esel", "ident_b"] + bdeps,
                        wr=["ps%d" % sbk])

            def finalizeA0(c):
                sch.op(DVE, lambda: nc.vector.tensor_copy(out=c["accb"][:, 0:512], in_=c["acc"][:, 0:512]),
                       rd=[c["acck"]], wr=[c["accbk"]])

            def finalizeA(c):
                nq = c["nq"]
                grp = 512 // nq
                fns = [lambda g=g: nc.tensor.matmul(ps[5][:, 0:nq], ones_b[:], c["accb"][:, g * nq:(g + 1) * nq],
                                                    start=(g == 0), stop=(g == grp - 1)) for g in range(grp)]
                sch.ops(PE, fns, rd=[c["accbk"], "ones_b"], wr=["ps5"])

            def finalizeB(c):
                nq, ab, h, q0 = c["nq"], c["ab"], c["h"], c["q0"]
                sch.op(ACT, lambda: nc.scalar.activation(out=lnd[ab][:, 0:nq], in_=ps[5][:, 0:nq],
                                                         func=AF.Ln, bias=zcol[:], scale=1.0),
                       rd=["ps5", "zcol"], wr=["lnd%d" % ab])

            def finalizeC(c):
                nq, ab, h, q0 = c["nq"], c["ab"], c["h"], c["q0"]
                sch.op(ACT, lambda: nc.scalar.activation(out=rec[ab][:, 0:nq], in_=lnd[ab][:, 0:nq],
                                                         func=AF.Exp, bias=zcol[:], scale=-1.0),
                       rd=["lnd%d" % ab, "zcol"], wr=["rec%d" % ab])
                sch.op(DVE, lambda: nc.vector.tensor_tensor(
                    out=aT[:, h, q0:q0 + nq], in0=c["pOap"][:, 0:nq], in1=rec[ab][:, 0:nq], op=ALU.mult),
                    rd=[c["pOk"], "rec%d" % ab], wr=["aT"])

            scores(0)
            scores(1)
            for n in range(NU):
                u = U[n]
                c = u["ctx"]
                h = c["h"]
                i = h % 2
                nq = c["nq"]
                k = u["hidx"]
                W = nq * len(u["kts"])
                if k == 0:
                    if h + 1 < NH:
                        load_head(h + 1)
                    if h == 0:
                        for cc in range(16):
                            sch.dma(POOL, wo_sb[:, cc, :], w_out_v[:, cc, :], wr=["wo%d" % cc])
                if h + 1 < NH:
                    if k in (34, 36, 38, 40):
                        gate_stage1a(h + 1, (k - 34) // 2)
                    if 42 <= k <= 46:
                        gate_stage1b(h + 1, k - 42)
                    if k >= 60 and (k - 60) % 3 == 0 and (k - 60) // 3 < 5:
                        gate_stage2_pe(h + 1, (k - 60) // 3)
                if n + 2 < NU:
                    scores(n + 2)
                sbk = u["sbk"]
                pk = PTK[sbk]
                bcol = zcol[:] if u["near"] else bias31[:, h:h + 1]
                sch.op(ACT, lambda: nc.scalar.activation(
                    out=PT[pk][:, 0:W], in_=ps[sbk][:, 0:W], func=AF.Exp, bias=bcol, scale=1.0),
                    rd=["ps%d" % sbk, "bias31", "zcol"], wr=["PT%d" % pk])
                if h + 1 < NH and k >= 60 and (k - 60) % 3 == 0 and (k - 60) // 3 < 5:
                    gate_stage2_act(h + 1, (k - 60) // 3)
                fns = []
                for idx, kt in enumerate(u["kts"]):
                    fns.append(lambda idx=idx, kt=kt: nc.tensor.matmul(
                        c["pOap"][:, 0:nq], vh[i][:, kt, :], PT[pk][:, idx * nq:(idx + 1) * nq],
                        start=(kt == 0), stop=(kt == c["nkt"] - 1)))
                sch.ops(PE, fns, rd=["PT%d" % pk] + ["vh%d_%d" % (i, a) for a in range(4)], wr=[c["pOk"]])
                if u["kts"][0] == 0:
                    sch.op(DVE, lambda: nc.vector.tensor_copy(out=c["acc"][:, 0:W], in_=PT[pk][:, 0:W]),
                           rd=["PT%d" % pk], wr=[c["acck"]])
                else:
                    sch.op(DVE, lambda: nc.vector.tensor_tensor(
                        out=c["acc"][:, 0:W], in0=c["acc"][:, 0:W], in1=PT[pk][:, 0:W], op=ALU.add),
                        rd=["PT%d" % pk, c["acck"]], wr=[c["acck"]])
                if u["kts"][-1] == c["nkt"] - 1:
                    at(n + 3, lambda c=c: finalizeA0(c))
                    at(n + 6, lambda c=c: finalizeA(c))
                    at(n + 8, lambda c=c: finalizeB(c))
                    at(n + 10, lambda c=c: finalizeC(c))
                for fn in later.pop(n, []):
                    fn()
            sch.barrier()

        with ExitStack() as p3:
            def s3(name, shape, dt):
                return sb(name, shape, dt, p3)
            mT = s3("mT", [128, 8, QW], BF16)
            g2bc = s3("g2bc", [128, D], F32)
            xo = [s3("xo%d" % i, [128, D], F32) for i in range(2)]
            x1t = [s3("x1t%d" % i, [128, D], F32) for i in range(2)]
            hn2 = [s3("hn2_%d" % i, [128, D], BF16) for i in range(2)]
            ss2 = [s3("ss2_%d" % i, [128, 1], F32) for i in range(2)]
            rt2 = [s3("rt2_%d" % i, [128, 1], F32) for i in range(2)]
            rs2 = [s3("rs2_%d" % i, [128, 1], F32) for i in range(2)]
            hst = [s3("hst%d" % i, [128, 16, 128], BF16) for i in range(2)]
            for cc in range(8):
                sch.dma(SP if cc % 2 == 0 else POOL, mT[:, cc, :], mT_s[cc], wr=["mT%d" % cc])
            sch.dma(POOL, g2bc[:], g2bc_d, wr=["gbc"])
            if debug:
                sch.dma(SP, mix_dbg[:, 0:8, :], aT[:], rd=["aT"])
                sch.dma(SP, mix_dbg[:, 8:16, :], mT[:], rd=["mT%d" % cc for cc in range(8)])
            wokeys = ["wo%d" % cc for cc in range(16)]

            def load_xo(qi):
                gt = 15 + qi
                sch.dma(POOL, xo[qi % 2][:], xs[gt * 128:(gt + 1) * 128, :], wr=["xo%d" % (qi % 2)])

            def o_mm(qi):
                i = qi % 2
                if qi + 1 < NQT:
                    load_xo(qi + 1)
                for sl in range(4):
                    bank = (4 * qi + sl) % 6
                    fns = []
                    for cc in range(16):
                        src = aT if cc < 8 else mT
                        fns.append(lambda cc=cc, src=src: nc.tensor.matmul(
                            ps[bank][:], src[:, cc % 8, qi * 128:(qi + 1) * 128], wo_sb[:, cc, sl * 512:(sl + 1) * 512],
                            start=(cc == 0), stop=(cc == 15)))
                    sch.ops(PE, fns, rd=["aT"] + ["mT%d" % cc for cc in range(8)] + wokeys, wr=["ps%d" % bank])
                    sch.op(DVE, lambda: nc.vector.tensor_tensor(
                        out=x1t[i][:, sl * 512:(sl + 1) * 512], in0=ps[bank][:], in1=xo[i][:, sl * 512:(sl + 1) * 512],
                        op=ALU.add), rd=["ps%d" % bank, "xo%d" % i], wr=["x1t%d" % i])
                sch.dma(SP, x1_s[qi], x1t[i][:], rd=["x1t%d" % i], wr=[("x1s", qi)])

            def o_norm(qi):
                i = qi % 2
                rmsnorm_tile(x1t[i][:], "x1t%d" % i, g2bc[:], hn2[i][:], "hn2_%d" % i, ss2[i], rt2[i], rs2[i], i)

            def o_tr(qi):
                i = qi % 2
                for a in range(4):
                    bank = 6 + (a % 2)
                    fns = []
                    for b in range(4):
                        dc = 4 * a + b
                        fns.append(lambda dc=dc, b=b: nc.tensor.transpose(
                            psb[bank][:, b * 128:(b + 1) * 128], hn2[i][:, dc * 128:(dc + 1) * 128], ident_b[:]))
                    sch.ops(PE, fns, rd=["hn2_%d" % i, "ident_b"], wr=["ps%d" % bank])
                    if a % 2 == 0:
                        sch.op(ACT, lambda a=a: nc.scalar.activation(
                            out=hst[i][:, 4 * a:4 * a + 4, :],
                            in_=psb[bank][:, 0:512].rearrange("p (a b) -> p a b", a=4), func=AF.Copy),
                            rd=["ps%d" % bank], wr=["hst%d_%d" % (i, a)])
                    else:
                        sch.op(DVE, lambda a=a: nc.vector.tensor_copy(
                            out=hst[i][:, 4 * a:4 * a + 4, :],
                            in_=psb[bank][:, 0:512].rearrange("p (a b) -> p a b", a=4)),
                            rd=["ps%d" % bank], wr=["hst%d_%d" % (i, a)])
                sch.dma(SP, hn2T_s[:, :, qi * 128:(qi + 1) * 128].rearrange("dc p t -> p dc t"), hst[i][:],
                        rd=["hst%d_%d" % (i, a) for a in range(4)], wr=[("hn2Ts", qi)])

            load_xo(0)
            o_mm(0)
            o_norm(0)
            for qi in range(NQT):
                if qi + 1 < NQT:
                    o_mm(qi + 1)
                o_tr(qi)
                if qi + 1 < NQT:
                    o_norm(qi + 1)
            sch.barrier()

        pm.close()
        with ExitStack() as p4:
            def s4(name, shape, dt):
                return sb(name, shape, dt, p4)
            gT = s4("gT", [128, NJ, 1024], BF16)
            halo_st = s4("halo_st", [128, NJ, 2, 2], F32)
            cw = s4("cw", [128, 3, 88], F32)
            cb = s4("cb", [128, 88], F32)
            sch.dma(SP, cw[:], cw_d, wr=["cw"])
            sch.dma(SP, cb[:], cb_d, wr=["cb"])
            w_up_v = w_up.rearrange("(dc p) f -> p dc f", p=128)
            w_down_v = w_down.rearrange("(j p) d -> p j d", p=128)
            wub = [s4("wub%d" % i, [128, 2, 16, 128], BF16) for i in range(2)]
            wdb = [s4("wdb%d" % i, [128, NJ, 128], BF16) for i in range(2)]

            def load_wu(j):
                i = j % 2
                for gv in range(2):
                    c0 = gv * DFF + j * 128
                    sch.dma(POOL, wub[i][:, gv, :, :], w_up_v[:, :, c0:c0 + 128], wr=["wub%d_%d" % (i, gv)])

            def load_wd(m):
                i = m % 2
                for a in range(4):
                    sch.dma(POOL, wdb[i][:, 11 * a:11 * a + 11, :],
                            w_down_v[:, 11 * a:11 * a + 11, m * 128:(m + 1) * 128], wr=["wdb%d_%d" % (i, a)])
            load_wu(0)
            out_v = out_d.rearrange("(t p) c -> p t c", p=128)
            for Gf in range(2):
                with ExitStack() as pu:
                    def su(name, shape, dt):
                        return sb(name, shape, dt, pu)
                    hn2T = su("hn2T", [128, 16, 1024], BF16)
                    hn2Th = su("hn2Th", [128, 16, 2], BF16)
                    c0 = 128 + 1024 * Gf
                    for cch in range(2):
                        for a in range(4):
                            sch.dma(SP if a % 2 == 0 else POOL,
                                    hn2T[:, 4 * a:4 * a + 4, cch * 512:(cch + 1) * 512],
                                    hn2T_s[4 * a:4 * a + 4, :, c0 + cch * 512:c0 + (cch + 1) * 512].rearrange(
                                        "dc p t -> p dc t"),
                                    wr=["hn2T_%d_%d" % (a, cch)])
                    if Gf == 0:
                        sch.dma(POOL, hn2Th[:], hn2T_s[:, :, 126:128].rearrange("dc p t -> p dc t"), wr=["hn2Th"])
                    ugb = [su("ugb%d" % i, [128, 1026], F32) for i in range(2)]
                    uvb = [su("uvb%d" % i, [128, 1026], F32) for i in range(2)]
                    ygb = [su("ygb%d" % i, [128, 512], F32) for i in range(2)]
                    yvb = [su("yvb%d" % i, [128, 512], F32) for i in range(2)]
                    sgb = [su("sgb%d" % i, [128, 512], F32) for i in range(2)]

                    ne = 0
                    for j in range(NJ):
                        if j + 1 < NJ:
                            load_wu(j + 1)
                        else:
                            load_wd(0)
                        wi = j % 2
                        ug, uv = ugb[wi], uvb[wi]
                        ugk, uvk = "ug%d" % wi, "uv%d" % wi
                        wk = ["wub%d_0" % wi, "wub%d_1" % wi]
                        def emit_halo(ug=ug, uv=uv, ugk=ugk, uvk=uvk, wk=wk, wi=wi, j=j):
                          if Gf == 0:
                            fns = []
                            for gv in range(2):
                                for dc in range(16):
                                    fns.append(lambda gv=gv, dc=dc, wi=wi: nc.tensor.matmul(
                                        ps[7][:, gv * 2:gv * 2 + 2], wub[wi][:, gv, dc, :], hn2Th[:, dc, :],
                                        start=(dc == 0), stop=(dc == 15)))
                            sch.ops(PE, fns, rd=wk + ["hn2Th"], wr=["ps7"])
                            sch.op(DVE, lambda ug=ug: nc.vector.tensor_scalar(
                                out=ug[:, 0:2], in0=ps[7][:, 0:2], scalar1=flag[:], scalar2=None, op0=ALU.mult),
                                rd=["ps7", "flag"], wr=[ugk + "h"])
                            sch.op(DVE, lambda uv=uv: nc.vector.tensor_scalar(
                                out=uv[:, 0:2], in0=ps[7][:, 2:4], scalar1=flag[:], scalar2=None, op0=ALU.mult),
                                rd=["ps7", "flag"], wr=[uvk + "h"])
                          else:
                            sch.op(DVE, lambda ug=ug, j=j: nc.vector.tensor_copy(out=ug[:, 0:2],
                                                                              in_=halo_st[:, j, 0, :]),
                                   rd=["halo_st"], wr=[ugk + "h"])
                            sch.op(DVE, lambda uv=uv, j=j: nc.vector.tensor_copy(out=uv[:, 0:2],
                                                                              in_=halo_st[:, j, 1, :]),
                                   rd=["halo_st"], wr=[uvk + "h"])
                        for c in range(2):
                            e2 = ne % 2
                            ne += 1
                            pg, pv = 0 + e2, 2 + e2
                            for gv, pbank in ((0, pg), (1, pv)):
                                fns = [lambda gv=gv, dc=dc, pbank=pbank, c=c, wi=wi: nc.tensor.matmul(
                                    ps[pbank][:], wub[wi][:, gv, dc, :], hn2T[:, dc, c * 512:(c + 1) * 512],
                                    start=(dc == 0), stop=(dc == 15)) for dc in range(16)]
                                sch.ops(PE, fns, rd=wk + ["hn2T_%d_%d" % (a, c) for a in range(4)], wr=["ps%d" % pbank])
                            if c == 0:
                                emit_halo()
                            lo = 2 + 512 * c
                            sch.op(ACT, lambda ug=ug, pg=pg, lo=lo: nc.scalar.activation(
                                out=ug[:, lo:lo + 512], in_=ps[pg][:], func=AF.Copy),
                                rd=["ps%d" % pg], wr=[ugk + "c%d" % c])
                            sch.op(ACT, lambda uv=uv, pv=pv, lo=lo: nc.scalar.activation(
                                out=uv[:, lo:lo + 512], in_=ps[pv][:], func=AF.Copy),
                                rd=["ps%d" % pv], wr=[uvk + "c%d" % c])
                            for (u, uk, y, yk, jj) in ((ug, ugk, ygb[e2], "yg%d" % e2, j),
                                                       (uv, uvk, yvb[e2], "yv%d" % e2, NJ + j)):
                                urd = [uk + "h", uk + "c0", uk + "c1"] if c == 1 else [uk + "h", uk + "c0"]
                                sch.op(DVE, lambda u=u, y=y, jj=jj, lo=lo: nc.vector.tensor_scalar(
                                    out=y[:], in0=u[:, lo:lo + 512], scalar1=cw[:, 2, jj:jj + 1],
                                    scalar2=cb[:, jj:jj + 1], op0=ALU.mult, op1=ALU.add),
                                    rd=urd + ["cw", "cb"], wr=[yk])
                                sch.op(DVE, lambda u=u, y=y, jj=jj, lo=lo: nc.vector.scalar_tensor_tensor(
                                    out=y[:], in0=u[:, lo - 1:lo + 511], scalar=cw[:, 1, jj:jj + 1], in1=y[:],
                                    op0=ALU.mult, op1=ALU.add), rd=urd + ["cw", yk], wr=[yk])
                                sch.op(DVE, lambda u=u, y=y, jj=jj, lo=lo: nc.vector.scalar_tensor_tensor(
                                    out=y[:], in0=u[:, lo - 2:lo + 510], scalar=cw[:, 0, jj:jj + 1], in1=y[:],
                                    op0=ALU.mult, op1=ALU.add), rd=urd + ["cw", yk], wr=[yk])
                            sch.op(ACT, lambda e2=e2: nc.scalar.activation(out=sgb[e2][:], in_=ygb[e2][:],
                                                                            func=AF.Silu),
                                   rd=["yg%d" % e2], wr=["sg%d" % e2])
                            sch.op(DVE, lambda e2=e2, j=j, c=c: nc.vector.tensor_tensor(
                                out=gT[:, j, c * 512:(c + 1) * 512], in0=sgb[e2][:], in1=yvb[e2][:], op=ALU.mult),
                                rd=["sg%d" % e2, "yv%d" % e2], wr=["gT"])
                        if Gf == 0:
                            sch.op(DVE, lambda ug=ug, j=j: nc.vector.tensor_copy(out=halo_st[:, j, 0, :],
                                                                              in_=ug[:, 1024:1026]),
                                   rd=[ugk + "c1"], wr=["halo_st"])
                            sch.op(DVE, lambda uv=uv, j=j: nc.vector.tensor_copy(out=halo_st[:, j, 1, :],
                                                                              in_=uv[:, 1024:1026]),
                                   rd=[uvk + "c1"], wr=["halo_st"])
                    sch.barrier(keep=["wdb0_%d" % a for a in range(4)])
                with ExitStack() as pd:
                    def sd(name, shape, dt):
                        return sb(name, shape, dt, pd)
                    x1g = [sd("x1g%d" % i, [128, 8, 512], F32) for i in range(2)]
                    obuf = [sd("obuf%d" % i, [128, 8, 512], F32) for i in range(2)]
                    yT = [sd("yT%d" % i, [128, 512], F32) for i in range(2)]

                    its = [(mg, mm, c) for mg in range(4) for mm in range(4) for c in range(2)]

                    def d_mm(n):
                        mg, mm, c = its[n]
                        m = 4 * mg + mm
                        if c == 0 and m + 1 < 16:
                            load_wd(m + 1)
                        if c == 0 and m == 15 and Gf == 0:
                            load_wu(0)
                        if mm == 0 and c == 0:
                            gi = mg % 2
                            qa = 1 + 8 * Gf
                            sch.dma(POOL, x1g[gi][:],
                                    x1_s[qa:qa + 8, :, mg * 512:(mg + 1) * 512].rearrange("t p c -> p t c"),
                                    wr=["x1g%d" % gi])
                        wi = m % 2
                        wk = ["wdb%d_%d" % (wi, a) for a in range(4)]
                        pa = n % 2
                        fns = [lambda j=j: nc.tensor.matmul(
                            ps[pa][:], wdb[wi][:, j, :], gT[:, j, c * 512:(c + 1) * 512],
                            start=(j == 0), stop=(j == NJ - 1)) for j in range(NJ)]
                        sch.ops(PE, fns, rd=wk + ["gT"], wr=["ps%d" % pa])

                    def d_post(n):
                        mg, mm, c = its[n]
                        gi = mg % 2
                        e2 = n % 2
                        pa, pt = e2, 2 + e2
                        sch.op(ACT, lambda: nc.scalar.activation(out=yT[e2][:], in_=ps[pa][:], func=AF.Copy),
                               rd=["ps%d" % pa], wr=["yT%d" % e2])
                        if n + 1 < len(its):
                            d_mm(n + 1)
                        fns = [lambda k=k: nc.tensor.transpose(
                            ps[pt][:, k * 128:(k + 1) * 128], yT[e2][:, k * 128:(k + 1) * 128], ident_f[:])
                            for k in range(4)]
                        sch.ops(PE, fns, rd=["yT%d" % e2, "ident_f"], wr=["ps%d" % pt])
                        sch.op(DVE, lambda: nc.vector.tensor_tensor(
                            out=obuf[gi][:, 4 * c:4 * c + 4, mm * 128:(mm + 1) * 128],
                            in0=ps[pt][:].rearrange("p (a b) -> p a b", a=4),
                            in1=x1g[gi][:, 4 * c:4 * c + 4, mm * 128:(mm + 1) * 128], op=ALU.add),
                            rd=["ps%d" % pt, "x1g%d" % gi], wr=["obuf%d_%d" % (gi, c)])
                        if mm == 3:
                            t0 = 8 * Gf + 4 * c
                            sch.dma(SP, out_v[:, t0:t0 + 4, mg * 512:(mg + 1) * 512], obuf[gi][:, 4 * c:4 * c + 4, :],
                                    rd=["obuf%d_%d" % (gi, c)], wr=[("out", Gf, mg, c)])

                    d_mm(0)
                    for n in range(len(its)):
                        d_post(n)
                    sch.barrier(keep=["wub0_0", "wub0_1"])
        sch.barrier()
    return nc


def _rel_bucket_np(n):
    n = np.maximum(n, 0)
    max_exact = 16
    nf = np.maximum(n, max_exact).astype(np.float32)
    large = max_exact + (np.log(nf / np.float32(max_exact)) / np.float32(math.log(1024 / max_exact))
                         * np.float32(32 - max_exact)).astype(np.int32)
    large = np.minimum(large, 31)
    return np.where(n < max_exact, n, large)


def _static_consts():
    c = {}
    c["ident"] = np.eye(128, dtype=np.float32)
    es = np.zeros((128, 16, 128), np.float32)
    for n in range(16):
        es[n, n, :] = 1.0
    c["esel"] = es
    k = np.arange(128)[:, None]
    q = np.arange(512)[None, :]
    dist = np.stack([(-384 + 128 * j) + q - k for j in range(NBT)], 0)
    c["dist"] = dist
    c["bucket"] = _rel_bucket_np(dist)
    tp = np.arange(128)[:, None]
    t = np.arange(128)[None, :]
    band = np.zeros((3, 4, 128, 128), np.float32)
    for g, w in enumerate((2, 4, 8, 16)):
        band[0, g] = np.where(tp >= t + 129 - w, 1.0 / w, 0.0)
        incl = (tp <= t) & (tp > t - w)
        band[1, g] = np.where(incl, 1.0 / w, 0.0) - (tp == t)
        cntf = np.minimum(t + 1, w).astype(np.float32)
        band[2, g] = np.where(incl, 1.0 / cntf, 0.0) - (tp == t)
    c["band"] = band
    no = np.ones((NQT, 16), np.float32)
    for qi in range(NQT):
        no[qi, (15 + qi) // 2] = 0.0
    c["notown"] = np.ascontiguousarray(np.broadcast_to(no[None], (128, NQT, 16)))
    return c


def _prep_inputs(inp):
    f32 = np.float32
    x = np.asarray(inp["x"], f32)
    cst = _static_consts()
    rel_bias = np.asarray(inp["rel_bias"], f32)
    bt = rel_bias[:, cst["bucket"]]
    bt = np.where(cst["dist"][None] >= 0, bt, f32(-BIG)).astype(f32)
    biasT = np.ascontiguousarray(bt.transpose(0, 2, 1, 3))
    rep = lambda v: np.ascontiguousarray(np.broadcast_to(np.asarray(v, f32).reshape(1, -1), (128, v.size)))
    common = {
        "w_in": np.ascontiguousarray(inp["w_in"][0], f32),
        "w_out": np.ascontiguousarray(inp["w_out"][0], f32),
        "w_up": np.ascontiguousarray(inp["w_up"][0], f32),
        "w_down": np.ascontiguousarray(inp["w_down"][0], f32),
        "pool_w": np.ascontiguousarray(inp["pool_w"][0], f32),
        "g1bc": rep(np.asarray(inp["attn_norm_g"][0])),
        "g2bc": rep(np.asarray(inp["ffn_norm_g"][0])),
        "gqk": np.ascontiguousarray(np.stack([np.asarray(inp["q_norm_g"][0], f32),
                                              np.asarray(inp["k_norm_g"][0], f32)], 1)),
        "pscale": np.ascontiguousarray(np.asarray(inp["pool_scale"][0], f32).reshape(8, 128).T),
        "cw": np.ascontiguousarray(np.asarray(inp["conv_w"][0], f32).reshape(3, 88, 128).transpose(2, 0, 1)),
        "cb": np.ascontiguousarray(np.asarray(inp["conv_b"][0], f32).reshape(88, 128).T),
        "bias31": rep(rel_bias[:, 31]),
        "biasT": biasT,
        "notown": cst["notown"],
        "ident": cst["ident"],
        "esel": cst["esel"],
    }
    maps = []
    for core in range(8):
        b, r = core // 2, core % 2
        m = dict(common)
        if r == 1:
            m["xs"] = np.ascontiguousarray(x[b])
        else:
            m["xs"] = np.ascontiguousarray(np.concatenate([np.zeros((2048, D), f32), x[b, :2048]], 0))
        band = cst["band"].copy()
        if r == 1:
            band[2] = band[1]
        m["band"] = np.ascontiguousarray(band.transpose(2, 0, 1, 3))
        first = 0 if r == 1 else 8
        gmk = np.full((NQT, 16), -1e30, f32)
        for qi in range(NQT):
            for n in range(16):
                if first <= n < (15 + qi) // 2:
                    gmk[qi, n] = 0.0
        m["gmask"] = np.ascontiguousarray(np.broadcast_to(gmk[None], (128, NQT, 16)))
        m["flag"] = np.full((128, 1), float(r), f32)
        maps.append(m)
    return maps


_NC_CACHE = {}


def kernel(**inputs):
    maps = _prep_inputs(inputs)
    if "nc" not in _NC_CACHE:
        _NC_CACHE["nc"] = build_program(DEBUG)
    nc = _NC_CACHE["nc"]
    res = run_bass_kernel_spmd(nc, maps, core_ids=list(range(8)))
    out = np.empty((4, S, D), np.float32)
    for core in range(8):
        b, r = core // 2, core % 2
        out[b, r * 2048:(r + 1) * 2048] = res.results[core]["out"]
    if DEBUG:
        kernel.last = res
    return out
```
